# Optimizing a Trainium2 kernel written in Bass

```python
import jax, jax.numpy as jnp
from jax import lax
import numpy as np

D_MODEL = 1024
BATCH = 8
SEQ = 2048
DEPTH = 1
DEC_BATCH = 128
DEC_SEQ = 4
PAST_LEN = 16384
PAGE_SIZE = 128

RG_WIDTH = D_MODEL
RG_BLOCKS = 8
RG_BLOCK = RG_WIDTH // RG_BLOCKS
RG_C = 8.0
CONV_W = 4
SSD_EXPAND = 2
SSD_INNER = SSD_EXPAND * D_MODEL
SSD_HEAD_DIM = 64
SSD_HEADS = SSD_INNER // SSD_HEAD_DIM
SSD_GROUPS = 8
SSD_HPG = SSD_HEADS // SSD_GROUPS
SSD_STATE = 128
SSD_CHUNK = 128
SSD_CONV_DIM = SSD_INNER + 2 * SSD_GROUPS * SSD_STATE
D_FF = 2816
EPS = 1e-6
IN_SIZES = (RG_WIDTH, RG_WIDTH, SSD_INNER, SSD_CONV_DIM, SSD_HEADS, D_MODEL, D_MODEL)
D_IN = sum(IN_SIZES)

kernel_name = "hawk_ssd_parallel_macaron_decoder_step"


def rmsnorm(x, w):
    xf = x.astype(jnp.float32)
    y = xf * lax.rsqrt(jnp.mean(xf * xf, axis=-1, keepdims=True) + EPS)
    return (y * w.astype(jnp.float32)).astype(x.dtype)


def swiglu(x, wg, wu, wd):
    return (jax.nn.silu(x @ wg) * (x @ wu)) @ wd


def causal_conv(x, buf, w, b):
    L = x.shape[1]
    xc = jnp.concatenate([buf.astype(x.dtype), x], axis=1)
    y = xc[:, 0:L] * w[0]
    for k in range(1, CONV_W):
        y = y + xc[:, k:k + L] * w[k]
    return y + b, xc[:, -(CONV_W - 1):]


def rg_lru(x, pos, h0, wa, ba, wx, bx, lam):
    b, l, _ = x.shape
    xf = x.astype(jnp.float32)
    xb = xf.reshape(b, l, RG_BLOCKS, RG_BLOCK)
    gate_a = jax.nn.sigmoid(jnp.einsum('blhi,hij->blhj', xb, wa).reshape(b, l, RG_WIDTH) + ba)
    gate_x = jax.nn.sigmoid(jnp.einsum('blhi,hij->blhj', xb, wx).reshape(b, l, RG_WIDTH) + bx)
    log_a = -RG_C * gate_a * jax.nn.softplus(-lam.astype(jnp.float32))
    reset = (pos == 0)[None, :, None]
    a = jnp.where(reset, 0.0, jnp.exp(log_a))
    mult = jnp.where(reset, 1.0, jnp.sqrt(-jnp.expm1(2.0 * log_a)))
    u = xf * gate_x * mult

    def combine(lft, rgt):
        return (lft[0] * rgt[0], rgt[0] * lft[1] + rgt[1])

    a_cum, u_cum = lax.associative_scan(combine, (a, u), axis=1)
    h = a_cum * h0.astype(jnp.float32)[:, None] + u_cum
    return h, h[:, -1]


def ssd_scan(x, dt, A, B, C, h0):
    b, l = x.shape[:2]
    q = min(SSD_CHUNK, l)
    nc = -(-l // q)
    pad = nc * q - l
    if pad:
        pw = lambda t: jnp.pad(t, [(0, 0), (0, pad)] + [(0, 0)] * (t.ndim - 2))
        x, dt, B, C = pw(x), pw(dt), pw(B), pw(C)
    G, R, P, N = SSD_GROUPS, SSD_HPG, SSD_HEAD_DIM, SSD_STATE
    x = x.reshape(b, nc, q, G, R, P)
    dt = dt.reshape(b, nc, q, G, R)
    B = B.reshape(b, nc, q, G, N)
    C = C.reshape(b, nc, q, G, N)
    acs = jnp.cumsum(dt * A.reshape(G, R), axis=2)
    seg = acs[:, :, :, None] - acs[:, :, None, :]
    mask = jnp.tril(jnp.ones((q, q), bool))[:, :, None, None]
    lmat = jnp.exp(jnp.where(mask, seg, -jnp.inf))
    cb = jnp.einsum('bcign,bcjgn->bcijg', C, B)
    w = cb[..., None] * lmat * dt[:, :, None]
    y_diag = jnp.einsum('bcijgr,bcjgrp->bcigrp', w, x)
    decay = jnp.exp(acs[:, :, -1:] - acs)
    chunk_states = jnp.einsum('bcjgn,bcjgr,bcjgrp->bcgrpn', B, decay * dt, x)
    chunk_decay = jnp.exp(acs[:, :, -1])

    def step(s, inp):
        cs, cd = inp
        return cd[..., None, None] * s + cs, s

    final, s_in = lax.scan(step, h0.astype(jnp.float32).reshape(b, G, R, P, N),
                           (jnp.moveaxis(chunk_states, 1, 0), jnp.moveaxis(chunk_decay, 1, 0)))
    s_in = jnp.moveaxis(s_in, 0, 1)
    y_off = jnp.einsum('bcign,bcgrpn->bcigrp', C, s_in) * jnp.exp(acs)[..., None]
    y = (y_diag + y_off).reshape(b, nc * q, SSD_HEADS, P)[:, :l]
    return y, final.reshape(b, SSD_HEADS, P, N)


def mixer(u, pos, rg_h0, rg_buf, ssd_h0, ssd_buf, p):
    b, l, _ = u.shape
    f32 = jnp.float32
    proj = u @ p['w_in']
    cuts = [int(c) for c in np.cumsum(IN_SIZES)[:-1]]
    rg_x, rg_g, z, xbc, dt_raw, g_rg, g_ssd = jnp.split(proj, cuts, axis=-1)
    rg_xc, rg_buf_new = causal_conv(rg_x, rg_buf, p['rg_conv_w'], p['rg_conv_b'])
    h, rg_h_new = rg_lru(rg_xc, pos, rg_h0, p['rg_wa'], p['rg_ba'], p['rg_wx'], p['rg_bx'], p['rg_lambda'])
    y_rg = h * jax.nn.gelu(rg_g.astype(f32))
    xbc_c, ssd_buf_new = causal_conv(xbc, ssd_buf, p['ssd_conv_w'], p['ssd_conv_b'])
    xbc_c = jax.nn.silu(xbc_c.astype(f32))
    xs, Bs, Cs = jnp.split(xbc_c, [SSD_INNER, SSD_INNER + SSD_GROUPS * SSD_STATE], axis=-1)
    dt = jax.nn.softplus(dt_raw.astype(f32) + p['ssd_dt_bias'])
    A = -jnp.exp(p['ssd_a_log'].astype(f32))
    xh = xs.reshape(b, l, SSD_HEADS, SSD_HEAD_DIM)
    y, ssd_h_new = ssd_scan(xh, dt, A, Bs.reshape(b, l, SSD_GROUPS, SSD_STATE),
                            Cs.reshape(b, l, SSD_GROUPS, SSD_STATE), ssd_h0)
    y = (y + p['ssd_d'][:, None] * xh).reshape(b, l, SSD_INNER) * jax.nn.silu(z.astype(f32))
    yg = y.reshape(b, l, SSD_GROUPS, SSD_INNER // SSD_GROUPS)
    yg = yg * lax.rsqrt(jnp.mean(yg * yg, axis=-1, keepdims=True) + EPS)
    y_ssd = yg.reshape(b, l, SSD_INNER) * p['ssd_norm_w']
    m = (jax.nn.sigmoid(g_rg.astype(f32)) * (y_rg @ p['w_proj_rg'])
         + jax.nn.sigmoid(g_ssd.astype(f32)) * (y_ssd @ p['w_proj_ssd']))
    out = (m @ p['w_out']).astype(u.dtype)
    new = (rg_h_new.astype(rg_h0.dtype), rg_buf_new.astype(rg_buf.dtype),
           ssd_h_new.astype(ssd_h0.dtype), ssd_buf_new.astype(ssd_buf.dtype))
    return out, new


def layer(x, pos, rg_h0, rg_buf, ssd_h0, ssd_buf, p):
    x = x + 0.5 * rmsnorm(swiglu(rmsnorm(x, p['n_ffn1_pre']), p['ffn1_wg'], p['ffn1_wu'], p['ffn1_wd']),
                          p['n_ffn1_post'])
    mix, new = mixer(rmsnorm(x, p['n_mix_pre']), pos, rg_h0, rg_buf, ssd_h0, ssd_buf, p)
    x = x + rmsnorm(mix, p['n_mix_post'])
    x = x + 0.5 * rmsnorm(swiglu(rmsnorm(x, p['n_ffn2_pre']), p['ffn2_wg'], p['ffn2_wu'], p['ffn2_wd']),
                          p['n_ffn2_post'])
    return x, new


def setup_inputs(seed: int = 0) -> dict:
    key = jax.random.key(seed)
    ks = iter(jax.random.split(key, 48))
    f32 = jnp.float32
    nrm = lambda shape, s: jax.random.normal(next(ks), shape, f32) * s
    gain = lambda n: 1.0 + nrm((DEPTH, n), 0.05)
    d = {}
    d['x_prompt'] = nrm((BATCH, SEQ, D_MODEL), 1.0)
    d['x_sample'] = nrm((DEC_BATCH, DEC_SEQ, D_MODEL), 1.0)
    d['state_rg_h'] = nrm((DEPTH, DEC_BATCH, RG_WIDTH), 0.5)
    d['state_rg_conv'] = nrm((DEPTH, DEC_BATCH, CONV_W - 1, RG_WIDTH), 1.0)
    d['state_ssd'] = nrm((DEPTH, DEC_BATCH, SSD_HEADS, SSD_HEAD_DIM, SSD_STATE), 0.3)
    d['state_ssd_conv'] = nrm((DEPTH, DEC_BATCH, CONV_W - 1, SSD_CONV_DIM), 1.0)
    d['n_ffn1_pre'] = gain(D_MODEL)
    d['n_ffn1_post'] = gain(D_MODEL)
    d['ffn1_wg'] = nrm((DEPTH, D_MODEL, D_FF), D_MODEL ** -0.5)
    d['ffn1_wu'] = nrm((DEPTH, D_MODEL, D_FF), D_MODEL ** -0.5)
    d['ffn1_wd'] = nrm((DEPTH, D_FF, D_MODEL), D_FF ** -0.5)
    d['n_mix_pre'] = gain(D_MODEL)
    d['n_mix_post'] = gain(D_MODEL)
    d['w_in'] = nrm((DEPTH, D_MODEL, D_IN), D_MODEL ** -0.5)
    d['rg_conv_w'] = nrm((DEPTH, CONV_W, RG_WIDTH), CONV_W ** -0.5)
    d['rg_conv_b'] = nrm((DEPTH, RG_WIDTH), 0.02)
    d['rg_wa'] = nrm((DEPTH, RG_BLOCKS, RG_BLOCK, RG_BLOCK), RG_BLOCK ** -0.5)
    d['rg_ba'] = nrm((DEPTH, RG_WIDTH), 0.1)
    d['rg_wx'] = nrm((DEPTH, RG_BLOCKS, RG_BLOCK, RG_BLOCK), RG_BLOCK ** -0.5)
    d['rg_bx'] = nrm((DEPTH, RG_WIDTH), 0.1)
    a_c = jax.random.uniform(next(ks), (DEPTH, RG_WIDTH), f32, 0.9, 0.999)
    s = a_c ** (1.0 / RG_C)
    d['rg_lambda'] = jnp.log(s) - jnp.log1p(-s)
    d['ssd_conv_w'] = nrm((DEPTH, CONV_W, SSD_CONV_DIM), CONV_W ** -0.5)
    d['ssd_conv_b'] = nrm((DEPTH, SSD_CONV_DIM), 0.02)
    dt0 = jnp.exp(jax.random.uniform(next(ks), (DEPTH, SSD_HEADS), f32, np.log(1e-3), np.log(1e-1)))
    d['ssd_dt_bias'] = dt0 + jnp.log(-jnp.expm1(-dt0))
    d['ssd_a_log'] = jnp.log(jax.random.uniform(next(ks), (DEPTH, SSD_HEADS), f32, 1.0, 16.0))
    d['ssd_d'] = 1.0 + nrm((DEPTH, SSD_HEADS), 0.1)
    d['ssd_norm_w'] = gain(SSD_INNER)
    d['w_proj_rg'] = nrm((DEPTH, RG_WIDTH, D_MODEL), RG_WIDTH ** -0.5)
    d['w_proj_ssd'] = nrm((DEPTH, SSD_INNER, D_MODEL), SSD_INNER ** -0.5)
    d['w_out'] = nrm((DEPTH, D_MODEL, D_MODEL), D_MODEL ** -0.5)
    d['n_ffn2_pre'] = gain(D_MODEL)
    d['n_ffn2_post'] = gain(D_MODEL)
    d['ffn2_wg'] = nrm((DEPTH, D_MODEL, D_FF), D_MODEL ** -0.5)
    d['ffn2_wu'] = nrm((DEPTH, D_MODEL, D_FF), D_MODEL ** -0.5)
    d['ffn2_wd'] = nrm((DEPTH, D_FF, D_MODEL), D_FF ** -0.5)
    return d


def reference(x_prompt, x_sample, state_rg_h, state_rg_conv, state_ssd, state_ssd_conv,
              n_ffn1_pre, n_ffn1_post, ffn1_wg, ffn1_wu, ffn1_wd, n_mix_pre, n_mix_post, w_in,
              rg_conv_w, rg_conv_b, rg_wa, rg_ba, rg_wx, rg_bx, rg_lambda,
              ssd_conv_w, ssd_conv_b, ssd_dt_bias, ssd_a_log, ssd_d, ssd_norm_w,
              w_proj_rg, w_proj_ssd, w_out, n_ffn2_pre, n_ffn2_post, ffn2_wg, ffn2_wu, ffn2_wd):
    params = dict(n_ffn1_pre=n_ffn1_pre, n_ffn1_post=n_ffn1_post, ffn1_wg=ffn1_wg, ffn1_wu=ffn1_wu,
                  ffn1_wd=ffn1_wd, n_mix_pre=n_mix_pre, n_mix_post=n_mix_post, w_in=w_in,
                  rg_conv_w=rg_conv_w, rg_conv_b=rg_conv_b, rg_wa=rg_wa, rg_ba=rg_ba, rg_wx=rg_wx,
                  rg_bx=rg_bx, rg_lambda=rg_lambda, ssd_conv_w=ssd_conv_w, ssd_conv_b=ssd_conv_b,
                  ssd_dt_bias=ssd_dt_bias, ssd_a_log=ssd_a_log, ssd_d=ssd_d, ssd_norm_w=ssd_norm_w,
                  w_proj_rg=w_proj_rg, w_proj_ssd=w_proj_ssd, w_out=w_out, n_ffn2_pre=n_ffn2_pre,
                  n_ffn2_post=n_ffn2_post, ffn2_wg=ffn2_wg, ffn2_wu=ffn2_wu, ffn2_wd=ffn2_wd)
    bp, lp = x_prompt.shape[:2]
    ls = x_sample.shape[1]
    pos_p = jnp.arange(lp, dtype=jnp.int32)
    pos_s = PAST_LEN + jnp.arange(ls, dtype=jnp.int32)
    yp, ys = x_prompt, x_sample
    p_new = ([], [], [], [])
    s_new = ([], [], [], [])
    for li in range(DEPTH):
        p = {k: v[li] for k, v in params.items()}
        z_rg_h = jnp.zeros((bp,) + state_rg_h.shape[2:], state_rg_h.dtype)
        z_rg_c = jnp.zeros((bp,) + state_rg_conv.shape[2:], state_rg_conv.dtype)
        z_ssd = jnp.zeros((bp,) + state_ssd.shape[2:], state_ssd.dtype)
        z_ssd_c = jnp.zeros((bp,) + state_ssd_conv.shape[2:], state_ssd_conv.dtype)
        yp, newp = layer(yp, pos_p, z_rg_h, z_rg_c, z_ssd, z_ssd_c, p)
        ys, news = layer(ys, pos_s, state_rg_h[li], state_rg_conv[li], state_ssd[li], state_ssd_conv[li], p)
        for lst, v in zip(p_new, newp):
            lst.append(v)
        for lst, v in zip(s_new, news):
            lst.append(v)
    prompt_rg_h, prompt_rg_conv, prompt_ssd, prompt_ssd_conv = [jnp.stack(v, 0) for v in p_new]
    sample_rg_h, sample_rg_conv, sample_ssd, sample_ssd_conv = [jnp.stack(v, 0) for v in s_new]
    return (yp, ys, prompt_rg_h, prompt_rg_conv, prompt_ssd, prompt_ssd_conv,
            sample_rg_h, sample_rg_conv, sample_ssd, sample_ssd_conv)
```

```python
import numpy as np
import concourse.bass as bass
import concourse.mybir as mybir
from concourse.bass_utils import run_bass_kernel_spmd

F32 = mybir.dt.float32
BF16 = mybir.dt.bfloat16
AF = mybir.ActivationFunctionType
ALU = mybir.AluOpType

D = 1024
DFF = 2816
NFC = 22
DIN = 10272
EPS = 1e-6
NSLOT = 4
SLOTC = 4096

C_ID, C_PTRI, C_PSA, C_PNEG, C_STRI, C_SSA, C_SNEG, C_SSEL, C_MASKB, C_END = (
    0, 128, 256, 384, 896, 960, 1024, 1280, 1296, 1296)
PF_RGCW, PF_RGCB, PF_BA, PF_BX, PF_LAM, PF_SCW, PF_SCB, PF_SNW, PF_END = 0, 32, 40, 48, 56, 64, 192, 224, 240


class Buf:
    __slots__ = ("name", "w", "rs", "const")

    def __init__(self, name, init=None, const=False):
        self.name = name
        self.w = {}
        self.rs = dict(init or {})
        self.const = const


class T:
    def __init__(self, t, b):
        self.t = t
        self.b = b

    def __getitem__(self, k):
        return self.t[k]


class Eng:
    def __init__(self, K, h, name, ndma=0, is_pe=False):
        self.K = K
        self.h = h
        self.name = name
        self.is_pe = is_pe
        self.sem = K.nc.alloc_semaphore("s_" + name)
        self.sid = K.newsid(self.sem)
        self.cnt = 0
        self.seen = {}
        self.dsems = [K.nc.alloc_semaphore("d_%s%d" % (name, i)) for i in range(ndma)]
        self.dsid = [K.newsid(s) for s in self.dsems]
        self.dcnt = [0] * ndma
        self.di = 0

    def need(self, sid, val):
        if val <= 0:
            return
        if self.is_pe and sid == self.sid:
            return
        if self.seen.get(sid, 0) >= val:
            return
        self.h.wait_ge(self.K.sems[sid], val)
        self.seen[sid] = val

    def _deps(self, reads, writes):
        for b in reads:
            for sid, v in b.w.items():
                self.need(sid, v)
        for b in writes:
            for sid, v in b.w.items():
                self.need(sid, v)
            for sid, v in b.rs.items():
                self.need(sid, v)

    def _upd(self, tok, reads, writes, add=False):
        for b in writes:
            if add:
                b.w[tok[0]] = max(b.w.get(tok[0], 0), tok[1])
            else:
                b.w = {tok[0]: tok[1]}
                b.rs = {}
        for b in reads:
            if not b.const:
                b.rs[tok[0]] = max(b.rs.get(tok[0], 0), tok[1])

    def op(self, fn, reads=(), writes=(), inc=True):
        self._deps(reads, writes)
        ins = fn(self.h)
        if inc:
            self.cnt += 1
            ins.then_inc(self.sem, 1)
            tok = (self.sid, self.cnt)
        else:
            tok = (self.sid, self.cnt + 1)
        self._upd(tok, reads, writes)
        self.K.nins += 1
        return ins

    def dma(self, out, in_, reads=(), writes=(), add=False):
        self._deps(reads, writes)
        i = self.di % len(self.dsems)
        self.di += 1
        self.need(self.dsid[i], self.dcnt[i])
        ins = self.h.dma_start(out=out, in_=in_)
        self.dcnt[i] += 16
        ins.then_inc(self.dsems[i], 16)
        tok = (self.dsid[i], self.dcnt[i])
        self._upd(tok, reads, writes, add=add)
        self.K.nins += 1
        return ins


class Kern:
    def __init__(self, nc):
        self.nc = nc
        self.sems = []
        self.nins = 0
        self.pe = Eng(self, nc.tensor, "pe", is_pe=True)
        self.act = Eng(self, nc.scalar, "act", ndma=8)
        self.dve = Eng(self, nc.vector, "dve")
        self.pool = Eng(self, nc.gpsimd, "pool", ndma=12)
        self.sp = Eng(self, nc.sync, "sp", ndma=12)
        self.engs = [self.pe, self.act, self.dve, self.pool, self.sp]
        self.banks = []
        for i in range(8):
            t = nc.alloc_psum_tensor("bank%d" % i, [128, 512], F32)
            self.banks.append(T(t, Buf("bank%d" % i)))
        self.bi = 0
        self.pinned = set()
        self.snap = {}
        self.ncnt = 0

    def newsid(self, sem):
        self.sems.append(sem)
        return len(self.sems) - 1

    def snapshot(self):
        s = {}
        for e in self.engs:
            if e.cnt:
                s[e.sid] = e.cnt
            for i, sid in enumerate(e.dsid):
                if e.dcnt[i]:
                    s[sid] = e.dcnt[i]
        self.snap = s

    def bank(self, pin=False):
        for _ in range(16):
            i = self.bi % 8
            self.bi += 1
            if i not in self.pinned:
                if pin:
                    self.pinned.add(i)
                return self.banks[i]
        raise RuntimeError("no bank")

    def unpin(self, bk):
        self.pinned.discard(self.banks.index(bk))

    def sb(self, name, shape, dt, const=False):
        self.ncnt += 1
        t = self.nc.alloc_sbuf_tensor("%s_%d" % (name, self.ncnt), list(shape), dt)
        return T(t, Buf(name, const=const))


class Arena:
    def __init__(self, K, nbytes):
        self.K = K
        self.t = K.nc.alloc_sbuf_tensor("arena", [128, nbytes // 2], BF16)
        self.n = nbytes
        self.off = 0

    def reset(self):
        self.off = 0
        self.K.snapshot()

    def alloc(self, name, shape, dt):
        esz = 4 if dt == F32 else 2
        n = int(np.prod(shape[1:])) * esz
        n = (n + 63) // 64 * 64
        assert self.off + n <= self.n, ("arena overflow", name, self.off, n, self.n)
        v = self.t[:, self.off // 2:(self.off + n) // 2]
        if dt == F32:
            v = v.bitcast(F32)
        cnt = int(np.prod(shape[1:]))
        v = v[:, 0:cnt]
        if len(shape) == 3:
            v = v.rearrange("p (a b) -> p a b", a=shape[1])
        elif len(shape) == 4:
            v = v.rearrange("p (a b c) -> p a b c", a=shape[1], b=shape[2])
        self.off += n
        return T(v, Buf(name, init=self.K.snap))


def bc(ap, shape):
    return ap.broadcast_to(list(shape))


def build(nc, NPT=4, DO_SAMPLE=True, DEBUG=False):
    K = Kern(nc)
    dbg = []
    pe, act, dve, pool, sp = K.pe, K.act, K.dve, K.pool, K.sp
    IOH = [pool]

    def din(name, shape):
        return nc.dram_tensor(name, list(shape), F32, kind="ExternalInput").ap()

    def dout(name, shape):
        return nc.dram_tensor(name, list(shape), F32, kind="ExternalOutput").ap()

    xp = din("xp", [2048, D]); xs = din("xs", [16, 4, D])
    srh = din("srh", [16, D]); src = din("src", [16, 3, D])
    sss = din("sss", [16, 2048, 128]); ssc = din("ssc", [16, 3, 4096])
    wg = [din("wg1", [D, DFF]), din("wg2", [D, DFF])]
    wu = [din("wu1", [D, DFF]), din("wu2", [D, DFF])]
    wd = [din("wd1", [DFF, D]), din("wd2", [DFF, D])]
    win = din("win", [D, DIN])
    wa = din("wa", [8, 128, 128]); wx = din("wx", [8, 128, 128])
    wprg = din("wprg", [D, D]); wpssd = din("wpssd", [2048, D]); wout = din("wout", [D, D])
    pf_d = din("pf", [128, PF_END]); nv = din("nv", [6, D])
    pt32 = din("pt32", [3, 32]); consts_d = din("consts", [128, C_END]); maskb_d = din("maskb", [128, 1024])

    yp = dout("yp", [2048, D]); ys = dout("ys", [16, 4, D])
    o_prh = dout("o_prh", [8, 128]); o_prc = dout("o_prc", [3, D])
    o_pss = dout("o_pss", [2048, 128]); o_psc = dout("o_psc", [3, 4096])
    o_srh = dout("o_srh", [16, D]); o_src = dout("o_src", [16, 3, D])
    o_sss = dout("o_sss", [16, 2048, 128]); o_ssc = dout("o_ssc", [16, 3, 4096])

    CONST = K.sb("const", [128, C_END], F32, const=True)
    IDB = K.sb("idb", [128, 128], BF16, const=True)
    PF = K.sb("pf", [128, PF_END], F32, const=True)
    NSP8 = K.sb("nsp8", [128, 8], F32, const=True)
    DTB = K.sb("dtb", [128, 32], F32, const=True)
    ABC = K.sb("abc", [128, 32], F32, const=True)
    DBC = K.sb("dbc", [128, 32], F32, const=True)
    NW = [K.sb("nw%d" % i, [128, D], F32) for i in range(2)]
    X = K.sb("x", [128, 4, D], F32)
    XNB = [K.sb("xnb0", [128, D], BF16)]
    XNT = K.sb("xnt", [128, 8, 512], BF16)
    RING = [K.sb("ring%d" % i, [128, SLOTC], BF16) for i in range(NSLOT)]
    RGC = K.sb("rgc", [128, 8, 48], F32)
    HC = K.sb("hc", [128, 8, 16], F32)
    XBCC = K.sb("xbcc", [128, 32, 48], F32)
    S = K.sb("sst", [128, 16, 128], F32)
    JUNK = K.sb("junk", [128, 256], BF16)
    SS = K.sb("ss", [128, 8], F32)
    RS = K.sb("rs", [128, 8], F32)
    SS2 = K.sb("ss2", [128, 4], F32)
    RS2 = K.sb("rs2", [128, 4], F32)
    TMPS = K.sb("tmps", [128, 64], F32)
    RGW = K.sb("rgw", [128, 2048], BF16, const=True)
    AR = Arena(K, nc.sbuf_bytes_remaining - 2048)

    ID = CONST.t[:, C_ID:C_ID + 128]
    pool.dma(RGW.t[:, 0:1024].rearrange("p (k c) -> p k c", k=8), wa.rearrange("h i j -> i h j"), writes=[RGW.b])
    pool.dma(RGW.t[:, 1024:2048].rearrange("p (k c) -> p k c", k=8), wx.rearrange("h i j -> i h j"), writes=[RGW.b], add=True)

    sp.dma(CONST.t[:], consts_d, writes=[CONST.b])
    sp.dma(PF.t[:], pf_d, writes=[PF.b])
    sp.dma(DTB.t[:], pt32[0:1, :].partition_broadcast(128).rearrange("p a b -> p (a b)"), writes=[DTB.b])
    sp.dma(ABC.t[:], pt32[1:2, :].partition_broadcast(128).rearrange("p a b -> p (a b)"), writes=[ABC.b])
    sp.dma(DBC.t[:], pt32[2:3, :].partition_broadcast(128).rearrange("p a b -> p (a b)"), writes=[DBC.b])
    dve.op(lambda h: h.tensor_copy(IDB.t[:], ID), [CONST.b], [IDB.b])
    act.op(lambda h: h.activation(out=ABC.t[:], in_=ABC.t[:], func=AF.Exp), [ABC.b], [ABC.b])
    dve.op(lambda h: h.tensor_scalar(out=ABC.t[:], in0=ABC.t[:], scalar1=-1.0, scalar2=None, op0=ALU.mult), [ABC.b], [ABC.b])
    act.op(lambda h: h.activation(out=NSP8.t[:], in_=PF.t[:, PF_LAM:PF_LAM + 8], func=AF.Exp, scale=-1.0), [PF.b], [NSP8.b])
    act.op(lambda h: h.activation(out=NSP8.t[:], in_=NSP8.t[:], func=AF.Ln, bias=1.0), [NSP8.b], [NSP8.b])
    dve.op(lambda h: h.tensor_scalar(out=NSP8.t[:], in0=NSP8.t[:], scalar1=-8.0, scalar2=None, op0=ALU.mult), [NSP8.b], [NSP8.b])

    HPF = K.sb("hpf", [128, 24], F32, const=True)
    dve.op(lambda h: h.tensor_scalar(out=HPF.t[:, 0:16], in0=PF.t[:, PF_BA:PF_BA + 16], scalar1=0.5, scalar2=None, op0=ALU.mult), [PF.b], [HPF.b])
    dve.op(lambda h: h.tensor_scalar(out=HPF.t[:, 16:24], in0=NSP8.t[:, 0:8], scalar1=0.5, scalar2=None, op0=ALU.mult), [NSP8.b, HPF.b], [HPF.b])

    def wview(w, r0, nk, c0, ncols):
        return w[r0:r0 + nk * 128, c0:c0 + ncols].rearrange("(k p) c -> p k c", p=128)

    def ffn_blocks(i):
        return [[(0, 8, 256, wview(wg[i], 0, 8, j * 256, 256)), (2048, 8, 256, wview(wu[i], 0, 8, j * 256, 256))]
                for j in range(11)]

    def mixer_blocks():
        bl = []
        for c2 in range(4):
            bl.append([(0, 8, 256, wview(win, 0, 8, c2 * 256, 256)), (2048, 8, 256, wview(win, 0, 8, 1024 + c2 * 256, 256))])
        for zc in range(4):
            bl.append([(0, 8, 512, wview(win, 0, 8, 2048 + zc * 512, 512))])
        bl.append([(0, 8, 32, wview(win, 0, 8, 8192, 32))])
        for i in range(8):
            bl.append([(0, 8, 512, wview(win, 0, 8, 4096 + i * 512, 512))])
        for dc2 in range(4):
            bl.append([(0, 8, 256, wview(wprg, 0, 8, dc2 * 256, 256)), (2048, 8, 256, wview(win, 0, 8, 8224 + dc2 * 256, 256))])
            bl.append([(0, 8, 256, wview(win, 0, 8, 9248 + dc2 * 256, 256))])
            bl.append([(0, 16, 256, wview(wpssd, 0, 16, dc2 * 256, 256))])
        for half in range(2):
            bl.append([(0, 8, 512, wview(wout, 0, 8, half * 512, 512))])
        return bl

    ntiles = NPT + (1 if DO_SAMPLE else 0)
    tile_seq = ffn_blocks(0) + mixer_blocks() + ffn_blocks(1)
    NB = len(tile_seq)
    seq = tile_seq * ntiles
    wst = {"next": 0, "emitted": 0}
    SCR = nc.dram_tensor("wscr", [NB, 128, SLOTC], BF16, kind="Internal").ap()
    SCRB = [Buf("scr%d" % i) for i in range(NB)]
    WDS = nc.dram_tensor("wdscr", [2, 11, 128, 2048], BF16, kind="Internal").ap()
    WDSB = [[Buf("wds%d_%d" % (f, j)) for j in range(11)] for f in range(2)]
    wd_tile = [0, 0]
    CDSCR = nc.dram_tensor("cdscr", [32, 64], F32, kind="Internal").ap()
    CDSB = Buf("cdscr")

    def emit_casts():
        for r in range(NB):
            for ii, (off, nk, ncols, src_ap) in enumerate(tile_seq[r]):
                dst = SCR[r, :, off:off + nk * ncols].rearrange("p (k c) -> p k c", k=nk)
                pool.dma(dst, src_ap, writes=[SCRB[r]], add=(ii > 0))
            fi = 0 if r < 11 else (1 if r >= NB - 11 else None)
            if fi is not None:
                j = r if fi == 0 else r - (NB - 11)
                pool.dma(WDS[fi, j].rearrange("p (k c) -> p k c", k=2), wd[fi][j * 256:(j + 1) * 256, :].rearrange("(k p) c -> p k c", p=128),
                         writes=[WDSB[fi][j]])

    def wnext(n=1, hold=0):
        r0 = wst["next"]
        wst["next"] += n
        lim = min(len(seq), r0 + NSLOT - hold)
        while wst["emitted"] < lim:
            r = wst["emitted"]
            slot = RING[r % NSLOT]
            used = max(off + nk * ncols for (off, nk, ncols, _) in seq[r])
            sp.dma(slot.t[:, 0:used], SCR[r % NB, :, 0:used], reads=[SCRB[r % NB]], writes=[slot.b])
            wst["emitted"] += 1
        return [RING[(r0 + i) % NSLOT] for i in range(n)]

    def wd_load(fi, j, dst):
        d2 = dst.t[:].rearrange("p k c -> p (k c)")
        sp.dma(d2, WDS[fi, j], reads=[WDSB[fi][j]], writes=[dst.b])

    def A(fn, r, w):
        return act.op(fn, r, w)

    def V(fn, r, w):
        return dve.op(fn, r, w)

    def MM(out, lhsT, rhs, start, stop, reads, bk):
        return pe.op(lambda h: h.matmul(out, lhsT, rhs, start=start, stop=stop), reads, [bk.b], inc=stop)

    def TR(out, in_, ident, reads, bk, inc=True):
        return pe.op(lambda h: h.transpose(out, in_, ident), reads, [bk.b], inc=inc)

    nwi = [0]

    def load_nw(idx):
        nw = NW[nwi[0] % 2]
        nwi[0] += 1
        IOH[0].dma(nw.t[:], nv[idx:idx + 1, :].partition_broadcast(128).rearrange("p a b -> p (a b)"), writes=[nw.b])
        return nw

    def rmsnorm_T(Pt, NS, nidx):
        nw = load_nw(nidx)
        for s in range(NS):
            A(lambda h: h.activation(out=XNB[0].t[0:Pt, :], in_=X.t[0:Pt, s, :], func=AF.Square, accum_out=SS.t[0:Pt, s:s + 1]),
              [X.b], [XNB[0].b, SS.b])
        A(lambda h: h.activation(out=RS.t[0:Pt, 0:NS], in_=SS.t[0:Pt, 0:NS], func=AF.Ln, scale=1.0 / D, bias=EPS), [SS.b], [RS.b])
        A(lambda h: h.activation(out=RS.t[0:Pt, 0:NS], in_=RS.t[0:Pt, 0:NS], func=AF.Exp, scale=-0.5), [RS.b], [RS.b])
        for s in range(NS):
            xb = XNB[0]
            V(lambda h: h.scalar_tensor_tensor(out=xb.t[0:Pt, :], in0=X.t[0:Pt, s, :], scalar=RS.t[0:Pt, s:s + 1], in1=nw.t[0:Pt, :],
                                               op0=ALU.mult, op1=ALU.mult), [X.b, RS.b, nw.b], [xb.b])
            bk = K.bank()
            psb = bk.t[:].bitcast(BF16)
            for c in range(8):
                TR(psb[:, c * 128:c * 128 + Pt], xb.t[0:Pt, c * 128:(c + 1) * 128], IDB.t[0:Pt, 0:Pt], [xb.b, IDB.b], bk, inc=(c == 7))
            A(lambda h: h.activation(out=XNT.t[:, :, s * 128:s * 128 + Pt], in_=psb.rearrange("p (c t) -> p c t", c=8)[:, :, 0:Pt], func=AF.Copy),
              [bk.b], [XNT.b])

    def post_norm_res(Pt, s, p0, p1, nw, YT, scale):
        A(lambda h: h.activation(out=YT.t[0:Pt, 0:512], in_=p0.t[0:Pt, :], func=AF.Square, accum_out=SS2.t[0:Pt, 0:1]), [p0.b], [YT.b, SS2.b])
        A(lambda h: h.activation(out=YT.t[0:Pt, 512:1024], in_=p1.t[0:Pt, :], func=AF.Square, accum_out=SS2.t[0:Pt, 1:2]), [p1.b], [YT.b, SS2.b])
        V(lambda h: h.tensor_tensor(out=SS2.t[0:Pt, 2:3], in0=SS2.t[0:Pt, 0:1], in1=SS2.t[0:Pt, 1:2], op=ALU.add), [SS2.b], [SS2.b])
        A(lambda h: h.activation(out=RS2.t[0:Pt, 0:1], in_=SS2.t[0:Pt, 2:3], func=AF.Ln, scale=1.0 / D, bias=EPS), [SS2.b], [RS2.b])
        A(lambda h: h.activation(out=RS2.t[0:Pt, 0:1], in_=RS2.t[0:Pt, 0:1], func=AF.Exp, scale=-0.5), [RS2.b], [RS2.b])
        for half, ph in enumerate((p0, p1)):
            V(lambda h: h.scalar_tensor_tensor(out=YT.t[0:Pt, half * 512:(half + 1) * 512], in0=ph.t[0:Pt, :], scalar=RS2.t[0:Pt, 0:1],
                                               in1=nw.t[0:Pt, half * 512:(half + 1) * 512], op0=ALU.mult, op1=ALU.mult),
              [ph.b, RS2.b, nw.b], [YT.b])
        V(lambda h: h.scalar_tensor_tensor(out=X.t[0:Pt, s, :], in0=YT.t[0:Pt, :], scalar=scale, in1=X.t[0:Pt, s, :],
                                           op0=ALU.mult, op1=ALU.add), [YT.b, X.b], [X.b])

    def ffn(fi, T_, Pt, NS, npre, npost):
        AR.reset()
        HT = AR.alloc("ht", [128, NFC, T_], BF16)
        WDp = [AR.alloc("wd%d" % j, [128, 2, D], BF16) for j in range(11)]
        SG = [AR.alloc("sg%d" % i, [128, 512], F32) for i in range(2)]
        YT = AR.alloc("yt", [128, D], F32)
        rmsnorm_T(Pt, NS, npre)
        nwp = load_nw(npost)
        FB = 1 if T_ >= 512 else 512 // T_
        nb = 0
        for j in range(11):
            slot = wnext(1)[0]
            wd_load(fi, j, WDp[j])
            for f2 in range(2):
                fc = 2 * j + f2
                if nb == 0:
                    pg = K.bank()
                    pu = K.bank()
                    fc0 = fc
                col = nb * T_
                for k in range(8):
                    MM(pg.t[:, col:col + T_], slot.t[:, k * 256 + f2 * 128:k * 256 + f2 * 128 + 128], XNT.t[:, k, 0:T_], k == 0, k == 7, [slot.b, XNT.b], pg)
                for k in range(8):
                    MM(pu.t[:, col:col + T_], slot.t[:, 2048 + k * 256 + f2 * 128:2048 + k * 256 + f2 * 128 + 128], XNT.t[:, k, 0:T_], k == 0, k == 7,
                       [slot.b, XNT.b], pu)
                nb += 1
                if nb == FB or fc == NFC - 1:
                    W_ = nb * T_
                    sg = SG[(fc // FB) % 2]
                    A(lambda h: h.activation(out=sg.t[:, 0:W_], in_=pg.t[:, 0:W_], func=AF.Silu), [pg.b], [sg.b])
                    V(lambda h: h.tensor_tensor(out=HT.t[:, fc0:fc0 + nb, :].rearrange("p a b -> p (a b)"), in0=sg.t[:, 0:W_], in1=pu.t[:, 0:W_], op=ALU.mult),
                      [sg.b, pu.b], [HT.b])
                    nb = 0
        for s in range(NS):
            p0 = K.bank()
            p1 = K.bank()
            for half, ph in enumerate((p0, p1)):
                for fc in range(NFC):
                    MM(ph.t[0:Pt, :], HT.t[:, fc, s * 128:s * 128 + Pt], WDp[fc // 2].t[:, fc % 2, half * 512:(half + 1) * 512],
                       fc == 0, fc == NFC - 1, [HT.b, WDp[fc // 2].b], ph)
            post_norm_res(Pt, s, p0, p1, nwp, YT, 0.5)

    def mixer(T_, Pt, NS, nseq, first, cst):
        sh = nseq
        TRI, SAm, NEG4, SEL, MASKB = cst
        AR.reset()
        YRG = AR.alloc("yrg", [128, 8, T_], BF16)
        SZ = AR.alloc("sz", [128, NS, 2048], BF16)
        XS = AR.alloc("xs", [128, NS, 2048], BF16)
        BT = AR.alloc("bt", [128, 8, T_], BF16)
        CT = AR.alloc("ct", [128, 8, T_], BF16)
        YST = AR.alloc("yst", [128, 16, T_], BF16)
        DT = AR.alloc("dt", [128, NS, 32], F32)
        DTA = AR.alloc("dta", [128, NS, 32], F32)
        CDa = AR.alloc("cda", [128, 16, NS * nseq], F32)
        TOT = AR.alloc("tot", [128, NS * nseq], F32)
        mark = AR.off
        WK = [AR.alloc("wk%d" % i, [128, 48 + T_], F32) for i in range(3)]
        XCs = [AR.alloc("xc%d" % i, [128, T_], F32) for i in range(2)]
        XCBs = [AR.alloc("xcb%d" % i, [128, T_], BF16) for i in range(2)]
        RT = [{n: AR.alloc("t_%s%d" % (n, i), [128, T_], F32) for n in ("sa", "sx", "mu", "h", "gg")} for i in range(2)]

        rmsnorm_T(Pt, NS, 2)
        nwp = load_nw(3)
        wi = [0]

        def conv(wk, base_w, cidx, bias_col, XC):
            if bias_col is None:
                V(lambda h: h.tensor_scalar(out=XC.t[:, 0:T_], in0=wk.t[:, 0:T_], scalar1=PF.t[:, base_w + cidx * 4:base_w + cidx * 4 + 1],
                                            scalar2=None, op0=ALU.mult), [wk.b, PF.b], [XC.b])
            else:
                V(lambda h: h.tensor_scalar(out=XC.t[:, 0:T_], in0=wk.t[:, 0:T_], scalar1=PF.t[:, base_w + cidx * 4:base_w + cidx * 4 + 1],
                                            scalar2=PF.t[:, bias_col:bias_col + 1], op0=ALU.mult, op1=ALU.add), [wk.b, PF.b], [XC.b])
            for k in range(1, 4):
                V(lambda h: h.scalar_tensor_tensor(out=XC.t[:, 0:T_], in0=wk.t[:, k * sh:k * sh + T_],
                                                   scalar=PF.t[:, base_w + cidx * 4 + k:base_w + cidx * 4 + k + 1], in1=XC.t[:, 0:T_],
                                                   op0=ALU.mult, op1=ALU.add), [wk.b, PF.b, XC.b], [XC.b])

        def proj_to_wk(slot, coloff, ncols_blk, ci, carry, cidx, on_dve=False):
            wk = WK[wi[0] % 3]
            wi[0] += 1
            A(lambda h: h.activation(out=wk.t[:, 0:3 * sh], in_=carry.t[:, cidx, 0:3 * sh], func=AF.Copy), [carry.b], [wk.b])
            bk = K.bank()
            for k in range(8):
                o = coloff + k * ncols_blk + ci * 128
                MM(bk.t[:, 0:T_], slot.t[:, o:o + 128], XNT.t[:, k, 0:T_], k == 0, k == 7, [slot.b, XNT.b], bk)
            if on_dve:
                V(lambda h: h.tensor_copy(wk.t[:, 3 * sh:3 * sh + T_], bk.t[:, 0:T_]), [bk.b], [wk.b])
            else:
                A(lambda h: h.activation(out=wk.t[:, 3 * sh:3 * sh + T_], in_=bk.t[:, 0:T_], func=AF.Copy), [bk.b], [wk.b])
            A(lambda h: h.activation(out=carry.t[:, cidx, 0:3 * sh], in_=wk.t[:, T_:T_ + 3 * sh], func=AF.Copy), [wk.b], [carry.b])
            return wk

        rgw = RGW
        rg_slots = {}
        rg_wk = {}

        def rgA(c):
            ci = c % 2
            if ci == 0:
                rg_slots[c // 2] = wnext(1, hold=1)[0]
            slot = rg_slots[c // 2]
            rg_wk[c] = proj_to_wk(slot, 0, 256, ci, RGC, c, on_dve=False)

        def rgA1b(c):
            conv(rg_wk.pop(c), PF_RGCW, c, PF_RGCB + c, XCs[c % 2])

        def rgCast(c):
            XC = XCs[c % 2]
            XCB = XCBs[c % 2]
            A(lambda h: h.activation(out=XCB.t[:, 0:T_], in_=XC.t[:, 0:T_], func=AF.Copy), [XC.b], [XCB.b])

        def rgA2(c):
            ci = c % 2
            slot = rg_slots[c // 2]
            XCB = XCBs[c % 2]
            pa = K.bank()
            MM(pa.t[:, 0:T_], rgw.t[:, c * 128:(c + 1) * 128], XCB.t[:, 0:T_], True, True, [rgw.b, XCB.b], pa)
            px = K.bank()
            MM(px.t[:, 0:T_], rgw.t[:, 1024 + c * 128:1024 + (c + 1) * 128], XCB.t[:, 0:T_], True, True, [rgw.b, XCB.b], px)
            pg = K.bank()
            for k in range(8):
                o = 2048 + k * 256 + ci * 128
                MM(pg.t[:, 0:T_], slot.t[:, o:o + 128], XNT.t[:, k, 0:T_], k == 0, k == 7, [slot.b, XNT.b], pg)
            return (pa, px, pg)

        def rgB(c, pa, px, pg):
            XC = XCs[c % 2]
            R_ = RT[c % 2]
            t_sa, t_sx, t_mu, t_h, t_gg = R_["sa"], R_["sx"], R_["mu"], R_["h"], R_["gg"]
            t_a = t_sa
            A(lambda h: h.activation(out=t_sa.t[:, 0:T_], in_=pa.t[:, 0:T_], func=AF.Tanh, scale=0.5, bias=HPF.t[:, c:c + 1]),
              [pa.b, HPF.b], [t_sa.b])
            A(lambda h: h.activation(out=t_sx.t[:, 0:T_], in_=px.t[:, 0:T_], func=AF.Tanh, scale=0.5, bias=HPF.t[:, 8 + c:9 + c]),
              [px.b, HPF.b], [t_sx.b])
            A(lambda h: h.activation(out=t_a.t[:, 0:T_], in_=t_sa.t[:, 0:T_], func=AF.Exp, scale=HPF.t[:, 16 + c:17 + c], bias=HPF.t[:, 16 + c:17 + c]),
              [t_sa.b, HPF.b], [t_a.b])
            A(lambda h: h.activation(out=t_gg.t[:, 0:T_], in_=pg.t[:, 0:T_], func=AF.Square), [pg.b], [t_gg.b])
            V(lambda h: h.tensor_scalar(out=t_gg.t[:, 0:T_], in0=t_gg.t[:, 0:T_], scalar1=0.044715, scalar2=1.0, op0=ALU.mult, op1=ALU.add),
              [t_gg.b], [t_gg.b])
            V(lambda h: h.tensor_tensor(out=t_gg.t[:, 0:T_], in0=t_gg.t[:, 0:T_], in1=pg.t[:, 0:T_], op=ALU.mult), [t_gg.b, pg.b], [t_gg.b])
            V(lambda h: h.tensor_tensor(out=t_mu.t[:, 0:T_], in0=t_a.t[:, 0:T_], in1=t_a.t[:, 0:T_], op=ALU.mult), [t_a.b], [t_mu.b])
            V(lambda h: h.tensor_scalar(out=t_mu.t[:, 0:T_], in0=t_mu.t[:, 0:T_], scalar1=1.0, scalar2=None, op0=ALU.min), [t_mu.b], [t_mu.b])
            A(lambda h: h.activation(out=t_gg.t[:, 0:T_], in_=t_gg.t[:, 0:T_], func=AF.Tanh, scale=0.7978845608028654), [t_gg.b], [t_gg.b])
            A(lambda h: h.activation(out=t_mu.t[:, 0:T_], in_=t_mu.t[:, 0:T_], func=AF.Sqrt, scale=-0.25, bias=0.25), [t_mu.b], [t_mu.b])
            V(lambda h: h.scalar_tensor_tensor(out=t_gg.t[:, 0:T_], in0=t_gg.t[:, 0:T_], scalar=1.0, in1=pg.t[:, 0:T_], op0=ALU.add, op1=ALU.mult),
              [t_gg.b, pg.b], [t_gg.b])
            if first:
                V(lambda h: h.memset(t_mu.t[:, 0:1], 0.5), [], [t_mu.b])
            V(lambda h: h.scalar_tensor_tensor(out=XC.t[:, 0:T_], in0=t_sx.t[:, 0:T_], scalar=1.0, in1=XC.t[:, 0:T_], op0=ALU.add, op1=ALU.mult),
              [XC.b, t_sx.b], [XC.b])
            V(lambda h: h.tensor_tensor(out=XC.t[:, 0:T_], in0=XC.t[:, 0:T_], in1=t_mu.t[:, 0:T_], op=ALU.mult), [XC.b, t_mu.b], [XC.b])
            if nseq == 1:
                V(lambda h: h.tensor_tensor_scan(out=t_h.t[:, 0:T_], data0=t_a.t[:, 0:T_], data1=XC.t[:, 0:T_], initial=HC.t[:, c, 0:1],
                                                 op0=ALU.mult, op1=ALU.add), [t_a.b, XC.b, HC.b], [t_h.b])
            else:
                for t in range(T_ // nseq):
                    prev = HC.t[:, c, 0:nseq] if t == 0 else t_h.t[:, (t - 1) * nseq:t * nseq]
                    V(lambda h: h.tensor_tensor(out=t_h.t[:, t * nseq:(t + 1) * nseq], in0=t_a.t[:, t * nseq:(t + 1) * nseq], in1=prev, op=ALU.mult),
                      [t_a.b, HC.b, t_h.b], [t_h.b])
                    V(lambda h: h.tensor_tensor(out=t_h.t[:, t * nseq:(t + 1) * nseq], in0=t_h.t[:, t * nseq:(t + 1) * nseq],
                                                in1=XC.t[:, t * nseq:(t + 1) * nseq], op=ALU.add), [t_h.b, XC.b], [t_h.b])
            A(lambda h: h.activation(out=HC.t[:, c, 0:nseq], in_=t_h.t[:, T_ - nseq:T_], func=AF.Copy), [t_h.b], [HC.b])
            V(lambda h: h.scalar_tensor_tensor(out=YRG.t[:, c, 0:T_], in0=t_gg.t[:, 0:T_], scalar=0.5, in1=t_h.t[:, 0:T_], op0=ALU.mult, op1=ALU.mult),
              [t_h.b, t_gg.b], [YRG.b])

        rgA(0)
        rgA1b(0)
        rgCast(0)
        rgA(1)
        pend = rgA2(0)
        for c in range(8):
            if c + 1 < 8:
                rgA1b(c + 1)
            rgB(c, *pend)
            if c + 1 < 8:
                rgCast(c + 1)
            if c + 2 < 8:
                rgA(c + 2)
            pend = rgA2(c + 1) if c + 1 < 8 else None

        for zc in range(4):
            slot = wnext(1)[0]
            for s in range(NS):
                bk = K.bank()
                for k in range(8):
                    MM(bk.t[0:Pt, :], XNT.t[:, k, s * 128:s * 128 + Pt], slot.t[:, k * 512:(k + 1) * 512], k == 0, k == 7, [slot.b, XNT.b], bk)
                A(lambda h: h.activation(out=SZ.t[0:Pt, s, zc * 512:(zc + 1) * 512], in_=bk.t[0:Pt, :], func=AF.Silu), [bk.b], [SZ.b])
        slot = wnext(1)[0]
        VV = AR.alloc("vv", [128, NS, 32], F32)
        AVt = AR.alloc("av", [128, NS, 32], F32)
        for s in range(NS):
            bk = K.bank()
            for k in range(8):
                MM(bk.t[0:Pt, 0:32], XNT.t[:, k, s * 128:s * 128 + Pt], slot.t[:, k * 32:(k + 1) * 32], k == 0, k == 7, [slot.b, XNT.b], bk)
            V(lambda h: h.tensor_tensor(out=VV.t[0:Pt, s, :], in0=bk.t[0:Pt, 0:32], in1=DTB.t[0:Pt, :], op=ALU.add), [bk.b, DTB.b], [VV.b])
        V(lambda h: h.scalar_tensor_tensor(out=AVt.t[0:Pt], in0=VV.t[0:Pt], scalar=-1.0, in1=VV.t[0:Pt], op0=ALU.mult, op1=ALU.max), [VV.b], [AVt.b])
        A(lambda h: h.activation(out=AVt.t[0:Pt], in_=AVt.t[0:Pt], func=AF.Exp, scale=-1.0), [AVt.b], [AVt.b])
        A(lambda h: h.activation(out=AVt.t[0:Pt], in_=AVt.t[0:Pt], func=AF.Ln, bias=1.0), [AVt.b], [AVt.b])
        V(lambda h: h.scalar_tensor_tensor(out=DT.t[0:Pt], in0=VV.t[0:Pt], scalar=0.0, in1=AVt.t[0:Pt], op0=ALU.max, op1=ALU.add),
          [VV.b, AVt.b], [DT.b])
        V(lambda h: h.tensor_tensor(out=DTA.t[0:Pt], in0=DT.t[0:Pt], in1=bc(ABC.t[0:Pt, :].unsqueeze(1), [Pt, NS, 32]), op=ALU.mult),
          [DT.b, ABC.b], [DTA.b])

        J = NS * nseq
        bk = K.bank()
        for s in range(NS):
            MM(bk.t[0:32, s * nseq:(s + 1) * nseq], DTA.t[0:Pt, s, :], SEL[0:Pt, 0:nseq], True, True, [DTA.b, CONST.b], bk)
        A(lambda h: h.activation(out=TOT.t[0:32, 0:J], in_=bk.t[0:32, 0:J], func=AF.Exp), [bk.b], [TOT.b])
        IOH[0].dma(CDSCR[:, 0:J], TOT.t[0:32, 0:J], reads=[TOT.b], writes=[CDSB])
        for qh in range(2):
            src_ap = CDSCR.rearrange("(hc two) j -> two hc j", two=2)[qh][:, 0:J].unsqueeze(0).broadcast_to([64, 16, J])
            IOH[0].dma(CDa.t[qh * 64:(qh + 1) * 64, :, :], src_ap, reads=[CDSB], writes=[CDa.b], add=(qh > 0))

        XSBs = [AR.alloc("xsb%d" % i, [128, T_], BF16) for i in range(4)]
        x_slot = [None]
        x_ev = {}

        def xA(cc):
            ci = cc % 4
            if ci == 0:
                x_slot[0] = wnext(1)[0]
            return proj_to_wk(x_slot[0], 0, 512, ci, XBCC, cc)

        def xB(cc, wk):
            XC = XCs[cc % 2]
            XSB = XSBs[cc % 4]
            conv(wk, PF_SCW, cc, None, XC)
            bcol = PF.t[:, PF_SCB + cc:PF_SCB + cc + 1]
            if cc < 16:
                A(lambda h: h.activation(out=XSB.t[:, 0:T_], in_=XC.t[:, 0:T_], func=AF.Silu, bias=bcol), [XC.b, PF.b], [XSB.b])
            elif cc < 24:
                A(lambda h: h.activation(out=BT.t[:, cc - 16, 0:T_], in_=XC.t[:, 0:T_], func=AF.Silu, bias=bcol), [XC.b, PF.b], [BT.b])
            else:
                A(lambda h: h.activation(out=CT.t[:, cc - 24, 0:T_], in_=XC.t[:, 0:T_], func=AF.Silu, bias=bcol), [XC.b, PF.b], [CT.b])

        def xTR(cc):
            if cc < 0 or cc >= 16:
                return
            XSB = XSBs[cc % 4]
            bk = K.bank()
            psb = bk.t[:].bitcast(BF16)
            for s in range(NS):
                TR(psb[0:Pt, s * 128:(s + 1) * 128], XSB.t[:, s * 128:s * 128 + Pt], IDB.t[:, :], [XSB.b, IDB.b], bk, inc=(s == NS - 1))
            K.pinned.add(K.banks.index(bk))
            x_ev[cc] = bk

        def xB2(cc):
            bk = x_ev.pop(cc, None)
            if bk is None:
                return
            psb = bk.t[:].bitcast(BF16)
            A(lambda h: h.activation(out=XS.t[0:Pt, :, cc * 128:(cc + 1) * 128],
                                     in_=psb[0:Pt, 0:NS * 128].rearrange("p (s c) -> p s c", s=NS), func=AF.Copy), [bk.b], [XS.b])
            K.unpin(bk)

        xpend = {0: xA(0), 1: xA(1)}
        for cc in range(32):
            if cc + 2 < 32:
                xpend[cc + 2] = xA(cc + 2)
            xB(cc, xpend.pop(cc))
            xTR(cc - 2)
            xB2(cc - 3)
        xTR(30); xTR(31)
        for cc in range(29, 32):
            xB2(cc)

        AR.off = mark
        K.snapshot()
        Pq = Pt
        XDT = AR.alloc("xdt", [128, 2048], BF16)
        XDD = AR.alloc("xdd", [128, 2048], BF16)
        BTM = AR.alloc("btm", [128, 1024], BF16)
        STTs = [AR.alloc("stt%d" % i, [128, 2048], BF16) for i in range(2 if nseq > 1 else 1)]
        if nseq == 1:
            STTs = STTs * 2
        CBSs = [AR.alloc("cbs%d" % i, [128, 128], F32) for i in range(2)]
        RHSPs = [AR.alloc("rhsp%d" % i, [128, 4, Pq], F32) for i in range(2)]
        LTs = [AR.alloc("lt%d" % i, [128, 4, Pq], F32) for i in range(2)]
        WTs = [AR.alloc("wt%d" % i, [128, 4, Pq], BF16) for i in range(2)]
        T1 = AR.alloc("t1", [128, 256], F32)
        T2 = AR.alloc("t2", [128, 256], F32)
        Y = AR.alloc("y", [128, 2048], F32)
        YN = XDD
        GS = AR.alloc("gs", [128, 8], F32)
        if nseq > 1:
            MKB = AR.alloc("maskb", [128, 1024], F32)
            IOH[0].dma(MKB.t[:], maskb_d, writes=[MKB.b])
            MASKB = MKB.t
            S0 = [AR.alloc("s0_%d" % i, [128, 16, 128], F32) for i in range(3)]
            SO = [AR.alloc("so_%d" % i, [128, 16, 128], F32) for i in range(2)]
            CTMs = [AR.alloc("ctm%d" % i, [128, 8, Pq], BF16) for i in range(2)]
            BMs = [AR.alloc("bm%d" % i, [128, 1024], BF16) for i in range(2)]

        EEa = AR.alloc("eea", [128, NS, 64], F32)
        DDa = AR.alloc("dda", [128, NS, 32], F32)
        for s in range(NS):
            dta_q = DTA.t[0:Pq, s, :]
            bk = K.bank()
            MM(bk.t[0:Pq, 0:32], TRI[0:Pq, 0:Pq], dta_q, True, True, [CONST.b, DTA.b], bk)
            MM(bk.t[0:Pq, 32:64], SAm[0:Pq, 0:Pq], dta_q, True, True, [CONST.b, DTA.b], bk)
            A(lambda h: h.activation(out=EEa.t[0:Pq, s, :], in_=bk.t[0:Pq, 0:64], func=AF.Exp), [bk.b], [EEa.b])
            V(lambda h: h.tensor_tensor(out=DDa.t[0:Pq, s, :], in0=DT.t[0:Pq, s, :], in1=EEa.t[0:Pq, s, 32:64], op=ALU.mult), [DT.b, EEa.b], [DDa.b])

        for s in range(NS):
            c0 = s * 128
            EE = T(EEa.t[:, s, :], EEa.b)
            DD = T(DDa.t[:, s, :], DDa.b)
            xs3 = XS.t[0:Pq, s, :].rearrange("p (h d) -> p h d", h=32)
            V(lambda h: h.tensor_tensor(out=XDT.t[0:Pq, :].rearrange("p (h d) -> p h d", h=32), in0=xs3,
                                        in1=bc(DT.t[0:Pq, s, :].unsqueeze(2), [Pq, 32, 64]), op=ALU.mult), [XS.b, DT.b], [XDT.b])
            V(lambda h: h.tensor_tensor(out=XDD.t[0:Pq, :].rearrange("p (h d) -> p h d", h=32), in0=xs3,
                                        in1=bc(DD.t[0:Pq, :].unsqueeze(2), [Pq, 32, 64]), op=ALU.mult), [XS.b, DD.b], [XDD.b])
            bk = K.bank()
            psb = bk.t[:].bitcast(BF16)
            for g in range(8):
                TR(psb[0:Pq, g * 128:(g + 1) * 128], BT.t[:, g, c0:c0 + Pq], IDB.t[:, :], [BT.b, IDB.b], bk, inc=(g == 7))
            A(lambda h: h.activation(out=BTM.t[0:Pq, :], in_=psb[0:Pq, :], func=AF.Copy), [bk.b], [BTM.b])

            YO = [K.bank(pin=True) for _ in range(4)]
            def sL(b):
                if nseq > 1:
                    IOH[0].dma(S0[b % 3].t[:], sss[b].rearrange("(hc q) n -> q hc n", q=128), writes=[S0[b % 3].b])

            def sA(b):
                Sb = S0[b % 3] if nseq > 1 else S
                stt = STTs[b % 2]
                for hq in range(4):
                    bk = K.bank()
                    for hi in range(4):
                        hc = 4 * hq + hi
                        TR(bk.t[:, hi * 128:(hi + 1) * 128], Sb.t[:, hc, :], ID, [Sb.b, CONST.b], bk, inc=(hi == 3))
                    A(lambda h: h.activation(out=stt.t[:, hq * 512:(hq + 1) * 512], in_=bk.t[:, :], func=AF.Copy), [bk.b], [stt.b])
                if nseq > 1:
                    ctm = CTMs[b % 2]
                    bm = BMs[b % 2]
                    V(lambda h: h.tensor_tensor(out=ctm.t[:, :, :], in0=CT.t[:, :, c0:c0 + Pq],
                                                in1=bc(MASKB[:, b * Pq:(b + 1) * Pq].unsqueeze(1), [128, 8, Pq]), op=ALU.mult),
                      [CT.b, MKB.b], [ctm.b])
                    V(lambda h: h.tensor_scalar(out=bm.t[0:Pq, :], in0=BTM.t[0:Pq, :], scalar1=SEL[0:Pq, b:b + 1], scalar2=None, op0=ALU.mult),
                      [BTM.b, CONST.b], [bm.b])

            def sB(b):
                Sb = S0[b % 3] if nseq > 1 else S
                Sn = SO[b % 2] if nseq > 1 else S
                stt = STTs[b % 2]
                for g in range(8):
                    lhs = CTMs[b % 2].t[:, g, 0:Pq] if nseq > 1 else CT.t[:, g, c0:c0 + Pq]
                    rb = [CTMs[b % 2].b] if nseq > 1 else [CT.b]
                    yb = YO[g // 2]
                    pe.op(lambda h: h.matmul(yb.t[0:Pq, (g % 2) * 256:(g % 2) * 256 + 256], lhs, stt.t[:, g * 256:(g + 1) * 256],
                                             start=(b == 0 and g % 2 == 0), stop=(b == nseq - 1), skip_group_check=True), rb + [stt.b], [yb.b], inc=True)
                bmt = BMs[b % 2] if nseq > 1 else BTM
                for hq in range(4):
                    bk = K.bank()
                    for hi in range(4):
                        hc = 4 * hq + hi
                        MM(bk.t[:, hi * 128:(hi + 1) * 128], XDD.t[0:Pq, hc * 128:(hc + 1) * 128], bmt.t[0:Pq, (hc // 2) * 128:(hc // 2 + 1) * 128],
                           True, True, [XDD.b, bmt.b], bk)
                    for hi in range(4):
                        hc = 4 * hq + hi
                        V(lambda h: h.scalar_tensor_tensor(out=Sn.t[:, hc, :], in0=Sb.t[:, hc, :], scalar=CDa.t[:, hc, s * nseq + b:s * nseq + b + 1],
                                                           in1=bk.t[:, hi * 128:(hi + 1) * 128], op0=ALU.mult, op1=ALU.add),
                          [Sb.b, CDa.b, bk.b], [Sn.b])
                if nseq > 1:
                    act.dma(o_sss[b].rearrange("(hc q) n -> q hc n", q=128), Sn.t[:], reads=[Sn.b])

            sL(0)
            if nseq > 1:
                sL(1)
            sA(0)
            for b in range(nseq):
                if b + 2 < nseq:
                    sL(b + 2)
                if b + 1 < nseq:
                    sA(b + 1)
                sB(b)

            def gA(g):
                CBS, RHSP, LT = CBSs[g % 2], RHSPs[g % 2], LTs[g % 2]
                bkc = K.bank()
                MM(bkc.t[0:Pq, 0:Pq], BT.t[:, g, c0:c0 + Pq], CT.t[:, g, c0:c0 + Pq], True, True, [BT.b, CT.b], bkc)
                V(lambda h: h.tensor_tensor(out=RHSP.t[0:Pq, :, :], in0=bc(TRI[0:Pq, 0:Pq].unsqueeze(1), [Pq, 4, Pq]),
                                            in1=bc(DTA.t[0:Pq, s, 4 * g:4 * g + 4].unsqueeze(2), [Pq, 4, Pq]), op=ALU.mult),
                  [CONST.b, DTA.b], [RHSP.b])
                bks = K.bank()
                MM(bks.t[0:Pq, 0:4 * Pq], SAm[0:Pq, 0:Pq], RHSP.t[0:Pq, :, :].rearrange("p a b -> p (a b)"), True, False, [CONST.b, RHSP.b], bks)
                MM(bks.t[0:Pq, 0:4 * Pq], ID[0:Pq, 0:Pq], NEG4[0:Pq, 0:4 * Pq], False, True, [CONST.b], bks)
                A(lambda h: h.activation(out=CBS.t[0:Pq, 0:Pq], in_=bkc.t[0:Pq, 0:Pq], func=AF.Copy), [bkc.b], [CBS.b])
                A(lambda h: h.activation(out=LT.t[0:Pq, :, :].rearrange("p a b -> p (a b)"), in_=bks.t[0:Pq, 0:4 * Pq], func=AF.Exp), [bks.b], [LT.b])

            def gB(g):
                CBS, LT, WT = CBSs[g % 2], LTs[g % 2], WTs[g % 2]
                V(lambda h: h.tensor_tensor(out=WT.t[0:Pq, :, :], in0=LT.t[0:Pq, :, :], in1=bc(CBS.t[0:Pq, 0:Pq].unsqueeze(1), [Pq, 4, Pq]),
                                            op=ALU.mult), [LT.b, CBS.b], [WT.b])
                bky = K.bank()
                for hh in range(4):
                    MM(bky.t[0:Pq, hh * 64:(hh + 1) * 64], WT.t[0:Pq, hh, :], XDT.t[0:Pq, (4 * g + hh) * 64:(4 * g + hh + 1) * 64], True, True,
                       [WT.b, XDT.b], bky)
                return bky

            def gC(g, bky):
                yb = YO[g // 2]
                V(lambda h: h.tensor_tensor(out=T1.t[0:Pq, :].rearrange("p (h d) -> p h d", h=4),
                                            in0=yb.t[0:Pq, (g % 2) * 256:(g % 2) * 256 + 256].rearrange("p (h d) -> p h d", h=4),
                                            in1=bc(EE.t[0:Pq, 4 * g:4 * g + 4].unsqueeze(2), [Pq, 4, 64]), op=ALU.mult), [yb.b, EE.b], [T1.b])
                V(lambda h: h.tensor_tensor(out=T2.t[0:Pq, :].rearrange("p (h d) -> p h d", h=4),
                                            in0=XS.t[0:Pq, s, g * 256:(g + 1) * 256].rearrange("p (h d) -> p h d", h=4),
                                            in1=bc(DBC.t[0:Pq, 4 * g:4 * g + 4].unsqueeze(2), [Pq, 4, 64]), op=ALU.mult), [XS.b, DBC.b], [T2.b])
                V(lambda h: h.tensor_tensor(out=T1.t[0:Pq, :], in0=T1.t[0:Pq, :], in1=T2.t[0:Pq, :], op=ALU.add), [T1.b, T2.b], [T1.b])
                V(lambda h: h.tensor_tensor(out=T1.t[0:Pq, :], in0=bky.t[0:Pq, 0:256], in1=T1.t[0:Pq, :], op=ALU.add),
                  [bky.b, T1.b], [T1.b])
                V(lambda h: h.tensor_tensor(out=Y.t[0:Pq, g * 256:(g + 1) * 256], in0=T1.t[0:Pq, :], in1=SZ.t[0:Pq, s, g * 256:(g + 1) * 256], op=ALU.mult),
                  [T1.b, SZ.b], [Y.b])
                A(lambda h: h.activation(out=JUNK.t[0:Pq, 0:256], in_=Y.t[0:Pq, g * 256:(g + 1) * 256], func=AF.Square, accum_out=GS.t[0:Pq, g:g + 1]),
                  [Y.b], [JUNK.b, GS.b])

            bkys = {}
            for i in range(10):
                if i < 8:
                    gA(i)
                if 0 <= i - 1 < 8:
                    bkys[i - 1] = gB(i - 1)
                if 0 <= i - 2 < 8:
                    gC(i - 2, bkys.pop(i - 2))
            for yb in YO:
                K.unpin(yb)
            A(lambda h: h.activation(out=GS.t[0:Pq, :], in_=GS.t[0:Pq, :], func=AF.Ln, scale=1.0 / 256, bias=EPS), [GS.b], [GS.b])
            A(lambda h: h.activation(out=GS.t[0:Pq, :], in_=GS.t[0:Pq, :], func=AF.Exp, scale=-0.5), [GS.b], [GS.b])
            V(lambda h: h.tensor_tensor(out=YN.t[0:Pq, :].rearrange("p (g d) -> p g d", g=8), in0=Y.t[0:Pq, :].rearrange("p (g d) -> p g d", g=8),
                                        in1=bc(GS.t[0:Pq, :].unsqueeze(2), [Pq, 8, 256]), op=ALU.mult), [Y.b, GS.b], [YN.b])
            for half in range(2):
                bk = K.bank()
                psb = bk.t[:].bitcast(BF16)
                for ci in range(8):
                    cc = half * 8 + ci
                    TR(psb[:, ci * 128:ci * 128 + Pq], YN.t[0:Pq, cc * 128:(cc + 1) * 128], IDB.t[0:Pq, 0:Pq], [YN.b, IDB.b], bk, inc=(ci == 7))
                V(lambda h: h.tensor_tensor(out=YST.t[:, half * 8:half * 8 + 8, c0:c0 + Pq],
                                            in0=psb.rearrange("p (c t) -> p c t", c=8)[:, :, 0:Pq],
                                            in1=bc(PF.t[:, PF_SNW + half * 8:PF_SNW + half * 8 + 8].unsqueeze(2), [128, 8, Pq]), op=ALU.mult),
                  [bk.b, PF.b], [YST.b])

        AR.off = mark
        K.snapshot()
        MT = AR.alloc("mt", [128, 8, T_], BF16)
        M1 = AR.alloc("m1", [128, T_], F32)
        M2 = AR.alloc("m2", [128, T_], F32)
        S1 = AR.alloc("s1", [128, T_], F32)
        S2 = AR.alloc("s2", [128, T_], F32)
        YT = AR.alloc("ytm", [128, D], F32)
        for dc2 in range(4):
            sA, sB, sC = wnext(3)
            for di in range(2):
                dc = 2 * dc2 + di
                p1 = K.bank(); g1 = K.bank(); p2 = K.bank(); g2 = K.bank()
                for k in range(8):
                    o = k * 256 + di * 128
                    MM(p1.t[:, 0:T_], sA.t[:, o:o + 128], YRG.t[:, k, 0:T_], k == 0, k == 7, [sA.b, YRG.b], p1)
                for k in range(8):
                    o = 2048 + k * 256 + di * 128
                    MM(g1.t[:, 0:T_], sA.t[:, o:o + 128], XNT.t[:, k, 0:T_], k == 0, k == 7, [sA.b, XNT.b], g1)
                for k in range(16):
                    o = k * 256 + di * 128
                    MM(p2.t[:, 0:T_], sC.t[:, o:o + 128], YST.t[:, k, 0:T_], k == 0, k == 15, [sC.b, YST.b], p2)
                for k in range(8):
                    o = k * 256 + di * 128
                    MM(g2.t[:, 0:T_], sB.t[:, o:o + 128], XNT.t[:, k, 0:T_], k == 0, k == 7, [sB.b, XNT.b], g2)
                A(lambda h: h.activation(out=S1.t[:, 0:T_], in_=g1.t[:, 0:T_], func=AF.Sigmoid), [g1.b], [S1.b])
                A(lambda h: h.activation(out=S2.t[:, 0:T_], in_=g2.t[:, 0:T_], func=AF.Sigmoid), [g2.b], [S2.b])
                V(lambda h: h.tensor_tensor(out=M1.t[:, 0:T_], in0=S1.t[:, 0:T_], in1=p1.t[:, 0:T_], op=ALU.mult), [S1.b, p1.b], [M1.b])
                V(lambda h: h.tensor_tensor(out=M2.t[:, 0:T_], in0=S2.t[:, 0:T_], in1=p2.t[:, 0:T_], op=ALU.mult), [S2.b, p2.b], [M2.b])
                V(lambda h: h.tensor_tensor(out=MT.t[:, dc, 0:T_], in0=M1.t[:, 0:T_], in1=M2.t[:, 0:T_], op=ALU.add), [M1.b, M2.b], [MT.b])
        w0, w1 = wnext(2)
        for s in range(NS):
            p0 = K.bank(); p1 = K.bank()
            for ph, ws in ((p0, w0), (p1, w1)):
                for k in range(8):
                    MM(ph.t[0:Pt, :], MT.t[:, k, s * 128:s * 128 + Pt], ws.t[:, k * 512:(k + 1) * 512], k == 0, k == 7, [MT.b, ws.b], ph)
            post_norm_res(Pt, s, p0, p1, nwp, YT, 1.0)

    def fm_to_rows(src_fn, nchunks, ncols, STw, store_fn):
        for g in range(nchunks // 4):
            bk = K.bank()
            for ci in range(4):
                ap, b = src_fn(4 * g + ci)
                TR(bk.t[0:ncols, ci * 128:(ci + 1) * 128], ap, ID, [b, CONST.b], bk, inc=(ci == 3))
            A(lambda h: h.activation(out=STw.t[0:ncols, g * 512:(g + 1) * 512], in_=bk.t[0:ncols, :], func=AF.Copy), [bk.b], [STw.b])
        store_fn(STw)

    STG = [None, None]
    stg_i = [0]
    TMh = [None]

    def alloc_stg():
        AR.reset()
        STG[0] = AR.alloc("stg0", [128, 512], F32)
        STG[1] = AR.alloc("stg1", [128, 512], F32)
        return [AR.alloc("wide0", [128, 4096], F32), AR.alloc("wide1", [128, 1024], F32), AR.alloc("wide2", [128, 1024], F32)]

    emit_casts()
    for v in (RGC, HC, XBCC, S):
        dve.op(lambda h: h.memset(v.t[:], 0.0), [], [v.b])

    PC = (CONST.t[:, C_PTRI:C_PTRI + 128], CONST.t[:, C_PSA:C_PSA + 128], CONST.t[:, C_PNEG:C_PNEG + 512], CONST.t[:, C_SSEL + 15:C_SSEL + 16], None)
    ONES = K.sb("ones", [128, 1], F32, const=True)
    dve.op(lambda h: h.memset(ONES.t[:], 1.0), [], [ONES.b])
    PC = (PC[0], PC[1], PC[2], ONES.t[:, 0:1], None)
    SC = (CONST.t[:, C_STRI:C_STRI + 64], CONST.t[:, C_SSA:C_SSA + 64], CONST.t[:, C_SNEG:C_SNEG + 256], CONST.t[:, C_SSEL:C_SSEL + 16],
          None)

    for ti in range(NPT):
        t0 = ti * 512
        if ti == 0:
            IOH[0] = act
        IOH[0].dma(X.t[:, :, :], xp[t0:t0 + 512, :].rearrange("(s p) d -> p s d", p=128), writes=[X.b])
        ffn(0, 512, 128, 4, 0, 1)
        mixer(512, 128, 4, 1, ti == 0, PC)
        ffn(1, 512, 128, 4, 4, 5)
        IOH[0].dma(yp[t0:t0 + 512, :].rearrange("(s p) d -> p s d", p=128), X.t[:, :, :], reads=[X.b])
        IOH[0] = pool

    W0, W1, W2 = alloc_stg()
    IOH[0].dma(o_pss.rearrange("(hc q) n -> q hc n", q=128), S.t[:], reads=[S.b])
    bk = K.bank()
    TR(bk.t[0:8, 0:128], HC.t[:, :, 0], ID, [HC.b, CONST.b], bk)
    A(lambda h: h.activation(out=STG[0].t[0:8, 0:128], in_=bk.t[0:8, 0:128], func=AF.Copy), [bk.b], [STG[0].b])
    IOH[0].dma(o_prh, STG[0].t[0:8, 0:128], reads=[STG[0].b])
    fm_to_rows(lambda c: (RGC.t[:, c, 0:3], RGC.b), 8, 3, W1, lambda st: IOH[0].dma(o_prc, st.t[0:3, 0:1024], reads=[st.b]))
    fm_to_rows(lambda c: (XBCC.t[:, c, 0:3], XBCC.b), 32, 3, W0, lambda st: IOH[0].dma(o_psc, st.t[0:3, 0:4096], reads=[st.b]))

    if DO_SAMPLE:
        W0, W1, W2 = alloc_stg()
        for t in range(4):
            IOH[0].dma(X.t[t * 16:(t + 1) * 16, 0, :], xs[:, t, :], writes=[X.b], add=(t > 0))

        def rows_to_fm(load_fn, nrows, nchunks, dst, TMw):
            load_fn(TMw)
            for g in range(nchunks // 4):
                bk = K.bank()
                for ci in range(4):
                    TR(bk.t[:, ci * 48:ci * 48 + nrows], TMw.t[0:nrows, (4 * g + ci) * 128:(4 * g + ci + 1) * 128], ID[0:nrows, 0:nrows],
                       [TMw.b, CONST.b], bk, inc=(ci == 3))
                A(lambda h: h.activation(out=dst.t[:, 4 * g:4 * g + 4, 0:nrows], in_=bk.t[:, 0:192].rearrange("p (c t) -> p c t", c=4)[:, :, 0:nrows],
                                         func=AF.Copy), [bk.b], [dst.b])

        def ld_rows3(srcd, TMw, w):
            for t in range(3):
                IOH[0].dma(TMw.t[t * 16:(t + 1) * 16, 0:w], srcd[:, t, :], writes=[TMw.b], add=(t > 0))

        rows_to_fm(lambda TMw: ld_rows3(src, TMw, 1024), 48, 8, RGC, W1)
        rows_to_fm(lambda TMw: IOH[0].dma(TMw.t[0:16, 0:1024], srh, writes=[TMw.b]), 16, 8, HC, W2)
        rows_to_fm(lambda TMw: ld_rows3(ssc, TMw, 4096), 48, 32, XBCC, W0)

        ffn(0, 64, 64, 1, 0, 1)
        mixer(64, 64, 1, 16, False, SC)
        ffn(1, 64, 64, 1, 4, 5)
        for t in range(4):
            IOH[0].dma(ys[:, t, :], X.t[t * 16:(t + 1) * 16, 0, :], reads=[X.b])
        W0, W1, W2 = alloc_stg()
        fm_to_rows(lambda c: (HC.t[:, c, 0:16], HC.b), 8, 16, W2, lambda st: IOH[0].dma(o_srh, st.t[0:16, 0:1024], reads=[st.b]))

        def st3(dst, st, w):
            for t in range(3):
                IOH[0].dma(dst[:, t, :], st.t[t * 16:(t + 1) * 16, 0:w], reads=[st.b])

        fm_to_rows(lambda c: (RGC.t[:, c, 0:48], RGC.b), 8, 48, W1, lambda st: st3(o_src, st, 1024))
        fm_to_rows(lambda c: (XBCC.t[:, c, 0:48], XBCC.b), 32, 48, W0, lambda st: st3(o_ssc, st, 4096))

    for e in K.engs:
        sp.need(e.sid, e.cnt)
        for i, sid in enumerate(e.dsid):
            sp.need(sid, e.dcnt[i])
    return K


def _consts():
    c = np.zeros((128, C_END), np.float32)
    c[:, C_ID:C_ID + 128] = np.eye(128, dtype=np.float32)
    k = np.arange(128)
    c[:, C_PTRI:C_PTRI + 128] = (k[:, None] <= k[None, :])
    c[:, C_PSA:C_PSA + 128] = (k[:, None] > k[None, :])
    neg = np.where(k[None, :] >= k[:, None], 0.0, -30000.0).astype(np.float32)
    c[:, C_PNEG:C_PNEG + 512] = np.tile(neg, (1, 4))
    q = np.arange(64)
    sq = q % 16
    tq = q // 16
    same = sq[:, None] == sq[None, :]
    c[0:64, C_STRI:C_STRI + 64] = same & (tq[:, None] <= tq[None, :])
    c[0:64, C_SSA:C_SSA + 64] = same & (tq[:, None] > tq[None, :])
    negs = np.where(same & (tq[None, :] >= tq[:, None]), 0.0, -30000.0).astype(np.float32)
    c[0:64, C_SNEG:C_SNEG + 256] = np.tile(negs, (1, 4))
    c[0:64, C_SSEL:C_SSEL + 16] = (sq[:, None] == np.arange(16)[None, :])
    return c


def _maskb():
    sq = np.arange(64) % 16
    mb = (np.arange(16)[:, None] == sq[None, :]).astype(np.float32).reshape(1, 1024)
    return np.ascontiguousarray(np.broadcast_to(mb, (128, 1024)))


def _fm(v, nch):
    return np.ascontiguousarray(v.reshape(nch, 128).T)


def _prep_shared(inp):
    f = lambda a: np.ascontiguousarray(a, dtype=np.float32)
    pf = np.zeros((128, PF_END), np.float32)
    rcw = inp["rg_conv_w"][0]
    pf[:, PF_RGCW:PF_RGCW + 32] = rcw.reshape(4, 8, 128).transpose(2, 1, 0).reshape(128, 32)
    pf[:, PF_RGCB:PF_RGCB + 8] = _fm(inp["rg_conv_b"][0], 8)
    pf[:, PF_BA:PF_BA + 8] = _fm(inp["rg_ba"][0], 8)
    pf[:, PF_BX:PF_BX + 8] = _fm(inp["rg_bx"][0], 8)
    pf[:, PF_LAM:PF_LAM + 8] = _fm(inp["rg_lambda"][0], 8)
    scw = inp["ssd_conv_w"][0]
    pf[:, PF_SCW:PF_SCW + 128] = scw.reshape(4, 32, 128).transpose(2, 1, 0).reshape(128, 128)
    pf[:, PF_SCB:PF_SCB + 32] = _fm(inp["ssd_conv_b"][0], 32)
    pf[:, PF_SNW:PF_SNW + 16] = _fm(inp["ssd_norm_w"][0], 16)
    nv = np.stack([inp["n_ffn1_pre"][0], inp["n_ffn1_post"][0], inp["n_mix_pre"][0], inp["n_mix_post"][0],
                   inp["n_ffn2_pre"][0], inp["n_ffn2_post"][0]], 0)
    pt32 = np.stack([inp["ssd_dt_bias"][0], inp["ssd_a_log"][0], inp["ssd_d"][0]], 0)
    return {
        "wg1": f(inp["ffn1_wg"][0]), "wu1": f(inp["ffn1_wu"][0]), "wd1": f(inp["ffn1_wd"][0]),
        "wg2": f(inp["ffn2_wg"][0]), "wu2": f(inp["ffn2_wu"][0]), "wd2": f(inp["ffn2_wd"][0]),
        "win": f(inp["w_in"][0]), "wa": f(inp["rg_wa"][0]), "wx": f(inp["rg_wx"][0]),
        "wprg": f(inp["w_proj_rg"][0]), "wpssd": f(inp["w_proj_ssd"][0]), "wout": f(inp["w_out"][0]),
        "pf": pf, "nv": f(nv), "pt32": f(pt32), "consts": _consts(), "maskb": _maskb(),
    }


def kernel(**inp):
    inp = {k: np.asarray(v) for k, v in inp.items()}
    nc = bass.Bass("TRN2", target_bir_lowering=False)
    build(nc)
    shared = _prep_shared(inp)
    in_maps = []
    for c in range(8):
        m = dict(shared)
        m["xp"] = np.ascontiguousarray(inp["x_prompt"][c], dtype=np.float32)
        m["xs"] = np.ascontiguousarray(inp["x_sample"][c * 16:(c + 1) * 16], dtype=np.float32)
        m["srh"] = np.ascontiguousarray(inp["state_rg_h"][0, c * 16:(c + 1) * 16], dtype=np.float32)
        m["src"] = np.ascontiguousarray(inp["state_rg_conv"][0, c * 16:(c + 1) * 16], dtype=np.float32)
        m["sss"] = np.ascontiguousarray(inp["state_ssd"][0, c * 16:(c + 1) * 16], dtype=np.float32).reshape(16, 2048, 128)
        m["ssc"] = np.ascontiguousarray(inp["state_ssd_conv"][0, c * 16:(c + 1) * 16], dtype=np.float32)
        in_maps.append(m)
    res = run_bass_kernel_spmd(nc, in_maps, core_ids=list(range(8)))
    R = res.results
    cat = lambda k: np.concatenate([np.asarray(r[k], dtype=np.float32) for r in R], 0)
    y_prompt = np.stack([np.asarray(r["yp"], np.float32) for r in R], 0)
    y_sample = cat("ys")
    p_rg_h = np.stack([np.asarray(r["o_prh"], np.float32).reshape(1024) for r in R], 0)[None]
    p_rg_conv = np.stack([np.asarray(r["o_prc"], np.float32) for r in R], 0)[None]
    p_ssd = np.stack([np.asarray(r["o_pss"], np.float32).reshape(32, 64, 128) for r in R], 0)[None]
    p_ssd_conv = np.stack([np.asarray(r["o_psc"], np.float32) for r in R], 0)[None]
    s_rg_h = cat("o_srh")[None]
    s_rg_conv = cat("o_src")[None]
    s_ssd = cat("o_sss").reshape(128, 32, 64, 128)[None]
    s_ssd_conv = cat("o_ssc")[None]
    return (y_prompt, y_sample, p_rg_h, p_rg_conv, p_ssd, p_ssd_conv, s_rg_h, s_rg_conv, s_ssd, s_ssd_conv)
```

```python
import numpy as np
import concourse.bass as bass
import concourse.mybir as mybir
from concourse.bass_utils import run_bass_kernel_spmd

F32 = mybir.dt.float32
BF16 = mybir.dt.bfloat16
AF = mybir.ActivationFunctionType
ALU = mybir.AluOpType

D = 1024
DFF = 2816
NFC = 22
DIN = 10272
EPS = 1e-6
NSLOT = 4
SLOTC = 4096

C_ID, C_PTRI, C_PSA, C_PNEG, C_STRI, C_SSA, C_SNEG, C_SSEL, C_MASKB, C_END = (
    0, 128, 256, 384, 896, 960, 1024, 1280, 1296, 1296)
PF_RGCW, PF_RGCB, PF_BA, PF_BX, PF_LAM, PF_SCW, PF_SCB, PF_SNW, PF_END = 0, 32, 40, 48, 56, 64, 192, 224, 240


class Buf:
    __slots__ = ("name", "w", "rs", "const")

    def __init__(self, name, init=None, const=False):
        self.name = name
        self.w = {}
        self.rs = dict(init or {})
        self.const = const


class T:
    def __init__(self, t, b):
        self.t = t
        self.b = b

    def __getitem__(self, k):
        return self.t[k]


class Eng:
    def __init__(self, K, h, name, ndma=0, is_pe=False):
        self.K = K
        self.h = h
        self.name = name
        self.is_pe = is_pe
        self.sem = K.nc.alloc_semaphore("s_" + name)
        self.sid = K.newsid(self.sem)
        self.cnt = 0
        self.seen = {}
        self.dsems = [K.nc.alloc_semaphore("d_%s%d" % (name, i)) for i in range(ndma)]
        self.dsid = [K.newsid(s) for s in self.dsems]
        self.dcnt = [0] * ndma
        self.di = 0

    def need(self, sid, val):
        if val <= 0:
            return
        if self.is_pe and sid == self.sid:
            return
        if self.seen.get(sid, 0) >= val:
            return
        self.h.wait_ge(self.K.sems[sid], val)
        self.seen[sid] = val

    def _deps(self, reads, writes):
        for b in reads:
            for sid, v in b.w.items():
                self.need(sid, v)
        for b in writes:
            for sid, v in b.w.items():
                self.need(sid, v)
            for sid, v in b.rs.items():
                self.need(sid, v)

    def _upd(self, tok, reads, writes, add=False):
        for b in writes:
            if add:
                b.w[tok[0]] = max(b.w.get(tok[0], 0), tok[1])
            else:
                b.w = {tok[0]: tok[1]}
                b.rs = {}
        for b in reads:
            if not b.const:
                b.rs[tok[0]] = max(b.rs.get(tok[0], 0), tok[1])

    def op(self, fn, reads=(), writes=(), inc=True):
        self._deps(reads, writes)
        ins = fn(self.h)
        if inc:
            self.cnt += 1
            ins.then_inc(self.sem, 1)
            tok = (self.sid, self.cnt)
        else:
            tok = (self.sid, self.cnt + 1)
        self._upd(tok, reads, writes)
        self.K.nins += 1
        return ins

    def dma(self, out, in_, reads=(), writes=(), add=False):
        self._deps(reads, writes)
        i = self.di % len(self.dsems)
        self.di += 1
        self.need(self.dsid[i], self.dcnt[i])
        ins = self.h.dma_start(out=out, in_=in_)
        self.dcnt[i] += 16
        ins.then_inc(self.dsems[i], 16)
        tok = (self.dsid[i], self.dcnt[i])
        self._upd(tok, reads, writes, add=add)
        self.K.nins += 1
        return ins


class Kern:
    def __init__(self, nc):
        self.nc = nc
        self.sems = []
        self.nins = 0
        self.pe = Eng(self, nc.tensor, "pe", is_pe=True)
        self.act = Eng(self, nc.scalar, "act", ndma=4)
        self.dve = Eng(self, nc.vector, "dve")
        self.pool = Eng(self, nc.gpsimd, "pool", ndma=12)
        self.sp = Eng(self, nc.sync, "sp", ndma=12)
        self.engs = [self.pe, self.act, self.dve, self.pool, self.sp]
        self.banks = []
        for i in range(8):
            t = nc.alloc_psum_tensor("bank%d" % i, [128, 512], F32)
            self.banks.append(T(t, Buf("bank%d" % i)))
        self.bi = 0
        self.pinned = set()
        self.snap = {}
        self.ncnt = 0

    def newsid(self, sem):
        self.sems.append(sem)
        return len(self.sems) - 1

    def snapshot(self):
        s = {}
        for e in self.engs:
            if e.cnt:
                s[e.sid] = e.cnt
            for i, sid in enumerate(e.dsid):
                if e.dcnt[i]:
                    s[sid] = e.dcnt[i]
        self.snap = s

    def bank(self, pin=False):
        for _ in range(16):
            i = self.bi % 8
            self.bi += 1
            if i not in self.pinned:
                if pin:
                    self.pinned.add(i)
                return self.banks[i]
        raise RuntimeError("no bank")

    def unpin(self, bk):
        self.pinned.discard(self.banks.index(bk))

    def sb(self, name, shape, dt, const=False):
        self.ncnt += 1
        t = self.nc.alloc_sbuf_tensor("%s_%d" % (name, self.ncnt), list(shape), dt)
        return T(t, Buf(name, const=const))


class Arena:
    def __init__(self, K, nbytes):
        self.K = K
        self.t = K.nc.alloc_sbuf_tensor("arena", [128, nbytes // 2], BF16)
        self.n = nbytes
        self.off = 0

    def reset(self):
        self.off = 0
        self.K.snapshot()

    def alloc(self, name, shape, dt):
        esz = 4 if dt == F32 else 2
        n = int(np.prod(shape[1:])) * esz
        n = (n + 63) // 64 * 64
        assert self.off + n <= self.n, ("arena overflow", name, self.off, n, self.n)
        v = self.t[:, self.off // 2:(self.off + n) // 2]
        if dt == F32:
            v = v.bitcast(F32)
        cnt = int(np.prod(shape[1:]))
        v = v[:, 0:cnt]
        if len(shape) == 3:
            v = v.rearrange("p (a b) -> p a b", a=shape[1])
        elif len(shape) == 4:
            v = v.rearrange("p (a b c) -> p a b c", a=shape[1], b=shape[2])
        self.off += n
        return T(v, Buf(name, init=self.K.snap))


def bc(ap, shape):
    return ap.broadcast_to(list(shape))


def build(nc, NPT=4, DO_SAMPLE=True, DEBUG=False):
    K = Kern(nc)
    dbg = []
    pe, act, dve, pool, sp = K.pe, K.act, K.dve, K.pool, K.sp
    io = pool

    def din(name, shape):
        return nc.dram_tensor(name, list(shape), F32, kind="ExternalInput").ap()

    def dout(name, shape):
        return nc.dram_tensor(name, list(shape), F32, kind="ExternalOutput").ap()

    xp = din("xp", [2048, D]); xs = din("xs", [16, 4, D])
    srh = din("srh", [16, D]); src = din("src", [16, 3, D])
    sss = din("sss", [16, 2048, 128]); ssc = din("ssc", [16, 3, 4096])
    wg = [din("wg1", [D, DFF]), din("wg2", [D, DFF])]
    wu = [din("wu1", [D, DFF]), din("wu2", [D, DFF])]
    wd = [din("wd1", [DFF, D]), din("wd2", [DFF, D])]
    win = din("win", [D, DIN])
    wa = din("wa", [8, 128, 128]); wx = din("wx", [8, 128, 128])
    wprg = din("wprg", [D, D]); wpssd = din("wpssd", [2048, D]); wout = din("wout", [D, D])
    pf_d = din("pf", [128, PF_END]); nv = din("nv", [6, D])
    pt32 = din("pt32", [3, 32]); consts_d = din("consts", [128, C_END]); maskb_d = din("maskb", [128, 1024])

    yp = dout("yp", [2048, D]); ys = dout("ys", [16, 4, D])
    o_prh = dout("o_prh", [8, 128]); o_prc = dout("o_prc", [3, D])
    o_pss = dout("o_pss", [2048, 128]); o_psc = dout("o_psc", [3, 4096])
    o_srh = dout("o_srh", [16, D]); o_src = dout("o_src", [16, 3, D])
    o_sss = dout("o_sss", [16, 2048, 128]); o_ssc = dout("o_ssc", [16, 3, 4096])

    CONST = K.sb("const", [128, C_END], F32, const=True)
    IDB = K.sb("idb", [128, 128], BF16, const=True)
    PF = K.sb("pf", [128, PF_END], F32, const=True)
    NSP8 = K.sb("nsp8", [128, 8], F32, const=True)
    DTB = K.sb("dtb", [128, 32], F32, const=True)
    ABC = K.sb("abc", [128, 32], F32, const=True)
    DBC = K.sb("dbc", [128, 32], F32, const=True)
    NW = [K.sb("nw%d" % i, [128, D], F32) for i in range(2)]
    X = K.sb("x", [128, 4, D], F32)
    XNB = [K.sb("xnb0", [128, D], BF16)]
    XNT = K.sb("xnt", [128, 8, 512], BF16)
    RING = [K.sb("ring%d" % i, [128, SLOTC], BF16) for i in range(NSLOT)]
    RGC = K.sb("rgc", [128, 8, 48], F32)
    HC = K.sb("hc", [128, 8, 16], F32)
    XBCC = K.sb("xbcc", [128, 32, 48], F32)
    S = K.sb("sst", [128, 16, 128], F32)
    JUNK = K.sb("junk", [128, 256], BF16)
    SS = K.sb("ss", [128, 8], F32)
    RS = K.sb("rs", [128, 8], F32)
    SS2 = K.sb("ss2", [128, 4], F32)
    RS2 = K.sb("rs2", [128, 4], F32)
    TMPS = K.sb("tmps", [128, 64], F32)
    RGW = K.sb("rgw", [128, 2048], BF16, const=True)
    AR = Arena(K, nc.sbuf_bytes_remaining - 2048)

    ID = CONST.t[:, C_ID:C_ID + 128]
    pool.dma(RGW.t[:, 0:1024].rearrange("p (k c) -> p k c", k=8), wa.rearrange("h i j -> i h j"), writes=[RGW.b])
    pool.dma(RGW.t[:, 1024:2048].rearrange("p (k c) -> p k c", k=8), wx.rearrange("h i j -> i h j"), writes=[RGW.b], add=True)

    sp.dma(CONST.t[:], consts_d, writes=[CONST.b])
    sp.dma(PF.t[:], pf_d, writes=[PF.b])
    sp.dma(DTB.t[:], pt32[0:1, :].partition_broadcast(128).rearrange("p a b -> p (a b)"), writes=[DTB.b])
    sp.dma(ABC.t[:], pt32[1:2, :].partition_broadcast(128).rearrange("p a b -> p (a b)"), writes=[ABC.b])
    sp.dma(DBC.t[:], pt32[2:3, :].partition_broadcast(128).rearrange("p a b -> p (a b)"), writes=[DBC.b])
    dve.op(lambda h: h.tensor_copy(IDB.t[:], ID), [CONST.b], [IDB.b])
    act.op(lambda h: h.activation(out=ABC.t[:], in_=ABC.t[:], func=AF.Exp), [ABC.b], [ABC.b])
    dve.op(lambda h: h.tensor_scalar(out=ABC.t[:], in0=ABC.t[:], scalar1=-1.0, scalar2=None, op0=ALU.mult), [ABC.b], [ABC.b])
    act.op(lambda h: h.activation(out=NSP8.t[:], in_=PF.t[:, PF_LAM:PF_LAM + 8], func=AF.Exp, scale=-1.0), [PF.b], [NSP8.b])
    act.op(lambda h: h.activation(out=NSP8.t[:], in_=NSP8.t[:], func=AF.Ln, bias=1.0), [NSP8.b], [NSP8.b])
    dve.op(lambda h: h.tensor_scalar(out=NSP8.t[:], in0=NSP8.t[:], scalar1=-8.0, scalar2=None, op0=ALU.mult), [NSP8.b], [NSP8.b])

    HPF = K.sb("hpf", [128, 24], F32, const=True)
    dve.op(lambda h: h.tensor_scalar(out=HPF.t[:, 0:16], in0=PF.t[:, PF_BA:PF_BA + 16], scalar1=0.5, scalar2=None, op0=ALU.mult), [PF.b], [HPF.b])
    dve.op(lambda h: h.tensor_scalar(out=HPF.t[:, 16:24], in0=NSP8.t[:, 0:8], scalar1=0.5, scalar2=None, op0=ALU.mult), [NSP8.b, HPF.b], [HPF.b])

    def wview(w, r0, nk, c0, ncols):
        return w[r0:r0 + nk * 128, c0:c0 + ncols].rearrange("(k p) c -> p k c", p=128)

    def ffn_blocks(i):
        return [[(0, 8, 256, wview(wg[i], 0, 8, j * 256, 256)), (2048, 8, 256, wview(wu[i], 0, 8, j * 256, 256))]
                for j in range(11)]

    def mixer_blocks():
        bl = []
        for c2 in range(4):
            bl.append([(0, 8, 256, wview(win, 0, 8, c2 * 256, 256)), (2048, 8, 256, wview(win, 0, 8, 1024 + c2 * 256, 256))])
        for zc in range(4):
            bl.append([(0, 8, 512, wview(win, 0, 8, 2048 + zc * 512, 512))])
        bl.append([(0, 8, 32, wview(win, 0, 8, 8192, 32))])
        for i in range(8):
            bl.append([(0, 8, 512, wview(win, 0, 8, 4096 + i * 512, 512))])
        for dc2 in range(4):
            bl.append([(0, 8, 256, wview(wprg, 0, 8, dc2 * 256, 256)), (2048, 8, 256, wview(win, 0, 8, 8224 + dc2 * 256, 256))])
            bl.append([(0, 8, 256, wview(win, 0, 8, 9248 + dc2 * 256, 256))])
            bl.append([(0, 16, 256, wview(wpssd, 0, 16, dc2 * 256, 256))])
        for half in range(2):
            bl.append([(0, 8, 512, wview(wout, 0, 8, half * 512, 512))])
        return bl

    ntiles = NPT + (1 if DO_SAMPLE else 0)
    tile_seq = ffn_blocks(0) + mixer_blocks() + ffn_blocks(1)
    NB = len(tile_seq)
    seq = tile_seq * ntiles
    wst = {"next": 0, "emitted": 0}
    SCR = nc.dram_tensor("wscr", [NB, 128, SLOTC], BF16, kind="Internal").ap()
    SCRB = [Buf("scr%d" % i) for i in range(NB)]
    WDS = nc.dram_tensor("wdscr", [2, 11, 128, 2048], BF16, kind="Internal").ap()
    WDSB = [[Buf("wds%d_%d" % (f, j)) for j in range(11)] for f in range(2)]
    wd_tile = [0, 0]
    CDSCR = nc.dram_tensor("cdscr", [32, 64], F32, kind="Internal").ap()
    CDSB = Buf("cdscr")

    def wnext(n=1, hold=0):
        r0 = wst["next"]
        wst["next"] += n
        lim = min(len(seq), r0 + NSLOT - hold)
        while wst["emitted"] < lim:
            r = wst["emitted"]
            slot = RING[r % NSLOT]
            used = max(off + nk * ncols for (off, nk, ncols, _) in seq[r])
            if r < NB or ntiles == 1:
                for ii, (off, nk, ncols, src_ap) in enumerate(seq[r]):
                    dst = slot.t[:, off:off + nk * ncols].rearrange("p (k c) -> p k c", k=nk)
                    pool.dma(dst, src_ap, writes=[slot.b], add=(ii > 0))
                if ntiles > 1:
                    sp.dma(SCR[r, :, 0:used], slot.t[:, 0:used], reads=[slot.b], writes=[SCRB[r]])
            else:
                sp.dma(slot.t[:, 0:used], SCR[r % NB, :, 0:used], reads=[SCRB[r % NB]], writes=[slot.b])
            wst["emitted"] += 1
        return [RING[(r0 + i) % NSLOT] for i in range(n)]

    def wd_load(fi, j, dst):
        first = wd_tile[fi] < 11
        wd_tile[fi] += 1
        d2 = dst.t[:].rearrange("p k c -> p (k c)")
        if first:
            pool.dma(dst.t[:], wd[fi][j * 256:(j + 1) * 256, :].rearrange("(k p) c -> p k c", p=128), writes=[dst.b])
            if ntiles > 1:
                sp.dma(WDS[fi, j], d2, reads=[dst.b], writes=[WDSB[fi][j]])
        else:
            sp.dma(d2, WDS[fi, j], reads=[WDSB[fi][j]], writes=[dst.b])

    def A(fn, r, w):
        return act.op(fn, r, w)

    def V(fn, r, w):
        return dve.op(fn, r, w)

    def MM(out, lhsT, rhs, start, stop, reads, bk):
        return pe.op(lambda h: h.matmul(out, lhsT, rhs, start=start, stop=stop), reads, [bk.b], inc=stop)

    def TR(out, in_, ident, reads, bk, inc=True):
        return pe.op(lambda h: h.transpose(out, in_, ident), reads, [bk.b], inc=inc)

    nwi = [0]

    def load_nw(idx):
        nw = NW[nwi[0] % 2]
        nwi[0] += 1
        io.dma(nw.t[:], nv[idx:idx + 1, :].partition_broadcast(128).rearrange("p a b -> p (a b)"), writes=[nw.b])
        return nw

    def rmsnorm_T(Pt, NS, nidx):
        nw = load_nw(nidx)
        for s in range(NS):
            A(lambda h: h.activation(out=XNB[0].t[0:Pt, :], in_=X.t[0:Pt, s, :], func=AF.Square, accum_out=SS.t[0:Pt, s:s + 1]),
              [X.b], [XNB[0].b, SS.b])
        A(lambda h: h.activation(out=RS.t[0:Pt, 0:NS], in_=SS.t[0:Pt, 0:NS], func=AF.Ln, scale=1.0 / D, bias=EPS), [SS.b], [RS.b])
        A(lambda h: h.activation(out=RS.t[0:Pt, 0:NS], in_=RS.t[0:Pt, 0:NS], func=AF.Exp, scale=-0.5), [RS.b], [RS.b])
        for s in range(NS):
            xb = XNB[0]
            V(lambda h: h.scalar_tensor_tensor(out=xb.t[0:Pt, :], in0=X.t[0:Pt, s, :], scalar=RS.t[0:Pt, s:s + 1], in1=nw.t[0:Pt, :],
                                               op0=ALU.mult, op1=ALU.mult), [X.b, RS.b, nw.b], [xb.b])
            bk = K.bank()
            psb = bk.t[:].bitcast(BF16)
            for c in range(8):
                TR(psb[:, c * 128:c * 128 + Pt], xb.t[0:Pt, c * 128:(c + 1) * 128], IDB.t[0:Pt, 0:Pt], [xb.b, IDB.b], bk, inc=(c == 7))
            A(lambda h: h.activation(out=XNT.t[:, :, s * 128:s * 128 + Pt], in_=psb.rearrange("p (c t) -> p c t", c=8)[:, :, 0:Pt], func=AF.Copy),
              [bk.b], [XNT.b])

    def post_norm_res(Pt, s, p0, p1, nw, YT, scale):
        A(lambda h: h.activation(out=YT.t[0:Pt, 0:512], in_=p0.t[0:Pt, :], func=AF.Square, accum_out=SS2.t[0:Pt, 0:1]), [p0.b], [YT.b, SS2.b])
        A(lambda h: h.activation(out=YT.t[0:Pt, 512:1024], in_=p1.t[0:Pt, :], func=AF.Square, accum_out=SS2.t[0:Pt, 1:2]), [p1.b], [YT.b, SS2.b])
        V(lambda h: h.tensor_tensor(out=SS2.t[0:Pt, 2:3], in0=SS2.t[0:Pt, 0:1], in1=SS2.t[0:Pt, 1:2], op=ALU.add), [SS2.b], [SS2.b])
        A(lambda h: h.activation(out=RS2.t[0:Pt, 0:1], in_=SS2.t[0:Pt, 2:3], func=AF.Ln, scale=1.0 / D, bias=EPS), [SS2.b], [RS2.b])
        A(lambda h: h.activation(out=RS2.t[0:Pt, 0:1], in_=RS2.t[0:Pt, 0:1], func=AF.Exp, scale=-0.5), [RS2.b], [RS2.b])
        for half, ph in enumerate((p0, p1)):
            V(lambda h: h.scalar_tensor_tensor(out=YT.t[0:Pt, half * 512:(half + 1) * 512], in0=ph.t[0:Pt, :], scalar=RS2.t[0:Pt, 0:1],
                                               in1=nw.t[0:Pt, half * 512:(half + 1) * 512], op0=ALU.mult, op1=ALU.mult),
              [ph.b, RS2.b, nw.b], [YT.b])
        V(lambda h: h.scalar_tensor_tensor(out=X.t[0:Pt, s, :], in0=YT.t[0:Pt, :], scalar=scale, in1=X.t[0:Pt, s, :],
                                           op0=ALU.mult, op1=ALU.add), [YT.b, X.b], [X.b])

    def ffn(fi, T_, Pt, NS, npre, npost):
        AR.reset()
        HT = AR.alloc("ht", [128, NFC, T_], BF16)
        WDp = [AR.alloc("wd%d" % j, [128, 2, D], BF16) for j in range(11)]
        SG = [AR.alloc("sg%d" % i, [128, 512], F32) for i in range(2)]
        YT = AR.alloc("yt", [128, D], F32)
        rmsnorm_T(Pt, NS, npre)
        nwp = load_nw(npost)
        FB = 1 if T_ >= 512 else 512 // T_
        nb = 0
        for j in range(11):
            slot = wnext(1)[0]
            wd_load(fi, j, WDp[j])
            for f2 in range(2):
                fc = 2 * j + f2
                if nb == 0:
                    pg = K.bank()
                    pu = K.bank()
                    fc0 = fc
                col = nb * T_
                for k in range(8):
                    MM(pg.t[:, col:col + T_], slot.t[:, k * 256 + f2 * 128:k * 256 + f2 * 128 + 128], XNT.t[:, k, 0:T_], k == 0, k == 7, [slot.b, XNT.b], pg)
                for k in range(8):
                    MM(pu.t[:, col:col + T_], slot.t[:, 2048 + k * 256 + f2 * 128:2048 + k * 256 + f2 * 128 + 128], XNT.t[:, k, 0:T_], k == 0, k == 7,
                       [slot.b, XNT.b], pu)
                nb += 1
                if nb == FB or fc == NFC - 1:
                    W_ = nb * T_
                    sg = SG[(fc // FB) % 2]
                    A(lambda h: h.activation(out=sg.t[:, 0:W_], in_=pg.t[:, 0:W_], func=AF.Silu), [pg.b], [sg.b])
                    V(lambda h: h.tensor_tensor(out=HT.t[:, fc0:fc0 + nb, :].rearrange("p a b -> p (a b)"), in0=sg.t[:, 0:W_], in1=pu.t[:, 0:W_], op=ALU.mult),
                      [sg.b, pu.b], [HT.b])
                    nb = 0
        for s in range(NS):
            p0 = K.bank()
            p1 = K.bank()
            for half, ph in enumerate((p0, p1)):
                for fc in range(NFC):
                    MM(ph.t[0:Pt, :], HT.t[:, fc, s * 128:s * 128 + Pt], WDp[fc // 2].t[:, fc % 2, half * 512:(half + 1) * 512],
                       fc == 0, fc == NFC - 1, [HT.b, WDp[fc // 2].b], ph)
            post_norm_res(Pt, s, p0, p1, nwp, YT, 0.5)

    def mixer(T_, Pt, NS, nseq, first, cst):
        sh = nseq
        TRI, SAm, NEG4, SEL, MASKB = cst
        AR.reset()
        YRG = AR.alloc("yrg", [128, 8, T_], BF16)
        SZ = AR.alloc("sz", [128, NS, 2048], BF16)
        XS = AR.alloc("xs", [128, NS, 2048], BF16)
        BT = AR.alloc("bt", [128, 8, T_], BF16)
        CT = AR.alloc("ct", [128, 8, T_], BF16)
        YST = AR.alloc("yst", [128, 16, T_], BF16)
        DT = AR.alloc("dt", [128, NS, 32], F32)
        DTA = AR.alloc("dta", [128, NS, 32], F32)
        CDa = AR.alloc("cda", [128, 16, NS * nseq], F32)
        TOT = AR.alloc("tot", [128, NS * nseq], F32)
        mark = AR.off
        WK = [AR.alloc("wk%d" % i, [128, 48 + T_], F32) for i in range(3)]
        XCs = [AR.alloc("xc%d" % i, [128, T_], F32) for i in range(2)]
        XCBs = [AR.alloc("xcb%d" % i, [128, T_], BF16) for i in range(2)]
        RT = [{n: AR.alloc("t_%s%d" % (n, i), [128, T_], F32) for n in ("sa", "sx", "mu", "h", "gg")} for i in range(2)]

        rmsnorm_T(Pt, NS, 2)
        nwp = load_nw(3)
        wi = [0]

        def conv(wk, base_w, cidx, bias_col, XC):
            if bias_col is None:
                V(lambda h: h.tensor_scalar(out=XC.t[:, 0:T_], in0=wk.t[:, 0:T_], scalar1=PF.t[:, base_w + cidx * 4:base_w + cidx * 4 + 1],
                                            scalar2=None, op0=ALU.mult), [wk.b, PF.b], [XC.b])
            else:
                V(lambda h: h.tensor_scalar(out=XC.t[:, 0:T_], in0=wk.t[:, 0:T_], scalar1=PF.t[:, base_w + cidx * 4:base_w + cidx * 4 + 1],
                                            scalar2=PF.t[:, bias_col:bias_col + 1], op0=ALU.mult, op1=ALU.add), [wk.b, PF.b], [XC.b])
            for k in range(1, 4):
                V(lambda h: h.scalar_tensor_tensor(out=XC.t[:, 0:T_], in0=wk.t[:, k * sh:k * sh + T_],
                                                   scalar=PF.t[:, base_w + cidx * 4 + k:base_w + cidx * 4 + k + 1], in1=XC.t[:, 0:T_],
                                                   op0=ALU.mult, op1=ALU.add), [wk.b, PF.b, XC.b], [XC.b])

        def proj_to_wk(slot, coloff, ncols_blk, ci, carry, cidx, on_dve=False):
            wk = WK[wi[0] % 3]
            wi[0] += 1
            A(lambda h: h.activation(out=wk.t[:, 0:3 * sh], in_=carry.t[:, cidx, 0:3 * sh], func=AF.Copy), [carry.b], [wk.b])
            bk = K.bank()
            for k in range(8):
                o = coloff + k * ncols_blk + ci * 128
                MM(bk.t[:, 0:T_], slot.t[:, o:o + 128], XNT.t[:, k, 0:T_], k == 0, k == 7, [slot.b, XNT.b], bk)
            if on_dve:
                V(lambda h: h.tensor_copy(wk.t[:, 3 * sh:3 * sh + T_], bk.t[:, 0:T_]), [bk.b], [wk.b])
            else:
                A(lambda h: h.activation(out=wk.t[:, 3 * sh:3 * sh + T_], in_=bk.t[:, 0:T_], func=AF.Copy), [bk.b], [wk.b])
            A(lambda h: h.activation(out=carry.t[:, cidx, 0:3 * sh], in_=wk.t[:, T_:T_ + 3 * sh], func=AF.Copy), [wk.b], [carry.b])
            return wk

        rgw = RGW
        rg_slots = {}
        rg_wk = {}

        def rgA(c):
            ci = c % 2
            if ci == 0:
                rg_slots[c // 2] = wnext(1, hold=1)[0]
            slot = rg_slots[c // 2]
            rg_wk[c] = proj_to_wk(slot, 0, 256, ci, RGC, c, on_dve=False)

        def rgA1b(c):
            conv(rg_wk.pop(c), PF_RGCW, c, PF_RGCB + c, XCs[c % 2])

        def rgCast(c):
            XC = XCs[c % 2]
            XCB = XCBs[c % 2]
            A(lambda h: h.activation(out=XCB.t[:, 0:T_], in_=XC.t[:, 0:T_], func=AF.Copy), [XC.b], [XCB.b])

        def rgA2(c):
            ci = c % 2
            slot = rg_slots[c // 2]
            XCB = XCBs[c % 2]
            pa = K.bank()
            MM(pa.t[:, 0:T_], rgw.t[:, c * 128:(c + 1) * 128], XCB.t[:, 0:T_], True, True, [rgw.b, XCB.b], pa)
            px = K.bank()
            MM(px.t[:, 0:T_], rgw.t[:, 1024 + c * 128:1024 + (c + 1) * 128], XCB.t[:, 0:T_], True, True, [rgw.b, XCB.b], px)
            pg = K.bank()
            for k in range(8):
                o = 2048 + k * 256 + ci * 128
                MM(pg.t[:, 0:T_], slot.t[:, o:o + 128], XNT.t[:, k, 0:T_], k == 0, k == 7, [slot.b, XNT.b], pg)
            return (pa, px, pg)

        def rgB(c, pa, px, pg):
            XC = XCs[c % 2]
            R_ = RT[c % 2]
            t_sa, t_sx, t_mu, t_h, t_gg = R_["sa"], R_["sx"], R_["mu"], R_["h"], R_["gg"]
            t_a = t_sa
            A(lambda h: h.activation(out=t_sa.t[:, 0:T_], in_=pa.t[:, 0:T_], func=AF.Tanh, scale=0.5, bias=HPF.t[:, c:c + 1]),
              [pa.b, HPF.b], [t_sa.b])
            A(lambda h: h.activation(out=t_sx.t[:, 0:T_], in_=px.t[:, 0:T_], func=AF.Tanh, scale=0.5, bias=HPF.t[:, 8 + c:9 + c]),
              [px.b, HPF.b], [t_sx.b])
            A(lambda h: h.activation(out=t_a.t[:, 0:T_], in_=t_sa.t[:, 0:T_], func=AF.Exp, scale=HPF.t[:, 16 + c:17 + c], bias=HPF.t[:, 16 + c:17 + c]),
              [t_sa.b, HPF.b], [t_a.b])
            A(lambda h: h.activation(out=t_gg.t[:, 0:T_], in_=pg.t[:, 0:T_], func=AF.Square), [pg.b], [t_gg.b])
            V(lambda h: h.tensor_scalar(out=t_gg.t[:, 0:T_], in0=t_gg.t[:, 0:T_], scalar1=0.044715, scalar2=1.0, op0=ALU.mult, op1=ALU.add),
              [t_gg.b], [t_gg.b])
            V(lambda h: h.tensor_tensor(out=t_gg.t[:, 0:T_], in0=t_gg.t[:, 0:T_], in1=pg.t[:, 0:T_], op=ALU.mult), [t_gg.b, pg.b], [t_gg.b])
            V(lambda h: h.tensor_tensor(out=t_mu.t[:, 0:T_], in0=t_a.t[:, 0:T_], in1=t_a.t[:, 0:T_], op=ALU.mult), [t_a.b], [t_mu.b])
            V(lambda h: h.tensor_scalar(out=t_mu.t[:, 0:T_], in0=t_mu.t[:, 0:T_], scalar1=1.0, scalar2=None, op0=ALU.min), [t_mu.b], [t_mu.b])
            A(lambda h: h.activation(out=t_gg.t[:, 0:T_], in_=t_gg.t[:, 0:T_], func=AF.Tanh, scale=0.7978845608028654), [t_gg.b], [t_gg.b])
            A(lambda h: h.activation(out=t_mu.t[:, 0:T_], in_=t_mu.t[:, 0:T_], func=AF.Sqrt, scale=-0.25, bias=0.25), [t_mu.b], [t_mu.b])
            V(lambda h: h.scalar_tensor_tensor(out=t_gg.t[:, 0:T_], in0=t_gg.t[:, 0:T_], scalar=1.0, in1=pg.t[:, 0:T_], op0=ALU.add, op1=ALU.mult),
              [t_gg.b, pg.b], [t_gg.b])
            if first:
                V(lambda h: h.memset(t_mu.t[:, 0:1], 0.5), [], [t_mu.b])
            V(lambda h: h.scalar_tensor_tensor(out=XC.t[:, 0:T_], in0=t_sx.t[:, 0:T_], scalar=1.0, in1=XC.t[:, 0:T_], op0=ALU.add, op1=ALU.mult),
              [XC.b, t_sx.b], [XC.b])
            V(lambda h: h.tensor_tensor(out=XC.t[:, 0:T_], in0=XC.t[:, 0:T_], in1=t_mu.t[:, 0:T_], op=ALU.mult), [XC.b, t_mu.b], [XC.b])
            if nseq == 1:
                V(lambda h: h.tensor_tensor_scan(out=t_h.t[:, 0:T_], data0=t_a.t[:, 0:T_], data1=XC.t[:, 0:T_], initial=HC.t[:, c, 0:1],
                                                 op0=ALU.mult, op1=ALU.add), [t_a.b, XC.b, HC.b], [t_h.b])
            else:
                for t in range(T_ // nseq):
                    prev = HC.t[:, c, 0:nseq] if t == 0 else t_h.t[:, (t - 1) * nseq:t * nseq]
                    V(lambda h: h.tensor_tensor(out=t_h.t[:, t * nseq:(t + 1) * nseq], in0=t_a.t[:, t * nseq:(t + 1) * nseq], in1=prev, op=ALU.mult),
                      [t_a.b, HC.b, t_h.b], [t_h.b])
                    V(lambda h: h.tensor_tensor(out=t_h.t[:, t * nseq:(t + 1) * nseq], in0=t_h.t[:, t * nseq:(t + 1) * nseq],
                                                in1=XC.t[:, t * nseq:(t + 1) * nseq], op=ALU.add), [t_h.b, XC.b], [t_h.b])
            A(lambda h: h.activation(out=HC.t[:, c, 0:nseq], in_=t_h.t[:, T_ - nseq:T_], func=AF.Copy), [t_h.b], [HC.b])
            V(lambda h: h.scalar_tensor_tensor(out=YRG.t[:, c, 0:T_], in0=t_gg.t[:, 0:T_], scalar=0.5, in1=t_h.t[:, 0:T_], op0=ALU.mult, op1=ALU.mult),
              [t_h.b, t_gg.b], [YRG.b])

        rgA(0)
        rgA1b(0)
        rgCast(0)
        rgA(1)
        pend = rgA2(0)
        for c in range(8):
            if c + 1 < 8:
                rgA1b(c + 1)
            rgB(c, *pend)
            if c + 1 < 8:
                rgCast(c + 1)
            if c + 2 < 8:
                rgA(c + 2)
            pend = rgA2(c + 1) if c + 1 < 8 else None

        for zc in range(4):
            slot = wnext(1)[0]
            for s in range(NS):
                bk = K.bank()
                for k in range(8):
                    MM(bk.t[0:Pt, :], XNT.t[:, k, s * 128:s * 128 + Pt], slot.t[:, k * 512:(k + 1) * 512], k == 0, k == 7, [slot.b, XNT.b], bk)
                A(lambda h: h.activation(out=SZ.t[0:Pt, s, zc * 512:(zc + 1) * 512], in_=bk.t[0:Pt, :], func=AF.Silu), [bk.b], [SZ.b])
        slot = wnext(1)[0]
        VV = AR.alloc("vv", [128, NS, 32], F32)
        AVt = AR.alloc("av", [128, NS, 32], F32)
        for s in range(NS):
            bk = K.bank()
            for k in range(8):
                MM(bk.t[0:Pt, 0:32], XNT.t[:, k, s * 128:s * 128 + Pt], slot.t[:, k * 32:(k + 1) * 32], k == 0, k == 7, [slot.b, XNT.b], bk)
            V(lambda h: h.tensor_tensor(out=VV.t[0:Pt, s, :], in0=bk.t[0:Pt, 0:32], in1=DTB.t[0:Pt, :], op=ALU.add), [bk.b, DTB.b], [VV.b])
        V(lambda h: h.scalar_tensor_tensor(out=AVt.t[0:Pt], in0=VV.t[0:Pt], scalar=-1.0, in1=VV.t[0:Pt], op0=ALU.mult, op1=ALU.max), [VV.b], [AVt.b])
        A(lambda h: h.activation(out=AVt.t[0:Pt], in_=AVt.t[0:Pt], func=AF.Exp, scale=-1.0), [AVt.b], [AVt.b])
        A(lambda h: h.activation(out=AVt.t[0:Pt], in_=AVt.t[0:Pt], func=AF.Ln, bias=1.0), [AVt.b], [AVt.b])
        V(lambda h: h.scalar_tensor_tensor(out=DT.t[0:Pt], in0=VV.t[0:Pt], scalar=0.0, in1=AVt.t[0:Pt], op0=ALU.max, op1=ALU.add),
          [VV.b, AVt.b], [DT.b])
        V(lambda h: h.tensor_tensor(out=DTA.t[0:Pt], in0=DT.t[0:Pt], in1=bc(ABC.t[0:Pt, :].unsqueeze(1), [Pt, NS, 32]), op=ALU.mult),
          [DT.b, ABC.b], [DTA.b])

        J = NS * nseq
        bk = K.bank()
        for s in range(NS):
            MM(bk.t[0:32, s * nseq:(s + 1) * nseq], DTA.t[0:Pt, s, :], SEL[0:Pt, 0:nseq], True, True, [DTA.b, CONST.b], bk)
        A(lambda h: h.activation(out=TOT.t[0:32, 0:J], in_=bk.t[0:32, 0:J], func=AF.Exp), [bk.b], [TOT.b])
        io.dma(CDSCR[:, 0:J], TOT.t[0:32, 0:J], reads=[TOT.b], writes=[CDSB])
        for qh in range(2):
            src_ap = CDSCR.rearrange("(hc two) j -> two hc j", two=2)[qh][:, 0:J].unsqueeze(0).broadcast_to([64, 16, J])
            io.dma(CDa.t[qh * 64:(qh + 1) * 64, :, :], src_ap, reads=[CDSB], writes=[CDa.b], add=(qh > 0))

        XSBs = [AR.alloc("xsb%d" % i, [128, T_], BF16) for i in range(4)]
        x_slot = [None]
        x_ev = {}

        def xA(cc):
            ci = cc % 4
            if ci == 0:
                x_slot[0] = wnext(1)[0]
            return proj_to_wk(x_slot[0], 0, 512, ci, XBCC, cc)

        def xB(cc, wk):
            XC = XCs[cc % 2]
            XSB = XSBs[cc % 4]
            conv(wk, PF_SCW, cc, None, XC)
            bcol = PF.t[:, PF_SCB + cc:PF_SCB + cc + 1]
            if cc < 16:
                A(lambda h: h.activation(out=XSB.t[:, 0:T_], in_=XC.t[:, 0:T_], func=AF.Silu, bias=bcol), [XC.b, PF.b], [XSB.b])
            elif cc < 24:
                A(lambda h: h.activation(out=BT.t[:, cc - 16, 0:T_], in_=XC.t[:, 0:T_], func=AF.Silu, bias=bcol), [XC.b, PF.b], [BT.b])
            else:
                A(lambda h: h.activation(out=CT.t[:, cc - 24, 0:T_], in_=XC.t[:, 0:T_], func=AF.Silu, bias=bcol), [XC.b, PF.b], [CT.b])

        def xTR(cc):
            if cc < 0 or cc >= 16:
                return
            XSB = XSBs[cc % 4]
            bk = K.bank()
            psb = bk.t[:].bitcast(BF16)
            for s in range(NS):
                TR(psb[0:Pt, s * 128:(s + 1) * 128], XSB.t[:, s * 128:s * 128 + Pt], IDB.t[:, :], [XSB.b, IDB.b], bk, inc=(s == NS - 1))
            K.pinned.add(K.banks.index(bk))
            x_ev[cc] = bk

        def xB2(cc):
            bk = x_ev.pop(cc, None)
            if bk is None:
                return
            psb = bk.t[:].bitcast(BF16)
            A(lambda h: h.activation(out=XS.t[0:Pt, :, cc * 128:(cc + 1) * 128],
                                     in_=psb[0:Pt, 0:NS * 128].rearrange("p (s c) -> p s c", s=NS), func=AF.Copy), [bk.b], [XS.b])
            K.unpin(bk)

        xpend = {0: xA(0), 1: xA(1)}
        for cc in range(32):
            if cc + 2 < 32:
                xpend[cc + 2] = xA(cc + 2)
            xB(cc, xpend.pop(cc))
            xTR(cc - 2)
            xB2(cc - 3)
        xTR(30); xTR(31)
        for cc in range(29, 32):
            xB2(cc)

        AR.off = mark
        K.snapshot()
        Pq = Pt
        XDT = AR.alloc("xdt", [128, 2048], BF16)
        XDD = AR.alloc("xdd", [128, 2048], BF16)
        BTM = AR.alloc("btm", [128, 1024], BF16)
        STTs = [AR.alloc("stt%d" % i, [128, 2048], BF16) for i in range(2 if nseq > 1 else 1)]
        if nseq == 1:
            STTs = STTs * 2
        CBSs = [AR.alloc("cbs%d" % i, [128, 128], F32) for i in range(2)]
        RHSPs = [AR.alloc("rhsp%d" % i, [128, 4, Pq], F32) for i in range(2)]
        LTs = [AR.alloc("lt%d" % i, [128, 4, Pq], F32) for i in range(2)]
        WTs = [AR.alloc("wt%d" % i, [128, 4, Pq], BF16) for i in range(2)]
        T1 = AR.alloc("t1", [128, 256], F32)
        T2 = AR.alloc("t2", [128, 256], F32)
        Y = AR.alloc("y", [128, 2048], F32)
        YN = XDD
        GS = AR.alloc("gs", [128, 8], F32)
        if nseq > 1:
            MKB = AR.alloc("maskb", [128, 1024], F32)
            io.dma(MKB.t[:], maskb_d, writes=[MKB.b])
            MASKB = MKB.t
            S0 = [AR.alloc("s0_%d" % i, [128, 16, 128], F32) for i in range(3)]
            SO = [AR.alloc("so_%d" % i, [128, 16, 128], F32) for i in range(2)]
            CTMs = [AR.alloc("ctm%d" % i, [128, 8, Pq], BF16) for i in range(2)]
            BMs = [AR.alloc("bm%d" % i, [128, 1024], BF16) for i in range(2)]

        EEa = AR.alloc("eea", [128, NS, 64], F32)
        DDa = AR.alloc("dda", [128, NS, 32], F32)
        for s in range(NS):
            dta_q = DTA.t[0:Pq, s, :]
            bk = K.bank()
            MM(bk.t[0:Pq, 0:32], TRI[0:Pq, 0:Pq], dta_q, True, True, [CONST.b, DTA.b], bk)
            MM(bk.t[0:Pq, 32:64], SAm[0:Pq, 0:Pq], dta_q, True, True, [CONST.b, DTA.b], bk)
            A(lambda h: h.activation(out=EEa.t[0:Pq, s, :], in_=bk.t[0:Pq, 0:64], func=AF.Exp), [bk.b], [EEa.b])
            V(lambda h: h.tensor_tensor(out=DDa.t[0:Pq, s, :], in0=DT.t[0:Pq, s, :], in1=EEa.t[0:Pq, s, 32:64], op=ALU.mult), [DT.b, EEa.b], [DDa.b])

        for s in range(NS):
            c0 = s * 128
            EE = T(EEa.t[:, s, :], EEa.b)
            DD = T(DDa.t[:, s, :], DDa.b)
            xs3 = XS.t[0:Pq, s, :].rearrange("p (h d) -> p h d", h=32)
            V(lambda h: h.tensor_tensor(out=XDT.t[0:Pq, :].rearrange("p (h d) -> p h d", h=32), in0=xs3,
                                        in1=bc(DT.t[0:Pq, s, :].unsqueeze(2), [Pq, 32, 64]), op=ALU.mult), [XS.b, DT.b], [XDT.b])
            V(lambda h: h.tensor_tensor(out=XDD.t[0:Pq, :].rearrange("p (h d) -> p h d", h=32), in0=xs3,
                                        in1=bc(DD.t[0:Pq, :].unsqueeze(2), [Pq, 32, 64]), op=ALU.mult), [XS.b, DD.b], [XDD.b])
            bk = K.bank()
            psb = bk.t[:].bitcast(BF16)
            for g in range(8):
                TR(psb[0:Pq, g * 128:(g + 1) * 128], BT.t[:, g, c0:c0 + Pq], IDB.t[:, :], [BT.b, IDB.b], bk, inc=(g == 7))
            A(lambda h: h.activation(out=BTM.t[0:Pq, :], in_=psb[0:Pq, :], func=AF.Copy), [bk.b], [BTM.b])

            YO = [K.bank(pin=True) for _ in range(4)]
            def sL(b):
                if nseq > 1:
                    sp.dma(S0[b % 3].t[:], sss[b].rearrange("(hc q) n -> q hc n", q=128), writes=[S0[b % 3].b])

            def sA(b):
                Sb = S0[b % 3] if nseq > 1 else S
                stt = STTs[b % 2]
                for hq in range(4):
                    bk = K.bank()
                    for hi in range(4):
                        hc = 4 * hq + hi
                        TR(bk.t[:, hi * 128:(hi + 1) * 128], Sb.t[:, hc, :], ID, [Sb.b, CONST.b], bk, inc=(hi == 3))
                    A(lambda h: h.activation(out=stt.t[:, hq * 512:(hq + 1) * 512], in_=bk.t[:, :], func=AF.Copy), [bk.b], [stt.b])
                if nseq > 1:
                    ctm = CTMs[b % 2]
                    bm = BMs[b % 2]
                    V(lambda h: h.tensor_tensor(out=ctm.t[:, :, :], in0=CT.t[:, :, c0:c0 + Pq],
                                                in1=bc(MASKB[:, b * Pq:(b + 1) * Pq].unsqueeze(1), [128, 8, Pq]), op=ALU.mult),
                      [CT.b, MKB.b], [ctm.b])
                    V(lambda h: h.tensor_scalar(out=bm.t[0:Pq, :], in0=BTM.t[0:Pq, :], scalar1=SEL[0:Pq, b:b + 1], scalar2=None, op0=ALU.mult),
                      [BTM.b, CONST.b], [bm.b])

            def sB(b):
                Sb = S0[b % 3] if nseq > 1 else S
                Sn = SO[b % 2] if nseq > 1 else S
                stt = STTs[b % 2]
                for g in range(8):
                    lhs = CTMs[b % 2].t[:, g, 0:Pq] if nseq > 1 else CT.t[:, g, c0:c0 + Pq]
                    rb = [CTMs[b % 2].b] if nseq > 1 else [CT.b]
                    yb = YO[g // 2]
                    pe.op(lambda h: h.matmul(yb.t[0:Pq, (g % 2) * 256:(g % 2) * 256 + 256], lhs, stt.t[:, g * 256:(g + 1) * 256],
                                             start=(b == 0 and g % 2 == 0), stop=(b == nseq - 1), skip_group_check=True), rb + [stt.b], [yb.b], inc=True)
                bmt = BMs[b % 2] if nseq > 1 else BTM
                for hq in range(4):
                    bk = K.bank()
                    for hi in range(4):
                        hc = 4 * hq + hi
                        MM(bk.t[:, hi * 128:(hi + 1) * 128], XDD.t[0:Pq, hc * 128:(hc + 1) * 128], bmt.t[0:Pq, (hc // 2) * 128:(hc // 2 + 1) * 128],
                           True, True, [XDD.b, bmt.b], bk)
                    for hi in range(4):
                        hc = 4 * hq + hi
                        V(lambda h: h.scalar_tensor_tensor(out=Sn.t[:, hc, :], in0=Sb.t[:, hc, :], scalar=CDa.t[:, hc, s * nseq + b:s * nseq + b + 1],
                                                           in1=bk.t[:, hi * 128:(hi + 1) * 128], op0=ALU.mult, op1=ALU.add),
                          [Sb.b, CDa.b, bk.b], [Sn.b])
                if nseq > 1:
                    act.dma(o_sss[b].rearrange("(hc q) n -> q hc n", q=128), Sn.t[:], reads=[Sn.b])

            sL(0)
            if nseq > 1:
                sL(1)
            sA(0)
            for b in range(nseq):
                if b + 2 < nseq:
                    sL(b + 2)
                if b + 1 < nseq:
                    sA(b + 1)
                sB(b)

            def gA(g):
                CBS, RHSP, LT = CBSs[g % 2], RHSPs[g % 2], LTs[g % 2]
                bkc = K.bank()
                MM(bkc.t[0:Pq, 0:Pq], BT.t[:, g, c0:c0 + Pq], CT.t[:, g, c0:c0 + Pq], True, True, [BT.b, CT.b], bkc)
                V(lambda h: h.tensor_tensor(out=RHSP.t[0:Pq, :, :], in0=bc(TRI[0:Pq, 0:Pq].unsqueeze(1), [Pq, 4, Pq]),
                                            in1=bc(DTA.t[0:Pq, s, 4 * g:4 * g + 4].unsqueeze(2), [Pq, 4, Pq]), op=ALU.mult),
                  [CONST.b, DTA.b], [RHSP.b])
                bks = K.bank()
                MM(bks.t[0:Pq, 0:4 * Pq], SAm[0:Pq, 0:Pq], RHSP.t[0:Pq, :, :].rearrange("p a b -> p (a b)"), True, False, [CONST.b, RHSP.b], bks)
                MM(bks.t[0:Pq, 0:4 * Pq], ID[0:Pq, 0:Pq], NEG4[0:Pq, 0:4 * Pq], False, True, [CONST.b], bks)
                A(lambda h: h.activation(out=CBS.t[0:Pq, 0:Pq], in_=bkc.t[0:Pq, 0:Pq], func=AF.Copy), [bkc.b], [CBS.b])
                A(lambda h: h.activation(out=LT.t[0:Pq, :, :].rearrange("p a b -> p (a b)"), in_=bks.t[0:Pq, 0:4 * Pq], func=AF.Exp), [bks.b], [LT.b])

            def gB(g):
                CBS, LT, WT = CBSs[g % 2], LTs[g % 2], WTs[g % 2]
                V(lambda h: h.tensor_tensor(out=WT.t[0:Pq, :, :], in0=LT.t[0:Pq, :, :], in1=bc(CBS.t[0:Pq, 0:Pq].unsqueeze(1), [Pq, 4, Pq]),
                                            op=ALU.mult), [LT.b, CBS.b], [WT.b])
                bky = K.bank()
                for hh in range(4):
                    MM(bky.t[0:Pq, hh * 64:(hh + 1) * 64], WT.t[0:Pq, hh, :], XDT.t[0:Pq, (4 * g + hh) * 64:(4 * g + hh + 1) * 64], True, True,
                       [WT.b, XDT.b], bky)
                return bky

            def gC(g, bky):
                yb = YO[g // 2]
                V(lambda h: h.tensor_tensor(out=T1.t[0:Pq, :].rearrange("p (h d) -> p h d", h=4),
                                            in0=yb.t[0:Pq, (g % 2) * 256:(g % 2) * 256 + 256].rearrange("p (h d) -> p h d", h=4),
                                            in1=bc(EE.t[0:Pq, 4 * g:4 * g + 4].unsqueeze(2), [Pq, 4, 64]), op=ALU.mult), [yb.b, EE.b], [T1.b])
                V(lambda h: h.tensor_tensor(out=T2.t[0:Pq, :].rearrange("p (h d) -> p h d", h=4),
                                            in0=XS.t[0:Pq, s, g * 256:(g + 1) * 256].rearrange("p (h d) -> p h d", h=4),
                                            in1=bc(DBC.t[0:Pq, 4 * g:4 * g + 4].unsqueeze(2), [Pq, 4, 64]), op=ALU.mult), [XS.b, DBC.b], [T2.b])
                V(lambda h: h.tensor_tensor(out=T1.t[0:Pq, :], in0=T1.t[0:Pq, :], in1=T2.t[0:Pq, :], op=ALU.add), [T1.b, T2.b], [T1.b])
                V(lambda h: h.tensor_tensor(out=T1.t[0:Pq, :], in0=bky.t[0:Pq, 0:256], in1=T1.t[0:Pq, :], op=ALU.add),
                  [bky.b, T1.b], [T1.b])
                V(lambda h: h.tensor_tensor(out=Y.t[0:Pq, g * 256:(g + 1) * 256], in0=T1.t[0:Pq, :], in1=SZ.t[0:Pq, s, g * 256:(g + 1) * 256], op=ALU.mult),
                  [T1.b, SZ.b], [Y.b])
                A(lambda h: h.activation(out=JUNK.t[0:Pq, 0:256], in_=Y.t[0:Pq, g * 256:(g + 1) * 256], func=AF.Square, accum_out=GS.t[0:Pq, g:g + 1]),
                  [Y.b], [JUNK.b, GS.b])

            bkys = {}
            for i in range(10):
                if i < 8:
                    gA(i)
                if 0 <= i - 1 < 8:
                    bkys[i - 1] = gB(i - 1)
                if 0 <= i - 2 < 8:
                    gC(i - 2, bkys.pop(i - 2))
            for yb in YO:
                K.unpin(yb)
            A(lambda h: h.activation(out=GS.t[0:Pq, :], in_=GS.t[0:Pq, :], func=AF.Ln, scale=1.0 / 256, bias=EPS), [GS.b], [GS.b])
            A(lambda h: h.activation(out=GS.t[0:Pq, :], in_=GS.t[0:Pq, :], func=AF.Exp, scale=-0.5), [GS.b], [GS.b])
            V(lambda h: h.tensor_tensor(out=YN.t[0:Pq, :].rearrange("p (g d) -> p g d", g=8), in0=Y.t[0:Pq, :].rearrange("p (g d) -> p g d", g=8),
                                        in1=bc(GS.t[0:Pq, :].unsqueeze(2), [Pq, 8, 256]), op=ALU.mult), [Y.b, GS.b], [YN.b])
            for half in range(2):
                bk = K.bank()
                psb = bk.t[:].bitcast(BF16)
                for ci in range(8):
                    cc = half * 8 + ci
                    TR(psb[:, ci * 128:ci * 128 + Pq], YN.t[0:Pq, cc * 128:(cc + 1) * 128], IDB.t[0:Pq, 0:Pq], [YN.b, IDB.b], bk, inc=(ci == 7))
                V(lambda h: h.tensor_tensor(out=YST.t[:, half * 8:half * 8 + 8, c0:c0 + Pq],
                                            in0=psb.rearrange("p (c t) -> p c t", c=8)[:, :, 0:Pq],
                                            in1=bc(PF.t[:, PF_SNW + half * 8:PF_SNW + half * 8 + 8].unsqueeze(2), [128, 8, Pq]), op=ALU.mult),
                  [bk.b, PF.b], [YST.b])

        AR.off = mark
        K.snapshot()
        MT = AR.alloc("mt", [128, 8, T_], BF16)
        M1 = AR.alloc("m1", [128, T_], F32)
        M2 = AR.alloc("m2", [128, T_], F32)
        S1 = AR.alloc("s1", [128, T_], F32)
        S2 = AR.alloc("s2", [128, T_], F32)
        YT = AR.alloc("ytm", [128, D], F32)
        for dc2 in range(4):
            sA, sB, sC = wnext(3)
            for di in range(2):
                dc = 2 * dc2 + di
                p1 = K.bank(); g1 = K.bank(); p2 = K.bank(); g2 = K.bank()
                for k in range(8):
                    o = k * 256 + di * 128
                    MM(p1.t[:, 0:T_], sA.t[:, o:o + 128], YRG.t[:, k, 0:T_], k == 0, k == 7, [sA.b, YRG.b], p1)
                for k in range(8):
                    o = 2048 + k * 256 + di * 128
                    MM(g1.t[:, 0:T_], sA.t[:, o:o + 128], XNT.t[:, k, 0:T_], k == 0, k == 7, [sA.b, XNT.b], g1)
                for k in range(16):
                    o = k * 256 + di * 128
                    MM(p2.t[:, 0:T_], sC.t[:, o:o + 128], YST.t[:, k, 0:T_], k == 0, k == 15, [sC.b, YST.b], p2)
                for k in range(8):
                    o = k * 256 + di * 128
                    MM(g2.t[:, 0:T_], sB.t[:, o:o + 128], XNT.t[:, k, 0:T_], k == 0, k == 7, [sB.b, XNT.b], g2)
                A(lambda h: h.activation(out=S1.t[:, 0:T_], in_=g1.t[:, 0:T_], func=AF.Sigmoid), [g1.b], [S1.b])
                A(lambda h: h.activation(out=S2.t[:, 0:T_], in_=g2.t[:, 0:T_], func=AF.Sigmoid), [g2.b], [S2.b])
                V(lambda h: h.tensor_tensor(out=M1.t[:, 0:T_], in0=S1.t[:, 0:T_], in1=p1.t[:, 0:T_], op=ALU.mult), [S1.b, p1.b], [M1.b])
                V(lambda h: h.tensor_tensor(out=M2.t[:, 0:T_], in0=S2.t[:, 0:T_], in1=p2.t[:, 0:T_], op=ALU.mult), [S2.b, p2.b], [M2.b])
                V(lambda h: h.tensor_tensor(out=MT.t[:, dc, 0:T_], in0=M1.t[:, 0:T_], in1=M2.t[:, 0:T_], op=ALU.add), [M1.b, M2.b], [MT.b])
        w0, w1 = wnext(2)
        for s in range(NS):
            p0 = K.bank(); p1 = K.bank()
            for ph, ws in ((p0, w0), (p1, w1)):
                for k in range(8):
                    MM(ph.t[0:Pt, :], MT.t[:, k, s * 128:s * 128 + Pt], ws.t[:, k * 512:(k + 1) * 512], k == 0, k == 7, [MT.b, ws.b], ph)
            post_norm_res(Pt, s, p0, p1, nwp, YT, 1.0)

    def fm_to_rows(src_fn, nchunks, ncols, STw, store_fn):
        for g in range(nchunks // 4):
            bk = K.bank()
            for ci in range(4):
                ap, b = src_fn(4 * g + ci)
                TR(bk.t[0:ncols, ci * 128:(ci + 1) * 128], ap, ID, [b, CONST.b], bk, inc=(ci == 3))
            A(lambda h: h.activation(out=STw.t[0:ncols, g * 512:(g + 1) * 512], in_=bk.t[0:ncols, :], func=AF.Copy), [bk.b], [STw.b])
        store_fn(STw)

    STG = [None, None]
    stg_i = [0]
    TMh = [None]

    def alloc_stg():
        AR.reset()
        STG[0] = AR.alloc("stg0", [128, 512], F32)
        STG[1] = AR.alloc("stg1", [128, 512], F32)
        return [AR.alloc("wide0", [128, 4096], F32), AR.alloc("wide1", [128, 1024], F32), AR.alloc("wide2", [128, 1024], F32)]

    for v in (RGC, HC, XBCC, S):
        dve.op(lambda h: h.memset(v.t[:], 0.0), [], [v.b])

    PC = (CONST.t[:, C_PTRI:C_PTRI + 128], CONST.t[:, C_PSA:C_PSA + 128], CONST.t[:, C_PNEG:C_PNEG + 512], CONST.t[:, C_SSEL + 15:C_SSEL + 16], None)
    ONES = K.sb("ones", [128, 1], F32, const=True)
    dve.op(lambda h: h.memset(ONES.t[:], 1.0), [], [ONES.b])
    PC = (PC[0], PC[1], PC[2], ONES.t[:, 0:1], None)
    SC = (CONST.t[:, C_STRI:C_STRI + 64], CONST.t[:, C_SSA:C_SSA + 64], CONST.t[:, C_SNEG:C_SNEG + 256], CONST.t[:, C_SSEL:C_SSEL + 16],
          None)

    for ti in range(NPT):
        t0 = ti * 512
        io.dma(X.t[:, :, :], xp[t0:t0 + 512, :].rearrange("(s p) d -> p s d", p=128), writes=[X.b])
        ffn(0, 512, 128, 4, 0, 1)
        mixer(512, 128, 4, 1, ti == 0, PC)
        ffn(1, 512, 128, 4, 4, 5)
        io.dma(yp[t0:t0 + 512, :].rearrange("(s p) d -> p s d", p=128), X.t[:, :, :], reads=[X.b])

    W0, W1, W2 = alloc_stg()
    io.dma(o_pss.rearrange("(hc q) n -> q hc n", q=128), S.t[:], reads=[S.b])
    bk = K.bank()
    TR(bk.t[0:8, 0:128], HC.t[:, :, 0], ID, [HC.b, CONST.b], bk)
    A(lambda h: h.activation(out=STG[0].t[0:8, 0:128], in_=bk.t[0:8, 0:128], func=AF.Copy), [bk.b], [STG[0].b])
    io.dma(o_prh, STG[0].t[0:8, 0:128], reads=[STG[0].b])
    fm_to_rows(lambda c: (RGC.t[:, c, 0:3], RGC.b), 8, 3, W1, lambda st: io.dma(o_prc, st.t[0:3, 0:1024], reads=[st.b]))
    fm_to_rows(lambda c: (XBCC.t[:, c, 0:3], XBCC.b), 32, 3, W0, lambda st: io.dma(o_psc, st.t[0:3, 0:4096], reads=[st.b]))

    if DO_SAMPLE:
        W0, W1, W2 = alloc_stg()
        for t in range(4):
            io.dma(X.t[t * 16:(t + 1) * 16, 0, :], xs[:, t, :], writes=[X.b], add=(t > 0))

        def rows_to_fm(load_fn, nrows, nchunks, dst, TMw):
            load_fn(TMw)
            for g in range(nchunks // 4):
                bk = K.bank()
                for ci in range(4):
                    TR(bk.t[:, ci * 48:ci * 48 + nrows], TMw.t[0:nrows, (4 * g + ci) * 128:(4 * g + ci + 1) * 128], ID[0:nrows, 0:nrows],
                       [TMw.b, CONST.b], bk, inc=(ci == 3))
                A(lambda h: h.activation(out=dst.t[:, 4 * g:4 * g + 4, 0:nrows], in_=bk.t[:, 0:192].rearrange("p (c t) -> p c t", c=4)[:, :, 0:nrows],
                                         func=AF.Copy), [bk.b], [dst.b])

        def ld_rows3(srcd, TMw, w):
            for t in range(3):
                io.dma(TMw.t[t * 16:(t + 1) * 16, 0:w], srcd[:, t, :], writes=[TMw.b], add=(t > 0))

        rows_to_fm(lambda TMw: ld_rows3(src, TMw, 1024), 48, 8, RGC, W1)
        rows_to_fm(lambda TMw: io.dma(TMw.t[0:16, 0:1024], srh, writes=[TMw.b]), 16, 8, HC, W2)
        rows_to_fm(lambda TMw: ld_rows3(ssc, TMw, 4096), 48, 32, XBCC, W0)

        ffn(0, 64, 64, 1, 0, 1)
        mixer(64, 64, 1, 16, False, SC)
        ffn(1, 64, 64, 1, 4, 5)
        for t in range(4):
            io.dma(ys[:, t, :], X.t[t * 16:(t + 1) * 16, 0, :], reads=[X.b])
        W0, W1, W2 = alloc_stg()
        fm_to_rows(lambda c: (HC.t[:, c, 0:16], HC.b), 8, 16, W2, lambda st: io.dma(o_srh, st.t[0:16, 0:1024], reads=[st.b]))

        def st3(dst, st, w):
            for t in range(3):
                io.dma(dst[:, t, :], st.t[t * 16:(t + 1) * 16, 0:w], reads=[st.b])

        fm_to_rows(lambda c: (RGC.t[:, c, 0:48], RGC.b), 8, 48, W1, lambda st: st3(o_src, st, 1024))
        fm_to_rows(lambda c: (XBCC.t[:, c, 0:48], XBCC.b), 32, 48, W0, lambda st: st3(o_ssc, st, 4096))

    for e in K.engs:
        sp.need(e.sid, e.cnt)
        for i, sid in enumerate(e.dsid):
            sp.need(sid, e.dcnt[i])
    return K


def _consts():
    c = np.zeros((128, C_END), np.float32)
    c[:, C_ID:C_ID + 128] = np.eye(128, dtype=np.float32)
    k = np.arange(128)
    c[:, C_PTRI:C_PTRI + 128] = (k[:, None] <= k[None, :])
    c[:, C_PSA:C_PSA + 128] = (k[:, None] > k[None, :])
    neg = np.where(k[None, :] >= k[:, None], 0.0, -30000.0).astype(np.float32)
    c[:, C_PNEG:C_PNEG + 512] = np.tile(neg, (1, 4))
    q = np.arange(64)
    sq = q % 16
    tq = q // 16
    same = sq[:, None] == sq[None, :]
    c[0:64, C_STRI:C_STRI + 64] = same & (tq[:, None] <= tq[None, :])
    c[0:64, C_SSA:C_SSA + 64] = same & (tq[:, None] > tq[None, :])
    negs = np.where(same & (tq[None, :] >= tq[:, None]), 0.0, -30000.0).astype(np.float32)
    c[0:64, C_SNEG:C_SNEG + 256] = np.tile(negs, (1, 4))
    c[0:64, C_SSEL:C_SSEL + 16] = (sq[:, None] == np.arange(16)[None, :])
    return c


def _maskb():
    sq = np.arange(64) % 16
    mb = (np.arange(16)[:, None] == sq[None, :]).astype(np.float32).reshape(1, 1024)
    return np.ascontiguousarray(np.broadcast_to(mb, (128, 1024)))


def _fm(v, nch):
    return np.ascontiguousarray(v.reshape(nch, 128).T)


def _prep_shared(inp):
    f = lambda a: np.ascontiguousarray(a, dtype=np.float32)
    pf = np.zeros((128, PF_END), np.float32)
    rcw = inp["rg_conv_w"][0]
    pf[:, PF_RGCW:PF_RGCW + 32] = rcw.reshape(4, 8, 128).transpose(2, 1, 0).reshape(128, 32)
    pf[:, PF_RGCB:PF_RGCB + 8] = _fm(inp["rg_conv_b"][0], 8)
    pf[:, PF_BA:PF_BA + 8] = _fm(inp["rg_ba"][0], 8)
    pf[:, PF_BX:PF_BX + 8] = _fm(inp["rg_bx"][0], 8)
    pf[:, PF_LAM:PF_LAM + 8] = _fm(inp["rg_lambda"][0], 8)
    scw = inp["ssd_conv_w"][0]
    pf[:, PF_SCW:PF_SCW + 128] = scw.reshape(4, 32, 128).transpose(2, 1, 0).reshape(128, 128)
    pf[:, PF_SCB:PF_SCB + 32] = _fm(inp["ssd_conv_b"][0], 32)
    pf[:, PF_SNW:PF_SNW + 16] = _fm(inp["ssd_norm_w"][0], 16)
    nv = np.stack([inp["n_ffn1_pre"][0], inp["n_ffn1_post"][0], inp["n_mix_pre"][0], inp["n_mix_post"][0],
                   inp["n_ffn2_pre"][0], inp["n_ffn2_post"][0]], 0)
    pt32 = np.stack([inp["ssd_dt_bias"][0], inp["ssd_a_log"][0], inp["ssd_d"][0]], 0)
    return {
        "wg1": f(inp["ffn1_wg"][0]), "wu1": f(inp["ffn1_wu"][0]), "wd1": f(inp["ffn1_wd"][0]),
        "wg2": f(inp["ffn2_wg"][0]), "wu2": f(inp["ffn2_wu"][0]), "wd2": f(inp["ffn2_wd"][0]),
        "win": f(inp["w_in"][0]), "wa": f(inp["rg_wa"][0]), "wx": f(inp["rg_wx"][0]),
        "wprg": f(inp["w_proj_rg"][0]), "wpssd": f(inp["w_proj_ssd"][0]), "wout": f(inp["w_out"][0]),
        "pf": pf, "nv": f(nv), "pt32": f(pt32), "consts": _consts(), "maskb": _maskb(),
    }


def kernel(**inp):
    inp = {k: np.asarray(v) for k, v in inp.items()}
    nc = bass.Bass("TRN2", target_bir_lowering=False)
    build(nc)
    shared = _prep_shared(inp)
    in_maps = []
    for c in range(8):
        m = dict(shared)
        m["xp"] = np.ascontiguousarray(inp["x_prompt"][c], dtype=np.float32)
        m["xs"] = np.ascontiguousarray(inp["x_sample"][c * 16:(c + 1) * 16], dtype=np.float32)
        m["srh"] = np.ascontiguousarray(inp["state_rg_h"][0, c * 16:(c + 1) * 16], dtype=np.float32)
        m["src"] = np.ascontiguousarray(inp["state_rg_conv"][0, c * 16:(c + 1) * 16], dtype=np.float32)
        m["sss"] = np.ascontiguousarray(inp["state_ssd"][0, c * 16:(c + 1) * 16], dtype=np.float32).reshape(16, 2048, 128)
        m["ssc"] = np.ascontiguousarray(inp["state_ssd_conv"][0, c * 16:(c + 1) * 16], dtype=np.float32)
        in_maps.append(m)
    res = run_bass_kernel_spmd(nc, in_maps, core_ids=list(range(8)))
    R = res.results
    cat = lambda k: np.concatenate([np.asarray(r[k], dtype=np.float32) for r in R], 0)
    y_prompt = np.stack([np.asarray(r["yp"], np.float32) for r in R], 0)
    y_sample = cat("ys")
    p_rg_h = np.stack([np.asarray(r["o_prh"], np.float32).reshape(1024) for r in R], 0)[None]
    p_rg_conv = np.stack([np.asarray(r["o_prc"], np.float32) for r in R], 0)[None]
    p_ssd = np.stack([np.asarray(r["o_pss"], np.float32).reshape(32, 64, 128) for r in R], 0)[None]
    p_ssd_conv = np.stack([np.asarray(r["o_psc"], np.float32) for r in R], 0)[None]
    s_rg_h = cat("o_srh")[None]
    s_rg_conv = cat("o_src")[None]
    s_ssd = cat("o_sss").reshape(128, 32, 64, 128)[None]
    s_ssd_conv = cat("o_ssc")[None]
    return (y_prompt, y_sample, p_rg_h, p_rg_conv, p_ssd, p_ssd_conv, s_rg_h, s_rg_conv, s_ssd, s_ssd_conv)
```

```python
import numpy as np
import concourse.bass as bass
import concourse.mybir as mybir
from concourse.bass_utils import run_bass_kernel_spmd

F32 = mybir.dt.float32
BF16 = mybir.dt.bfloat16
AF = mybir.ActivationFunctionType
ALU = mybir.AluOpType

D = 1024
DFF = 2816
NFC = 22
DIN = 10272
EPS = 1e-6
NSLOT = 4
SLOTC = 4096

C_ID, C_PTRI, C_PSA, C_PNEG, C_STRI, C_SSA, C_SNEG, C_SSEL, C_MASKB, C_END = (
    0, 128, 256, 384, 896, 960, 1024, 1280, 1296, 1296)
PF_RGCW, PF_RGCB, PF_BA, PF_BX, PF_LAM, PF_SCW, PF_SCB, PF_SNW, PF_END = 0, 32, 40, 48, 56, 64, 192, 224, 240


class Buf:
    __slots__ = ("name", "w", "rs", "const")

    def __init__(self, name, init=None, const=False):
        self.name = name
        self.w = {}
        self.rs = dict(init or {})
        self.const = const


class T:
    def __init__(self, t, b):
        self.t = t
        self.b = b

    def __getitem__(self, k):
        return self.t[k]


class Eng:
    def __init__(self, K, h, name, ndma=0, is_pe=False):
        self.K = K
        self.h = h
        self.name = name
        self.is_pe = is_pe
        self.sem = K.nc.alloc_semaphore("s_" + name)
        self.sid = K.newsid(self.sem)
        self.cnt = 0
        self.seen = {}
        self.dsems = [K.nc.alloc_semaphore("d_%s%d" % (name, i)) for i in range(ndma)]
        self.dsid = [K.newsid(s) for s in self.dsems]
        self.dcnt = [0] * ndma
        self.di = 0

    def need(self, sid, val):
        if val <= 0:
            return
        if self.is_pe and sid == self.sid:
            return
        if self.seen.get(sid, 0) >= val:
            return
        self.h.wait_ge(self.K.sems[sid], val)
        self.seen[sid] = val

    def _deps(self, reads, writes):
        for b in reads:
            for sid, v in b.w.items():
                self.need(sid, v)
        for b in writes:
            for sid, v in b.w.items():
                self.need(sid, v)
            for sid, v in b.rs.items():
                self.need(sid, v)

    def _upd(self, tok, reads, writes, add=False):
        for b in writes:
            if add:
                b.w[tok[0]] = max(b.w.get(tok[0], 0), tok[1])
            else:
                b.w = {tok[0]: tok[1]}
                b.rs = {}
        for b in reads:
            if not b.const:
                b.rs[tok[0]] = max(b.rs.get(tok[0], 0), tok[1])

    def op(self, fn, reads=(), writes=(), inc=True):
        self._deps(reads, writes)
        ins = fn(self.h)
        if inc:
            self.cnt += 1
            ins.then_inc(self.sem, 1)
            tok = (self.sid, self.cnt)
        else:
            tok = (self.sid, self.cnt + 1)
        self._upd(tok, reads, writes)
        self.K.nins += 1
        return ins

    def dma(self, out, in_, reads=(), writes=(), add=False):
        self._deps(reads, writes)
        i = self.di % len(self.dsems)
        self.di += 1
        self.need(self.dsid[i], self.dcnt[i])
        ins = self.h.dma_start(out=out, in_=in_)
        self.dcnt[i] += 16
        ins.then_inc(self.dsems[i], 16)
        tok = (self.dsid[i], self.dcnt[i])
        self._upd(tok, reads, writes, add=add)
        self.K.nins += 1
        return ins


class Kern:
    def __init__(self, nc):
        self.nc = nc
        self.sems = []
        self.nins = 0
        self.pe = Eng(self, nc.tensor, "pe", is_pe=True)
        self.act = Eng(self, nc.scalar, "act", ndma=4)
        self.dve = Eng(self, nc.vector, "dve")
        self.pool = Eng(self, nc.gpsimd, "pool", ndma=12)
        self.sp = Eng(self, nc.sync, "sp", ndma=12)
        self.engs = [self.pe, self.act, self.dve, self.pool, self.sp]
        self.banks = []
        for i in range(8):
            t = nc.alloc_psum_tensor("bank%d" % i, [128, 512], F32)
            self.banks.append(T(t, Buf("bank%d" % i)))
        self.bi = 0
        self.pinned = set()
        self.snap = {}
        self.ncnt = 0

    def newsid(self, sem):
        self.sems.append(sem)
        return len(self.sems) - 1

    def snapshot(self):
        s = {}
        for e in self.engs:
            if e.cnt:
                s[e.sid] = e.cnt
            for i, sid in enumerate(e.dsid):
                if e.dcnt[i]:
                    s[sid] = e.dcnt[i]
        self.snap = s

    def bank(self, pin=False):
        for _ in range(16):
            i = self.bi % 8
            self.bi += 1
            if i not in self.pinned:
                if pin:
                    self.pinned.add(i)
                return self.banks[i]
        raise RuntimeError("no bank")

    def unpin(self, bk):
        self.pinned.discard(self.banks.index(bk))

    def sb(self, name, shape, dt, const=False):
        self.ncnt += 1
        t = self.nc.alloc_sbuf_tensor("%s_%d" % (name, self.ncnt), list(shape), dt)
        return T(t, Buf(name, const=const))


class Arena:
    def __init__(self, K, nbytes):
        self.K = K
        self.t = K.nc.alloc_sbuf_tensor("arena", [128, nbytes // 2], BF16)
        self.n = nbytes
        self.off = 0

    def reset(self):
        self.off = 0
        self.K.snapshot()

    def alloc(self, name, shape, dt):
        esz = 4 if dt == F32 else 2
        n = int(np.prod(shape[1:])) * esz
        n = (n + 63) // 64 * 64
        assert self.off + n <= self.n, ("arena overflow", name, self.off, n, self.n)
        v = self.t[:, self.off // 2:(self.off + n) // 2]
        if dt == F32:
            v = v.bitcast(F32)
        cnt = int(np.prod(shape[1:]))
        v = v[:, 0:cnt]
        if len(shape) == 3:
            v = v.rearrange("p (a b) -> p a b", a=shape[1])
        elif len(shape) == 4:
            v = v.rearrange("p (a b c) -> p a b c", a=shape[1], b=shape[2])
        self.off += n
        return T(v, Buf(name, init=self.K.snap))


def bc(ap, shape):
    return ap.broadcast_to(list(shape))


def build(nc, NPT=4, DO_SAMPLE=True, DEBUG=False):
    K = Kern(nc)
    dbg = []
    pe, act, dve, pool, sp = K.pe, K.act, K.dve, K.pool, K.sp
    io = pool

    def din(name, shape):
        return nc.dram_tensor(name, list(shape), F32, kind="ExternalInput").ap()

    def dout(name, shape):
        return nc.dram_tensor(name, list(shape), F32, kind="ExternalOutput").ap()

    xp = din("xp", [2048, D]); xs = din("xs", [16, 4, D])
    srh = din("srh", [16, D]); src = din("src", [16, 3, D])
    sss = din("sss", [16, 2048, 128]); ssc = din("ssc", [16, 3, 4096])
    wg = [din("wg1", [D, DFF]), din("wg2", [D, DFF])]
    wu = [din("wu1", [D, DFF]), din("wu2", [D, DFF])]
    wd = [din("wd1", [DFF, D]), din("wd2", [DFF, D])]
    win = din("win", [D, DIN])
    wa = din("wa", [8, 128, 128]); wx = din("wx", [8, 128, 128])
    wprg = din("wprg", [D, D]); wpssd = din("wpssd", [2048, D]); wout = din("wout", [D, D])
    pf_d = din("pf", [128, PF_END]); nv = din("nv", [6, D])
    pt32 = din("pt32", [3, 32]); consts_d = din("consts", [128, C_END]); maskb_d = din("maskb", [128, 1024])

    yp = dout("yp", [2048, D]); ys = dout("ys", [16, 4, D])
    o_prh = dout("o_prh", [8, 128]); o_prc = dout("o_prc", [3, D])
    o_pss = dout("o_pss", [2048, 128]); o_psc = dout("o_psc", [3, 4096])
    o_srh = dout("o_srh", [16, D]); o_src = dout("o_src", [16, 3, D])
    o_sss = dout("o_sss", [16, 2048, 128]); o_ssc = dout("o_ssc", [16, 3, 4096])

    CONST = K.sb("const", [128, C_END], F32, const=True)
    IDB = K.sb("idb", [128, 128], BF16, const=True)
    PF = K.sb("pf", [128, PF_END], F32, const=True)
    NSP8 = K.sb("nsp8", [128, 8], F32, const=True)
    DTB = K.sb("dtb", [128, 32], F32, const=True)
    ABC = K.sb("abc", [128, 32], F32, const=True)
    DBC = K.sb("dbc", [128, 32], F32, const=True)
    NW = [K.sb("nw%d" % i, [128, D], F32) for i in range(2)]
    X = K.sb("x", [128, 4, D], F32)
    XNB = [K.sb("xnb0", [128, D], BF16)]
    XNT = K.sb("xnt", [128, 8, 512], BF16)
    RING = [K.sb("ring%d" % i, [128, SLOTC], BF16) for i in range(NSLOT)]
    RGC = K.sb("rgc", [128, 8, 48], F32)
    HC = K.sb("hc", [128, 8, 16], F32)
    XBCC = K.sb("xbcc", [128, 32, 48], F32)
    S = K.sb("sst", [128, 16, 128], F32)
    JUNK = K.sb("junk", [128, 256], BF16)
    SS = K.sb("ss", [128, 8], F32)
    RS = K.sb("rs", [128, 8], F32)
    SS2 = K.sb("ss2", [128, 4], F32)
    RS2 = K.sb("rs2", [128, 4], F32)
    TMPS = K.sb("tmps", [128, 64], F32)
    RGW = K.sb("rgw", [128, 2048], BF16, const=True)
    AR = Arena(K, nc.sbuf_bytes_remaining - 2048)

    ID = CONST.t[:, C_ID:C_ID + 128]
    pool.dma(RGW.t[:, 0:1024].rearrange("p (k c) -> p k c", k=8), wa.rearrange("h i j -> i h j"), writes=[RGW.b])
    pool.dma(RGW.t[:, 1024:2048].rearrange("p (k c) -> p k c", k=8), wx.rearrange("h i j -> i h j"), writes=[RGW.b], add=True)

    sp.dma(CONST.t[:], consts_d, writes=[CONST.b])
    sp.dma(PF.t[:], pf_d, writes=[PF.b])
    sp.dma(DTB.t[:], pt32[0:1, :].partition_broadcast(128).rearrange("p a b -> p (a b)"), writes=[DTB.b])
    sp.dma(ABC.t[:], pt32[1:2, :].partition_broadcast(128).rearrange("p a b -> p (a b)"), writes=[ABC.b])
    sp.dma(DBC.t[:], pt32[2:3, :].partition_broadcast(128).rearrange("p a b -> p (a b)"), writes=[DBC.b])
    dve.op(lambda h: h.tensor_copy(IDB.t[:], ID), [CONST.b], [IDB.b])
    act.op(lambda h: h.activation(out=ABC.t[:], in_=ABC.t[:], func=AF.Exp), [ABC.b], [ABC.b])
    dve.op(lambda h: h.tensor_scalar(out=ABC.t[:], in0=ABC.t[:], scalar1=-1.0, scalar2=None, op0=ALU.mult), [ABC.b], [ABC.b])
    act.op(lambda h: h.activation(out=NSP8.t[:], in_=PF.t[:, PF_LAM:PF_LAM + 8], func=AF.Exp, scale=-1.0), [PF.b], [NSP8.b])
    act.op(lambda h: h.activation(out=NSP8.t[:], in_=NSP8.t[:], func=AF.Ln, bias=1.0), [NSP8.b], [NSP8.b])
    dve.op(lambda h: h.tensor_scalar(out=NSP8.t[:], in0=NSP8.t[:], scalar1=-8.0, scalar2=None, op0=ALU.mult), [NSP8.b], [NSP8.b])

    HPF = K.sb("hpf", [128, 24], F32, const=True)
    dve.op(lambda h: h.tensor_scalar(out=HPF.t[:, 0:16], in0=PF.t[:, PF_BA:PF_BA + 16], scalar1=0.5, scalar2=None, op0=ALU.mult), [PF.b], [HPF.b])
    dve.op(lambda h: h.tensor_scalar(out=HPF.t[:, 16:24], in0=NSP8.t[:, 0:8], scalar1=0.5, scalar2=None, op0=ALU.mult), [NSP8.b, HPF.b], [HPF.b])

    def wview(w, r0, nk, c0, ncols):
        return w[r0:r0 + nk * 128, c0:c0 + ncols].rearrange("(k p) c -> p k c", p=128)

    def ffn_blocks(i):
        return [[(0, 8, 256, wview(wg[i], 0, 8, j * 256, 256)), (2048, 8, 256, wview(wu[i], 0, 8, j * 256, 256))]
                for j in range(11)]

    def mixer_blocks():
        bl = []
        for c2 in range(4):
            bl.append([(0, 8, 256, wview(win, 0, 8, c2 * 256, 256)), (2048, 8, 256, wview(win, 0, 8, 1024 + c2 * 256, 256))])
        for zc in range(4):
            bl.append([(0, 8, 512, wview(win, 0, 8, 2048 + zc * 512, 512))])
        bl.append([(0, 8, 32, wview(win, 0, 8, 8192, 32))])
        for i in range(8):
            bl.append([(0, 8, 512, wview(win, 0, 8, 4096 + i * 512, 512))])
        for dc2 in range(4):
            bl.append([(0, 8, 256, wview(wprg, 0, 8, dc2 * 256, 256)), (2048, 8, 256, wview(win, 0, 8, 8224 + dc2 * 256, 256))])
            bl.append([(0, 8, 256, wview(win, 0, 8, 9248 + dc2 * 256, 256))])
            bl.append([(0, 16, 256, wview(wpssd, 0, 16, dc2 * 256, 256))])
        for half in range(2):
            bl.append([(0, 8, 512, wview(wout, 0, 8, half * 512, 512))])
        return bl

    ntiles = NPT + (1 if DO_SAMPLE else 0)
    tile_seq = ffn_blocks(0) + mixer_blocks() + ffn_blocks(1)
    NB = len(tile_seq)
    seq = tile_seq * ntiles
    wst = {"next": 0, "emitted": 0}
    SCR = nc.dram_tensor("wscr", [NB, 128, SLOTC], BF16, kind="Internal").ap()
    SCRB = [Buf("scr%d" % i) for i in range(NB)]
    WDS = nc.dram_tensor("wdscr", [2, 11, 128, 2048], BF16, kind="Internal").ap()
    WDSB = [[Buf("wds%d_%d" % (f, j)) for j in range(11)] for f in range(2)]
    wd_tile = [0, 0]
    CDSCR = nc.dram_tensor("cdscr", [32, 64], F32, kind="Internal").ap()
    CDSB = Buf("cdscr")

    def wnext(n=1, hold=0):
        r0 = wst["next"]
        wst["next"] += n
        lim = min(len(seq), r0 + NSLOT - hold)
        while wst["emitted"] < lim:
            r = wst["emitted"]
            slot = RING[r % NSLOT]
            used = max(off + nk * ncols for (off, nk, ncols, _) in seq[r])
            if r < NB or ntiles == 1:
                for ii, (off, nk, ncols, src_ap) in enumerate(seq[r]):
                    dst = slot.t[:, off:off + nk * ncols].rearrange("p (k c) -> p k c", k=nk)
                    pool.dma(dst, src_ap, writes=[slot.b], add=(ii > 0))
                if ntiles > 1:
                    sp.dma(SCR[r, :, 0:used], slot.t[:, 0:used], reads=[slot.b], writes=[SCRB[r]])
            else:
                sp.dma(slot.t[:, 0:used], SCR[r % NB, :, 0:used], reads=[SCRB[r % NB]], writes=[slot.b])
            wst["emitted"] += 1
        return [RING[(r0 + i) % NSLOT] for i in range(n)]

    def wd_load(fi, j, dst):
        first = wd_tile[fi] < 11
        wd_tile[fi] += 1
        d2 = dst.t[:].rearrange("p k c -> p (k c)")
        if first:
            pool.dma(dst.t[:], wd[fi][j * 256:(j + 1) * 256, :].rearrange("(k p) c -> p k c", p=128), writes=[dst.b])
            if ntiles > 1:
                sp.dma(WDS[fi, j], d2, reads=[dst.b], writes=[WDSB[fi][j]])
        else:
            sp.dma(d2, WDS[fi, j], reads=[WDSB[fi][j]], writes=[dst.b])

    def A(fn, r, w):
        return act.op(fn, r, w)

    def V(fn, r, w):
        return dve.op(fn, r, w)

    def MM(out, lhsT, rhs, start, stop, reads, bk):
        return pe.op(lambda h: h.matmul(out, lhsT, rhs, start=start, stop=stop), reads, [bk.b], inc=stop)

    def TR(out, in_, ident, reads, bk, inc=True):
        return pe.op(lambda h: h.transpose(out, in_, ident), reads, [bk.b], inc=inc)

    nwi = [0]

    def load_nw(idx):
        nw = NW[nwi[0] % 2]
        nwi[0] += 1
        io.dma(nw.t[:], nv[idx:idx + 1, :].partition_broadcast(128).rearrange("p a b -> p (a b)"), writes=[nw.b])
        return nw

    def rmsnorm_T(Pt, NS, nidx):
        nw = load_nw(nidx)
        for s in range(NS):
            A(lambda h: h.activation(out=XNB[0].t[0:Pt, :], in_=X.t[0:Pt, s, :], func=AF.Square, accum_out=SS.t[0:Pt, s:s + 1]),
              [X.b], [XNB[0].b, SS.b])
        A(lambda h: h.activation(out=RS.t[0:Pt, 0:NS], in_=SS.t[0:Pt, 0:NS], func=AF.Ln, scale=1.0 / D, bias=EPS), [SS.b], [RS.b])
        A(lambda h: h.activation(out=RS.t[0:Pt, 0:NS], in_=RS.t[0:Pt, 0:NS], func=AF.Exp, scale=-0.5), [RS.b], [RS.b])
        for s in range(NS):
            xb = XNB[0]
            V(lambda h: h.scalar_tensor_tensor(out=xb.t[0:Pt, :], in0=X.t[0:Pt, s, :], scalar=RS.t[0:Pt, s:s + 1], in1=nw.t[0:Pt, :],
                                               op0=ALU.mult, op1=ALU.mult), [X.b, RS.b, nw.b], [xb.b])
            bk = K.bank()
            psb = bk.t[:].bitcast(BF16)
            for c in range(8):
                TR(psb[:, c * 128:c * 128 + Pt], xb.t[0:Pt, c * 128:(c + 1) * 128], IDB.t[0:Pt, 0:Pt], [xb.b, IDB.b], bk, inc=(c == 7))
            A(lambda h: h.activation(out=XNT.t[:, :, s * 128:s * 128 + Pt], in_=psb.rearrange("p (c t) -> p c t", c=8)[:, :, 0:Pt], func=AF.Copy),
              [bk.b], [XNT.b])

    def post_norm_res(Pt, s, p0, p1, nw, YT, scale):
        A(lambda h: h.activation(out=YT.t[0:Pt, 0:512], in_=p0.t[0:Pt, :], func=AF.Square, accum_out=SS2.t[0:Pt, 0:1]), [p0.b], [YT.b, SS2.b])
        A(lambda h: h.activation(out=YT.t[0:Pt, 512:1024], in_=p1.t[0:Pt, :], func=AF.Square, accum_out=SS2.t[0:Pt, 1:2]), [p1.b], [YT.b, SS2.b])
        V(lambda h: h.tensor_tensor(out=SS2.t[0:Pt, 2:3], in0=SS2.t[0:Pt, 0:1], in1=SS2.t[0:Pt, 1:2], op=ALU.add), [SS2.b], [SS2.b])
        A(lambda h: h.activation(out=RS2.t[0:Pt, 0:1], in_=SS2.t[0:Pt, 2:3], func=AF.Ln, scale=1.0 / D, bias=EPS), [SS2.b], [RS2.b])
        A(lambda h: h.activation(out=RS2.t[0:Pt, 0:1], in_=RS2.t[0:Pt, 0:1], func=AF.Exp, scale=-0.5), [RS2.b], [RS2.b])
        for half, ph in enumerate((p0, p1)):
            V(lambda h: h.scalar_tensor_tensor(out=YT.t[0:Pt, half * 512:(half + 1) * 512], in0=ph.t[0:Pt, :], scalar=RS2.t[0:Pt, 0:1],
                                               in1=nw.t[0:Pt, half * 512:(half + 1) * 512], op0=ALU.mult, op1=ALU.mult),
              [ph.b, RS2.b, nw.b], [YT.b])
        V(lambda h: h.scalar_tensor_tensor(out=X.t[0:Pt, s, :], in0=YT.t[0:Pt, :], scalar=scale, in1=X.t[0:Pt, s, :],
                                           op0=ALU.mult, op1=ALU.add), [YT.b, X.b], [X.b])

    def ffn(fi, T_, Pt, NS, npre, npost):
        AR.reset()
        HT = AR.alloc("ht", [128, NFC, T_], BF16)
        WDp = [AR.alloc("wd%d" % j, [128, 2, D], BF16) for j in range(11)]
        SG = [AR.alloc("sg%d" % i, [128, 512], F32) for i in range(2)]
        YT = AR.alloc("yt", [128, D], F32)
        rmsnorm_T(Pt, NS, npre)
        nwp = load_nw(npost)
        FB = 1 if T_ >= 512 else 512 // T_
        nb = 0
        for j in range(11):
            slot = wnext(1)[0]
            wd_load(fi, j, WDp[j])
            for f2 in range(2):
                fc = 2 * j + f2
                if nb == 0:
                    pg = K.bank()
                    pu = K.bank()
                    fc0 = fc
                col = nb * T_
                for k in range(8):
                    MM(pg.t[:, col:col + T_], slot.t[:, k * 256 + f2 * 128:k * 256 + f2 * 128 + 128], XNT.t[:, k, 0:T_], k == 0, k == 7, [slot.b, XNT.b], pg)
                for k in range(8):
                    MM(pu.t[:, col:col + T_], slot.t[:, 2048 + k * 256 + f2 * 128:2048 + k * 256 + f2 * 128 + 128], XNT.t[:, k, 0:T_], k == 0, k == 7,
                       [slot.b, XNT.b], pu)
                nb += 1
                if nb == FB or fc == NFC - 1:
                    W_ = nb * T_
                    sg = SG[(fc // FB) % 2]
                    A(lambda h: h.activation(out=sg.t[:, 0:W_], in_=pg.t[:, 0:W_], func=AF.Silu), [pg.b], [sg.b])
                    V(lambda h: h.tensor_tensor(out=HT.t[:, fc0:fc0 + nb, :].rearrange("p a b -> p (a b)"), in0=sg.t[:, 0:W_], in1=pu.t[:, 0:W_], op=ALU.mult),
                      [sg.b, pu.b], [HT.b])
                    nb = 0
        for s in range(NS):
            p0 = K.bank()
            p1 = K.bank()
            for half, ph in enumerate((p0, p1)):
                for fc in range(NFC):
                    MM(ph.t[0:Pt, :], HT.t[:, fc, s * 128:s * 128 + Pt], WDp[fc // 2].t[:, fc % 2, half * 512:(half + 1) * 512],
                       fc == 0, fc == NFC - 1, [HT.b, WDp[fc // 2].b], ph)
            post_norm_res(Pt, s, p0, p1, nwp, YT, 0.5)

    def mixer(T_, Pt, NS, nseq, first, cst):
        sh = nseq
        TRI, SAm, NEG4, SEL, MASKB = cst
        AR.reset()
        YRG = AR.alloc("yrg", [128, 8, T_], BF16)
        SZ = AR.alloc("sz", [128, NS, 2048], BF16)
        XS = AR.alloc("xs", [128, NS, 2048], BF16)
        BT = AR.alloc("bt", [128, 8, T_], BF16)
        CT = AR.alloc("ct", [128, 8, T_], BF16)
        YST = AR.alloc("yst", [128, 16, T_], BF16)
        DT = AR.alloc("dt", [128, NS, 32], F32)
        DTA = AR.alloc("dta", [128, NS, 32], F32)
        CDa = AR.alloc("cda", [128, 16, NS * nseq], F32)
        TOT = AR.alloc("tot", [128, NS * nseq], F32)
        mark = AR.off
        WK = [AR.alloc("wk%d" % i, [128, 48 + T_], F32) for i in range(3)]
        XCs = [AR.alloc("xc%d" % i, [128, T_], F32) for i in range(2)]
        XCBs = [AR.alloc("xcb%d" % i, [128, T_], BF16) for i in range(2)]
        RT = [{n: AR.alloc("t_%s%d" % (n, i), [128, T_], F32) for n in ("sa", "sx", "mu", "h", "gg")} for i in range(2)]

        rmsnorm_T(Pt, NS, 2)
        nwp = load_nw(3)
        wi = [0]

        def conv(wk, base_w, cidx, bias_col, XC):
            if bias_col is None:
                V(lambda h: h.tensor_scalar(out=XC.t[:, 0:T_], in0=wk.t[:, 0:T_], scalar1=PF.t[:, base_w + cidx * 4:base_w + cidx * 4 + 1],
                                            scalar2=None, op0=ALU.mult), [wk.b, PF.b], [XC.b])
            else:
                V(lambda h: h.tensor_scalar(out=XC.t[:, 0:T_], in0=wk.t[:, 0:T_], scalar1=PF.t[:, base_w + cidx * 4:base_w + cidx * 4 + 1],
                                            scalar2=PF.t[:, bias_col:bias_col + 1], op0=ALU.mult, op1=ALU.add), [wk.b, PF.b], [XC.b])
            for k in range(1, 4):
                V(lambda h: h.scalar_tensor_tensor(out=XC.t[:, 0:T_], in0=wk.t[:, k * sh:k * sh + T_],
                                                   scalar=PF.t[:, base_w + cidx * 4 + k:base_w + cidx * 4 + k + 1], in1=XC.t[:, 0:T_],
                                                   op0=ALU.mult, op1=ALU.add), [wk.b, PF.b, XC.b], [XC.b])

        def proj_to_wk(slot, coloff, ncols_blk, ci, carry, cidx, on_dve=False):
            wk = WK[wi[0] % 3]
            wi[0] += 1
            A(lambda h: h.activation(out=wk.t[:, 0:3 * sh], in_=carry.t[:, cidx, 0:3 * sh], func=AF.Copy), [carry.b], [wk.b])
            bk = K.bank()
            for k in range(8):
                o = coloff + k * ncols_blk + ci * 128
                MM(bk.t[:, 0:T_], slot.t[:, o:o + 128], XNT.t[:, k, 0:T_], k == 0, k == 7, [slot.b, XNT.b], bk)
            if on_dve:
                V(lambda h: h.tensor_copy(wk.t[:, 3 * sh:3 * sh + T_], bk.t[:, 0:T_]), [bk.b], [wk.b])
            else:
                A(lambda h: h.activation(out=wk.t[:, 3 * sh:3 * sh + T_], in_=bk.t[:, 0:T_], func=AF.Copy), [bk.b], [wk.b])
            A(lambda h: h.activation(out=carry.t[:, cidx, 0:3 * sh], in_=wk.t[:, T_:T_ + 3 * sh], func=AF.Copy), [wk.b], [carry.b])
            return wk

        rgw = RGW
        rg_slots = {}
        rg_wk = {}

        def rgA(c):
            ci = c % 2
            if ci == 0:
                rg_slots[c // 2] = wnext(1, hold=1)[0]
            slot = rg_slots[c // 2]
            rg_wk[c] = proj_to_wk(slot, 0, 256, ci, RGC, c, on_dve=False)

        def rgA1b(c):
            conv(rg_wk.pop(c), PF_RGCW, c, PF_RGCB + c, XCs[c % 2])

        def rgCast(c):
            XC = XCs[c % 2]
            XCB = XCBs[c % 2]
            A(lambda h: h.activation(out=XCB.t[:, 0:T_], in_=XC.t[:, 0:T_], func=AF.Copy), [XC.b], [XCB.b])

        def rgA2(c):
            ci = c % 2
            slot = rg_slots[c // 2]
            XCB = XCBs[c % 2]
            pa = K.bank()
            MM(pa.t[:, 0:T_], rgw.t[:, c * 128:(c + 1) * 128], XCB.t[:, 0:T_], True, True, [rgw.b, XCB.b], pa)
            px = K.bank()
            MM(px.t[:, 0:T_], rgw.t[:, 1024 + c * 128:1024 + (c + 1) * 128], XCB.t[:, 0:T_], True, True, [rgw.b, XCB.b], px)
            pg = K.bank()
            for k in range(8):
                o = 2048 + k * 256 + ci * 128
                MM(pg.t[:, 0:T_], slot.t[:, o:o + 128], XNT.t[:, k, 0:T_], k == 0, k == 7, [slot.b, XNT.b], pg)
            return (pa, px, pg)

        def rgB(c, pa, px, pg):
            XC = XCs[c % 2]
            R_ = RT[c % 2]
            t_sa, t_sx, t_mu, t_h, t_gg = R_["sa"], R_["sx"], R_["mu"], R_["h"], R_["gg"]
            t_a = t_sa
            A(lambda h: h.activation(out=t_sa.t[:, 0:T_], in_=pa.t[:, 0:T_], func=AF.Tanh, scale=0.5, bias=HPF.t[:, c:c + 1]),
              [pa.b, HPF.b], [t_sa.b])
            A(lambda h: h.activation(out=t_sx.t[:, 0:T_], in_=px.t[:, 0:T_], func=AF.Tanh, scale=0.5, bias=HPF.t[:, 8 + c:9 + c]),
              [px.b, HPF.b], [t_sx.b])
            A(lambda h: h.activation(out=t_a.t[:, 0:T_], in_=t_sa.t[:, 0:T_], func=AF.Exp, scale=HPF.t[:, 16 + c:17 + c], bias=HPF.t[:, 16 + c:17 + c]),
              [t_sa.b, HPF.b], [t_a.b])
            A(lambda h: h.activation(out=t_gg.t[:, 0:T_], in_=pg.t[:, 0:T_], func=AF.Square), [pg.b], [t_gg.b])
            V(lambda h: h.tensor_scalar(out=t_gg.t[:, 0:T_], in0=t_gg.t[:, 0:T_], scalar1=0.044715, scalar2=1.0, op0=ALU.mult, op1=ALU.add),
              [t_gg.b], [t_gg.b])
            V(lambda h: h.tensor_tensor(out=t_gg.t[:, 0:T_], in0=t_gg.t[:, 0:T_], in1=pg.t[:, 0:T_], op=ALU.mult), [t_gg.b, pg.b], [t_gg.b])
            V(lambda h: h.tensor_tensor(out=t_mu.t[:, 0:T_], in0=t_a.t[:, 0:T_], in1=t_a.t[:, 0:T_], op=ALU.mult), [t_a.b], [t_mu.b])
            V(lambda h: h.tensor_scalar(out=t_mu.t[:, 0:T_], in0=t_mu.t[:, 0:T_], scalar1=1.0, scalar2=None, op0=ALU.min), [t_mu.b], [t_mu.b])
            A(lambda h: h.activation(out=t_gg.t[:, 0:T_], in_=t_gg.t[:, 0:T_], func=AF.Tanh, scale=0.7978845608028654), [t_gg.b], [t_gg.b])
            A(lambda h: h.activation(out=t_mu.t[:, 0:T_], in_=t_mu.t[:, 0:T_], func=AF.Sqrt, scale=-0.25, bias=0.25), [t_mu.b], [t_mu.b])
            V(lambda h: h.scalar_tensor_tensor(out=t_gg.t[:, 0:T_], in0=t_gg.t[:, 0:T_], scalar=1.0, in1=pg.t[:, 0:T_], op0=ALU.add, op1=ALU.mult),
              [t_gg.b, pg.b], [t_gg.b])
            if first:
                V(lambda h: h.memset(t_mu.t[:, 0:1], 0.5), [], [t_mu.b])
            V(lambda h: h.scalar_tensor_tensor(out=XC.t[:, 0:T_], in0=t_sx.t[:, 0:T_], scalar=1.0, in1=XC.t[:, 0:T_], op0=ALU.add, op1=ALU.mult),
              [XC.b, t_sx.b], [XC.b])
            V(lambda h: h.tensor_tensor(out=XC.t[:, 0:T_], in0=XC.t[:, 0:T_], in1=t_mu.t[:, 0:T_], op=ALU.mult), [XC.b, t_mu.b], [XC.b])
            if nseq == 1:
                V(lambda h: h.tensor_tensor_scan(out=t_h.t[:, 0:T_], data0=t_a.t[:, 0:T_], data1=XC.t[:, 0:T_], initial=HC.t[:, c, 0:1],
                                                 op0=ALU.mult, op1=ALU.add), [t_a.b, XC.b, HC.b], [t_h.b])
            else:
                for t in range(T_ // nseq):
                    prev = HC.t[:, c, 0:nseq] if t == 0 else t_h.t[:, (t - 1) * nseq:t * nseq]
                    V(lambda h: h.tensor_tensor(out=t_h.t[:, t * nseq:(t + 1) * nseq], in0=t_a.t[:, t * nseq:(t + 1) * nseq], in1=prev, op=ALU.mult),
                      [t_a.b, HC.b, t_h.b], [t_h.b])
                    V(lambda h: h.tensor_tensor(out=t_h.t[:, t * nseq:(t + 1) * nseq], in0=t_h.t[:, t * nseq:(t + 1) * nseq],
                                                in1=XC.t[:, t * nseq:(t + 1) * nseq], op=ALU.add), [t_h.b, XC.b], [t_h.b])
            A(lambda h: h.activation(out=HC.t[:, c, 0:nseq], in_=t_h.t[:, T_ - nseq:T_], func=AF.Copy), [t_h.b], [HC.b])
            V(lambda h: h.scalar_tensor_tensor(out=YRG.t[:, c, 0:T_], in0=t_gg.t[:, 0:T_], scalar=0.5, in1=t_h.t[:, 0:T_], op0=ALU.mult, op1=ALU.mult),
              [t_h.b, t_gg.b], [YRG.b])

        rgA(0)
        rgA1b(0)
        rgCast(0)
        rgA(1)
        pend = rgA2(0)
        for c in range(8):
            if c + 1 < 8:
                rgA1b(c + 1)
            rgB(c, *pend)
            if c + 1 < 8:
                rgCast(c + 1)
            if c + 2 < 8:
                rgA(c + 2)
            pend = rgA2(c + 1) if c + 1 < 8 else None

        for zc in range(4):
            slot = wnext(1)[0]
            for s in range(NS):
                bk = K.bank()
                for k in range(8):
                    MM(bk.t[0:Pt, :], XNT.t[:, k, s * 128:s * 128 + Pt], slot.t[:, k * 512:(k + 1) * 512], k == 0, k == 7, [slot.b, XNT.b], bk)
                A(lambda h: h.activation(out=SZ.t[0:Pt, s, zc * 512:(zc + 1) * 512], in_=bk.t[0:Pt, :], func=AF.Silu), [bk.b], [SZ.b])
        slot = wnext(1)[0]
        VV = AR.alloc("vv", [128, NS, 32], F32)
        AVt = AR.alloc("av", [128, NS, 32], F32)
        for s in range(NS):
            bk = K.bank()
            for k in range(8):
                MM(bk.t[0:Pt, 0:32], XNT.t[:, k, s * 128:s * 128 + Pt], slot.t[:, k * 32:(k + 1) * 32], k == 0, k == 7, [slot.b, XNT.b], bk)
            V(lambda h: h.tensor_tensor(out=VV.t[0:Pt, s, :], in0=bk.t[0:Pt, 0:32], in1=DTB.t[0:Pt, :], op=ALU.add), [bk.b, DTB.b], [VV.b])
        V(lambda h: h.scalar_tensor_tensor(out=AVt.t[0:Pt], in0=VV.t[0:Pt], scalar=-1.0, in1=VV.t[0:Pt], op0=ALU.mult, op1=ALU.max), [VV.b], [AVt.b])
        A(lambda h: h.activation(out=AVt.t[0:Pt], in_=AVt.t[0:Pt], func=AF.Exp, scale=-1.0), [AVt.b], [AVt.b])
        A(lambda h: h.activation(out=AVt.t[0:Pt], in_=AVt.t[0:Pt], func=AF.Ln, bias=1.0), [AVt.b], [AVt.b])
        V(lambda h: h.scalar_tensor_tensor(out=DT.t[0:Pt], in0=VV.t[0:Pt], scalar=0.0, in1=AVt.t[0:Pt], op0=ALU.max, op1=ALU.add),
          [VV.b, AVt.b], [DT.b])
        V(lambda h: h.tensor_tensor(out=DTA.t[0:Pt], in0=DT.t[0:Pt], in1=bc(ABC.t[0:Pt, :].unsqueeze(1), [Pt, NS, 32]), op=ALU.mult),
          [DT.b, ABC.b], [DTA.b])

        J = NS * nseq
        bk = K.bank()
        for s in range(NS):
            MM(bk.t[0:32, s * nseq:(s + 1) * nseq], DTA.t[0:Pt, s, :], SEL[0:Pt, 0:nseq], True, True, [DTA.b, CONST.b], bk)
        A(lambda h: h.activation(out=TOT.t[0:32, 0:J], in_=bk.t[0:32, 0:J], func=AF.Exp), [bk.b], [TOT.b])
        io.dma(CDSCR[:, 0:J], TOT.t[0:32, 0:J], reads=[TOT.b], writes=[CDSB])
        for qh in range(2):
            src_ap = CDSCR.rearrange("(hc two) j -> two hc j", two=2)[qh][:, 0:J].unsqueeze(0).broadcast_to([64, 16, J])
            io.dma(CDa.t[qh * 64:(qh + 1) * 64, :, :], src_ap, reads=[CDSB], writes=[CDa.b], add=(qh > 0))

        XSBs = [AR.alloc("xsb%d" % i, [128, T_], BF16) for i in range(4)]
        x_slot = [None]
        x_ev = {}

        def xA(cc):
            ci = cc % 4
            if ci == 0:
                x_slot[0] = wnext(1)[0]
            return proj_to_wk(x_slot[0], 0, 512, ci, XBCC, cc)

        def xB(cc, wk):
            XC = XCs[cc % 2]
            XSB = XSBs[cc % 4]
            conv(wk, PF_SCW, cc, None, XC)
            bcol = PF.t[:, PF_SCB + cc:PF_SCB + cc + 1]
            if cc < 16:
                A(lambda h: h.activation(out=XSB.t[:, 0:T_], in_=XC.t[:, 0:T_], func=AF.Silu, bias=bcol), [XC.b, PF.b], [XSB.b])
            elif cc < 24:
                A(lambda h: h.activation(out=BT.t[:, cc - 16, 0:T_], in_=XC.t[:, 0:T_], func=AF.Silu, bias=bcol), [XC.b, PF.b], [BT.b])
            else:
                A(lambda h: h.activation(out=CT.t[:, cc - 24, 0:T_], in_=XC.t[:, 0:T_], func=AF.Silu, bias=bcol), [XC.b, PF.b], [CT.b])

        def xTR(cc):
            if cc < 0 or cc >= 16:
                return
            XSB = XSBs[cc % 4]
            bk = K.bank()
            psb = bk.t[:].bitcast(BF16)
            for s in range(NS):
                TR(psb[0:Pt, s * 128:(s + 1) * 128], XSB.t[:, s * 128:s * 128 + Pt], IDB.t[:, :], [XSB.b, IDB.b], bk, inc=(s == NS - 1))
            K.pinned.add(K.banks.index(bk))
            x_ev[cc] = bk

        def xB2(cc):
            bk = x_ev.pop(cc, None)
            if bk is None:
                return
            psb = bk.t[:].bitcast(BF16)
            A(lambda h: h.activation(out=XS.t[0:Pt, :, cc * 128:(cc + 1) * 128],
                                     in_=psb[0:Pt, 0:NS * 128].rearrange("p (s c) -> p s c", s=NS), func=AF.Copy), [bk.b], [XS.b])
            K.unpin(bk)

        xpend = {0: xA(0), 1: xA(1)}
        for cc in range(32):
            if cc + 2 < 32:
                xpend[cc + 2] = xA(cc + 2)
            xB(cc, xpend.pop(cc))
            xTR(cc - 2)
            xB2(cc - 3)
        xTR(30); xTR(31)
        for cc in range(29, 32):
            xB2(cc)

        AR.off = mark
        K.snapshot()
        Pq = Pt
        XDT = AR.alloc("xdt", [128, 2048], BF16)
        XDD = AR.alloc("xdd", [128, 2048], BF16)
        BTM = AR.alloc("btm", [128, 1024], BF16)
        STTs = [AR.alloc("stt%d" % i, [128, 2048], BF16) for i in range(2 if nseq > 1 else 1)]
        if nseq == 1:
            STTs = STTs * 2
        CBSs = [AR.alloc("cbs%d" % i, [128, 128], F32) for i in range(2)]
        RHSPs = [AR.alloc("rhsp%d" % i, [128, 4, Pq], F32) for i in range(2)]
        LTs = [AR.alloc("lt%d" % i, [128, 4, Pq], F32) for i in range(2)]
        WTs = [AR.alloc("wt%d" % i, [128, 4, Pq], BF16) for i in range(2)]
        T1 = AR.alloc("t1", [128, 256], F32)
        T2 = AR.alloc("t2", [128, 256], F32)
        Y = AR.alloc("y", [128, 2048], F32)
        YN = XDD
        GS = AR.alloc("gs", [128, 8], F32)
        if nseq > 1:
            MKB = AR.alloc("maskb", [128, 1024], F32)
            io.dma(MKB.t[:], maskb_d, writes=[MKB.b])
            MASKB = MKB.t
            S0 = [AR.alloc("s0_%d" % i, [128, 16, 128], F32) for i in range(3)]
            SO = [AR.alloc("so_%d" % i, [128, 16, 128], F32) for i in range(2)]
            CTMs = [AR.alloc("ctm%d" % i, [128, 8, Pq], BF16) for i in range(2)]
            BMs = [AR.alloc("bm%d" % i, [128, 1024], BF16) for i in range(2)]

        EEa = AR.alloc("eea", [128, NS, 64], F32)
        DDa = AR.alloc("dda", [128, NS, 32], F32)
        for s in range(NS):
            dta_q = DTA.t[0:Pq, s, :]
            bk = K.bank()
            MM(bk.t[0:Pq, 0:32], TRI[0:Pq, 0:Pq], dta_q, True, True, [CONST.b, DTA.b], bk)
            MM(bk.t[0:Pq, 32:64], SAm[0:Pq, 0:Pq], dta_q, True, True, [CONST.b, DTA.b], bk)
            A(lambda h: h.activation(out=EEa.t[0:Pq, s, :], in_=bk.t[0:Pq, 0:64], func=AF.Exp), [bk.b], [EEa.b])
            V(lambda h: h.tensor_tensor(out=DDa.t[0:Pq, s, :], in0=DT.t[0:Pq, s, :], in1=EEa.t[0:Pq, s, 32:64], op=ALU.mult), [DT.b, EEa.b], [DDa.b])

        for s in range(NS):
            c0 = s * 128
            EE = T(EEa.t[:, s, :], EEa.b)
            DD = T(DDa.t[:, s, :], DDa.b)
            xs3 = XS.t[0:Pq, s, :].rearrange("p (h d) -> p h d", h=32)
            V(lambda h: h.tensor_tensor(out=XDT.t[0:Pq, :].rearrange("p (h d) -> p h d", h=32), in0=xs3,
                                        in1=bc(DT.t[0:Pq, s, :].unsqueeze(2), [Pq, 32, 64]), op=ALU.mult), [XS.b, DT.b], [XDT.b])
            V(lambda h: h.tensor_tensor(out=XDD.t[0:Pq, :].rearrange("p (h d) -> p h d", h=32), in0=xs3,
                                        in1=bc(DD.t[0:Pq, :].unsqueeze(2), [Pq, 32, 64]), op=ALU.mult), [XS.b, DD.b], [XDD.b])
            bk = K.bank()
            psb = bk.t[:].bitcast(BF16)
            for g in range(8):
                TR(psb[0:Pq, g * 128:(g + 1) * 128], BT.t[:, g, c0:c0 + Pq], IDB.t[:, :], [BT.b, IDB.b], bk, inc=(g == 7))
            A(lambda h: h.activation(out=BTM.t[0:Pq, :], in_=psb[0:Pq, :], func=AF.Copy), [bk.b], [BTM.b])

            YO = [K.bank(pin=True) for _ in range(4)] if nseq > 1 else []
            def sL(b):
                if nseq > 1:
                    sp.dma(S0[b % 3].t[:], sss[b].rearrange("(hc q) n -> q hc n", q=128), writes=[S0[b % 3].b])

            def sA(b):
                Sb = S0[b % 3] if nseq > 1 else S
                stt = STTs[b % 2]
                for hq in range(4):
                    bk = K.bank()
                    for hi in range(4):
                        hc = 4 * hq + hi
                        TR(bk.t[:, hi * 128:(hi + 1) * 128], Sb.t[:, hc, :], ID, [Sb.b, CONST.b], bk, inc=(hi == 3))
                    A(lambda h: h.activation(out=stt.t[:, hq * 512:(hq + 1) * 512], in_=bk.t[:, :], func=AF.Copy), [bk.b], [stt.b])
                if nseq > 1:
                    ctm = CTMs[b % 2]
                    bm = BMs[b % 2]
                    V(lambda h: h.tensor_tensor(out=ctm.t[:, :, :], in0=CT.t[:, :, c0:c0 + Pq],
                                                in1=bc(MASKB[:, b * Pq:(b + 1) * Pq].unsqueeze(1), [128, 8, Pq]), op=ALU.mult),
                      [CT.b, MKB.b], [ctm.b])
                    V(lambda h: h.tensor_scalar(out=bm.t[0:Pq, :], in0=BTM.t[0:Pq, :], scalar1=SEL[0:Pq, b:b + 1], scalar2=None, op0=ALU.mult),
                      [BTM.b, CONST.b], [bm.b])

            def sB(b):
                Sb = S0[b % 3] if nseq > 1 else S
                Sn = SO[b % 2] if nseq > 1 else S
                stt = STTs[b % 2]
                for g in (range(8) if nseq > 1 else ()):
                    lhs = CTMs[b % 2].t[:, g, 0:Pq] if nseq > 1 else CT.t[:, g, c0:c0 + Pq]
                    rb = [CTMs[b % 2].b] if nseq > 1 else [CT.b]
                    yb = YO[g // 2]
                    pe.op(lambda h: h.matmul(yb.t[0:Pq, (g % 2) * 256:(g % 2) * 256 + 256], lhs, stt.t[:, g * 256:(g + 1) * 256],
                                             start=(b == 0 and g % 2 == 0), stop=(b == nseq - 1), skip_group_check=True), rb + [stt.b], [yb.b], inc=True)
                bmt = BMs[b % 2] if nseq > 1 else BTM
                for hq in range(4):
                    bk = K.bank()
                    for hi in range(4):
                        hc = 4 * hq + hi
                        MM(bk.t[:, hi * 128:(hi + 1) * 128], XDD.t[0:Pq, hc * 128:(hc + 1) * 128], bmt.t[0:Pq, (hc // 2) * 128:(hc // 2 + 1) * 128],
                           True, True, [XDD.b, bmt.b], bk)
                    for hi in range(4):
                        hc = 4 * hq + hi
                        V(lambda h: h.scalar_tensor_tensor(out=Sn.t[:, hc, :], in0=Sb.t[:, hc, :], scalar=CDa.t[:, hc, s * nseq + b:s * nseq + b + 1],
                                                           in1=bk.t[:, hi * 128:(hi + 1) * 128], op0=ALU.mult, op1=ALU.add),
                          [Sb.b, CDa.b, bk.b], [Sn.b])
                if nseq > 1:
                    act.dma(o_sss[b].rearrange("(hc q) n -> q hc n", q=128), Sn.t[:], reads=[Sn.b])

            sL(0)
            if nseq > 1:
                sL(1)
            sA(0)
            for b in range(nseq):
                if b + 2 < nseq:
                    sL(b + 2)
                if b + 1 < nseq:
                    sA(b + 1)
                sB(b)

            def gA(g):
                CBS, RHSP, LT = CBSs[g % 2], RHSPs[g % 2], LTs[g % 2]
                bkc = K.bank()
                MM(bkc.t[0:Pq, 0:Pq], BT.t[:, g, c0:c0 + Pq], CT.t[:, g, c0:c0 + Pq], True, True, [BT.b, CT.b], bkc)
                V(lambda h: h.tensor_tensor(out=RHSP.t[0:Pq, :, :], in0=bc(TRI[0:Pq, 0:Pq].unsqueeze(1), [Pq, 4, Pq]),
                                            in1=bc(DTA.t[0:Pq, s, 4 * g:4 * g + 4].unsqueeze(2), [Pq, 4, Pq]), op=ALU.mult),
                  [CONST.b, DTA.b], [RHSP.b])
                bks = K.bank()
                MM(bks.t[0:Pq, 0:4 * Pq], SAm[0:Pq, 0:Pq], RHSP.t[0:Pq, :, :].rearrange("p a b -> p (a b)"), True, False, [CONST.b, RHSP.b], bks)
                MM(bks.t[0:Pq, 0:4 * Pq], ID[0:Pq, 0:Pq], NEG4[0:Pq, 0:4 * Pq], False, True, [CONST.b], bks)
                A(lambda h: h.activation(out=CBS.t[0:Pq, 0:Pq], in_=bkc.t[0:Pq, 0:Pq], func=AF.Copy), [bkc.b], [CBS.b])
                A(lambda h: h.activation(out=LT.t[0:Pq, :, :].rearrange("p a b -> p (a b)"), in_=bks.t[0:Pq, 0:4 * Pq], func=AF.Exp), [bks.b], [LT.b])

            def gB(g):
                CBS, LT, WT = CBSs[g % 2], LTs[g % 2], WTs[g % 2]
                V(lambda h: h.tensor_tensor(out=WT.t[0:Pq, :, :], in0=LT.t[0:Pq, :, :], in1=bc(CBS.t[0:Pq, 0:Pq].unsqueeze(1), [Pq, 4, Pq]),
                                            op=ALU.mult), [LT.b, CBS.b], [WT.b])
                bky = K.bank()
                for hh in range(4):
                    MM(bky.t[0:Pq, hh * 64:(hh + 1) * 64], WT.t[0:Pq, hh, :], XDT.t[0:Pq, (4 * g + hh) * 64:(4 * g + hh + 1) * 64], True, True,
                       [WT.b, XDT.b], bky)
                if nseq == 1:
                    MM(bky.t[0:Pq, 256:512], CT.t[:, g, c0:c0 + Pq], STTs[0].t[:, g * 256:(g + 1) * 256], True, True, [CT.b, STTs[0].b], bky)
                return bky

            def gC(g, bky):
                yb = YO[g // 2] if nseq > 1 else bky
                yoff = (g % 2) * 256 if nseq > 1 else 256
                V(lambda h: h.tensor_tensor(out=T1.t[0:Pq, :].rearrange("p (h d) -> p h d", h=4),
                                            in0=yb.t[0:Pq, yoff:yoff + 256].rearrange("p (h d) -> p h d", h=4),
                                            in1=bc(EE.t[0:Pq, 4 * g:4 * g + 4].unsqueeze(2), [Pq, 4, 64]), op=ALU.mult), [yb.b, EE.b], [T1.b])
                V(lambda h: h.tensor_tensor(out=T2.t[0:Pq, :].rearrange("p (h d) -> p h d", h=4),
                                            in0=XS.t[0:Pq, s, g * 256:(g + 1) * 256].rearrange("p (h d) -> p h d", h=4),
                                            in1=bc(DBC.t[0:Pq, 4 * g:4 * g + 4].unsqueeze(2), [Pq, 4, 64]), op=ALU.mult), [XS.b, DBC.b], [T2.b])
                V(lambda h: h.tensor_tensor(out=T1.t[0:Pq, :], in0=T1.t[0:Pq, :], in1=T2.t[0:Pq, :], op=ALU.add), [T1.b, T2.b], [T1.b])
                V(lambda h: h.tensor_tensor(out=T1.t[0:Pq, :], in0=bky.t[0:Pq, 0:256], in1=T1.t[0:Pq, :], op=ALU.add),
                  [bky.b, T1.b], [T1.b])
                V(lambda h: h.tensor_tensor(out=Y.t[0:Pq, g * 256:(g + 1) * 256], in0=T1.t[0:Pq, :], in1=SZ.t[0:Pq, s, g * 256:(g + 1) * 256], op=ALU.mult),
                  [T1.b, SZ.b], [Y.b])
                A(lambda h: h.activation(out=JUNK.t[0:Pq, 0:256], in_=Y.t[0:Pq, g * 256:(g + 1) * 256], func=AF.Square, accum_out=GS.t[0:Pq, g:g + 1]),
                  [Y.b], [JUNK.b, GS.b])

            bkys = {}
            for i in range(10):
                if i < 8:
                    gA(i)
                if 0 <= i - 1 < 8:
                    bkys[i - 1] = gB(i - 1)
                if 0 <= i - 2 < 8:
                    gC(i - 2, bkys.pop(i - 2))
            for yb in YO:
                K.unpin(yb)
            A(lambda h: h.activation(out=GS.t[0:Pq, :], in_=GS.t[0:Pq, :], func=AF.Ln, scale=1.0 / 256, bias=EPS), [GS.b], [GS.b])
            A(lambda h: h.activation(out=GS.t[0:Pq, :], in_=GS.t[0:Pq, :], func=AF.Exp, scale=-0.5), [GS.b], [GS.b])
            V(lambda h: h.tensor_tensor(out=YN.t[0:Pq, :].rearrange("p (g d) -> p g d", g=8), in0=Y.t[0:Pq, :].rearrange("p (g d) -> p g d", g=8),
                                        in1=bc(GS.t[0:Pq, :].unsqueeze(2), [Pq, 8, 256]), op=ALU.mult), [Y.b, GS.b], [YN.b])
            for half in range(2):
                bk = K.bank()
                psb = bk.t[:].bitcast(BF16)
                for ci in range(8):
                    cc = half * 8 + ci
                    TR(psb[:, ci * 128:ci * 128 + Pq], YN.t[0:Pq, cc * 128:(cc + 1) * 128], IDB.t[0:Pq, 0:Pq], [YN.b, IDB.b], bk, inc=(ci == 7))
                V(lambda h: h.tensor_tensor(out=YST.t[:, half * 8:half * 8 + 8, c0:c0 + Pq],
                                            in0=psb.rearrange("p (c t) -> p c t", c=8)[:, :, 0:Pq],
                                            in1=bc(PF.t[:, PF_SNW + half * 8:PF_SNW + half * 8 + 8].unsqueeze(2), [128, 8, Pq]), op=ALU.mult),
                  [bk.b, PF.b], [YST.b])

        AR.off = mark
        K.snapshot()
        MT = AR.alloc("mt", [128, 8, T_], BF16)
        M1 = AR.alloc("m1", [128, T_], F32)
        M2 = AR.alloc("m2", [128, T_], F32)
        S1 = AR.alloc("s1", [128, T_], F32)
        S2 = AR.alloc("s2", [128, T_], F32)
        YT = AR.alloc("ytm", [128, D], F32)
        for dc2 in range(4):
            sA, sB, sC = wnext(3)
            for di in range(2):
                dc = 2 * dc2 + di
                p1 = K.bank(); g1 = K.bank(); p2 = K.bank(); g2 = K.bank()
                for k in range(8):
                    o = k * 256 + di * 128
                    MM(p1.t[:, 0:T_], sA.t[:, o:o + 128], YRG.t[:, k, 0:T_], k == 0, k == 7, [sA.b, YRG.b], p1)
                for k in range(8):
                    o = 2048 + k * 256 + di * 128
                    MM(g1.t[:, 0:T_], sA.t[:, o:o + 128], XNT.t[:, k, 0:T_], k == 0, k == 7, [sA.b, XNT.b], g1)
                for k in range(16):
                    o = k * 256 + di * 128
                    MM(p2.t[:, 0:T_], sC.t[:, o:o + 128], YST.t[:, k, 0:T_], k == 0, k == 15, [sC.b, YST.b], p2)
                for k in range(8):
                    o = k * 256 + di * 128
                    MM(g2.t[:, 0:T_], sB.t[:, o:o + 128], XNT.t[:, k, 0:T_], k == 0, k == 7, [sB.b, XNT.b], g2)
                A(lambda h: h.activation(out=S1.t[:, 0:T_], in_=g1.t[:, 0:T_], func=AF.Sigmoid), [g1.b], [S1.b])
                A(lambda h: h.activation(out=S2.t[:, 0:T_], in_=g2.t[:, 0:T_], func=AF.Sigmoid), [g2.b], [S2.b])
                V(lambda h: h.tensor_tensor(out=M1.t[:, 0:T_], in0=S1.t[:, 0:T_], in1=p1.t[:, 0:T_], op=ALU.mult), [S1.b, p1.b], [M1.b])
                V(lambda h: h.tensor_tensor(out=M2.t[:, 0:T_], in0=S2.t[:, 0:T_], in1=p2.t[:, 0:T_], op=ALU.mult), [S2.b, p2.b], [M2.b])
                V(lambda h: h.tensor_tensor(out=MT.t[:, dc, 0:T_], in0=M1.t[:, 0:T_], in1=M2.t[:, 0:T_], op=ALU.add), [M1.b, M2.b], [MT.b])
        w0, w1 = wnext(2)
        for s in range(NS):
            p0 = K.bank(); p1 = K.bank()
            for ph, ws in ((p0, w0), (p1, w1)):
                for k in range(8):
                    MM(ph.t[0:Pt, :], MT.t[:, k, s * 128:s * 128 + Pt], ws.t[:, k * 512:(k + 1) * 512], k == 0, k == 7, [MT.b, ws.b], ph)
            post_norm_res(Pt, s, p0, p1, nwp, YT, 1.0)

    def fm_to_rows(src_fn, nchunks, ncols, STw, store_fn):
        for g in range(nchunks // 4):
            bk = K.bank()
            for ci in range(4):
                ap, b = src_fn(4 * g + ci)
                TR(bk.t[0:ncols, ci * 128:(ci + 1) * 128], ap, ID, [b, CONST.b], bk, inc=(ci == 3))
            A(lambda h: h.activation(out=STw.t[0:ncols, g * 512:(g + 1) * 512], in_=bk.t[0:ncols, :], func=AF.Copy), [bk.b], [STw.b])
        store_fn(STw)

    STG = [None, None]
    stg_i = [0]
    TMh = [None]

    def alloc_stg():
        AR.reset()
        STG[0] = AR.alloc("stg0", [128, 512], F32)
        STG[1] = AR.alloc("stg1", [128, 512], F32)
        return [AR.alloc("wide0", [128, 4096], F32), AR.alloc("wide1", [128, 1024], F32), AR.alloc("wide2", [128, 1024], F32)]

    for v in (RGC, HC, XBCC, S):
        dve.op(lambda h: h.memset(v.t[:], 0.0), [], [v.b])

    PC = (CONST.t[:, C_PTRI:C_PTRI + 128], CONST.t[:, C_PSA:C_PSA + 128], CONST.t[:, C_PNEG:C_PNEG + 512], CONST.t[:, C_SSEL + 15:C_SSEL + 16], None)
    ONES = K.sb("ones", [128, 1], F32, const=True)
    dve.op(lambda h: h.memset(ONES.t[:], 1.0), [], [ONES.b])
    PC = (PC[0], PC[1], PC[2], ONES.t[:, 0:1], None)
    SC = (CONST.t[:, C_STRI:C_STRI + 64], CONST.t[:, C_SSA:C_SSA + 64], CONST.t[:, C_SNEG:C_SNEG + 256], CONST.t[:, C_SSEL:C_SSEL + 16],
          None)

    for ti in range(NPT):
        t0 = ti * 512
        io.dma(X.t[:, :, :], xp[t0:t0 + 512, :].rearrange("(s p) d -> p s d", p=128), writes=[X.b])
        ffn(0, 512, 128, 4, 0, 1)
        mixer(512, 128, 4, 1, ti == 0, PC)
        ffn(1, 512, 128, 4, 4, 5)
        io.dma(yp[t0:t0 + 512, :].rearrange("(s p) d -> p s d", p=128), X.t[:, :, :], reads=[X.b])

    W0, W1, W2 = alloc_stg()
    io.dma(o_pss.rearrange("(hc q) n -> q hc n", q=128), S.t[:], reads=[S.b])
    bk = K.bank()
    TR(bk.t[0:8, 0:128], HC.t[:, :, 0], ID, [HC.b, CONST.b], bk)
    A(lambda h: h.activation(out=STG[0].t[0:8, 0:128], in_=bk.t[0:8, 0:128], func=AF.Copy), [bk.b], [STG[0].b])
    io.dma(o_prh, STG[0].t[0:8, 0:128], reads=[STG[0].b])
    fm_to_rows(lambda c: (RGC.t[:, c, 0:3], RGC.b), 8, 3, W1, lambda st: io.dma(o_prc, st.t[0:3, 0:1024], reads=[st.b]))
    fm_to_rows(lambda c: (XBCC.t[:, c, 0:3], XBCC.b), 32, 3, W0, lambda st: io.dma(o_psc, st.t[0:3, 0:4096], reads=[st.b]))

    if DO_SAMPLE:
        W0, W1, W2 = alloc_stg()
        for t in range(4):
            io.dma(X.t[t * 16:(t + 1) * 16, 0, :], xs[:, t, :], writes=[X.b], add=(t > 0))

        def rows_to_fm(load_fn, nrows, nchunks, dst, TMw):
            load_fn(TMw)
            for g in range(nchunks // 4):
                bk = K.bank()
                for ci in range(4):
                    TR(bk.t[:, ci * 48:ci * 48 + nrows], TMw.t[0:nrows, (4 * g + ci) * 128:(4 * g + ci + 1) * 128], ID[0:nrows, 0:nrows],
                       [TMw.b, CONST.b], bk, inc=(ci == 3))
                A(lambda h: h.activation(out=dst.t[:, 4 * g:4 * g + 4, 0:nrows], in_=bk.t[:, 0:192].rearrange("p (c t) -> p c t", c=4)[:, :, 0:nrows],
                                         func=AF.Copy), [bk.b], [dst.b])

        def ld_rows3(srcd, TMw, w):
            for t in range(3):
                io.dma(TMw.t[t * 16:(t + 1) * 16, 0:w], srcd[:, t, :], writes=[TMw.b], add=(t > 0))

        rows_to_fm(lambda TMw: ld_rows3(src, TMw, 1024), 48, 8, RGC, W1)
        rows_to_fm(lambda TMw: io.dma(TMw.t[0:16, 0:1024], srh, writes=[TMw.b]), 16, 8, HC, W2)
        rows_to_fm(lambda TMw: ld_rows3(ssc, TMw, 4096), 48, 32, XBCC, W0)

        ffn(0, 64, 64, 1, 0, 1)
        mixer(64, 64, 1, 16, False, SC)
        ffn(1, 64, 64, 1, 4, 5)
        for t in range(4):
            io.dma(ys[:, t, :], X.t[t * 16:(t + 1) * 16, 0, :], reads=[X.b])
        W0, W1, W2 = alloc_stg()
        fm_to_rows(lambda c: (HC.t[:, c, 0:16], HC.b), 8, 16, W2, lambda st: io.dma(o_srh, st.t[0:16, 0:1024], reads=[st.b]))

        def st3(dst, st, w):
            for t in range(3):
                io.dma(dst[:, t, :], st.t[t * 16:(t + 1) * 16, 0:w], reads=[st.b])

        fm_to_rows(lambda c: (RGC.t[:, c, 0:48], RGC.b), 8, 48, W1, lambda st: st3(o_src, st, 1024))
        fm_to_rows(lambda c: (XBCC.t[:, c, 0:48], XBCC.b), 32, 48, W0, lambda st: st3(o_ssc, st, 4096))

    for e in K.engs:
        sp.need(e.sid, e.cnt)
        for i, sid in enumerate(e.dsid):
            sp.need(sid, e.dcnt[i])
    return K


def _consts():
    c = np.zeros((128, C_END), np.float32)
    c[:, C_ID:C_ID + 128] = np.eye(128, dtype=np.float32)
    k = np.arange(128)
    c[:, C_PTRI:C_PTRI + 128] = (k[:, None] <= k[None, :])
    c[:, C_PSA:C_PSA + 128] = (k[:, None] > k[None, :])
    neg = np.where(k[None, :] >= k[:, None], 0.0, -30000.0).astype(np.float32)
    c[:, C_PNEG:C_PNEG + 512] = np.tile(neg, (1, 4))
    q = np.arange(64)
    sq = q % 16
    tq = q // 16
    same = sq[:, None] == sq[None, :]
    c[0:64, C_STRI:C_STRI + 64] = same & (tq[:, None] <= tq[None, :])
    c[0:64, C_SSA:C_SSA + 64] = same & (tq[:, None] > tq[None, :])
    negs = np.where(same & (tq[None, :] >= tq[:, None]), 0.0, -30000.0).astype(np.float32)
    c[0:64, C_SNEG:C_SNEG + 256] = np.tile(negs, (1, 4))
    c[0:64, C_SSEL:C_SSEL + 16] = (sq[:, None] == np.arange(16)[None, :])
    return c


def _maskb():
    sq = np.arange(64) % 16
    mb = (np.arange(16)[:, None] == sq[None, :]).astype(np.float32).reshape(1, 1024)
    return np.ascontiguousarray(np.broadcast_to(mb, (128, 1024)))


def _fm(v, nch):
    return np.ascontiguousarray(v.reshape(nch, 128).T)


def _prep_shared(inp):
    f = lambda a: np.ascontiguousarray(a, dtype=np.float32)
    pf = np.zeros((128, PF_END), np.float32)
    rcw = inp["rg_conv_w"][0]
    pf[:, PF_RGCW:PF_RGCW + 32] = rcw.reshape(4, 8, 128).transpose(2, 1, 0).reshape(128, 32)
    pf[:, PF_RGCB:PF_RGCB + 8] = _fm(inp["rg_conv_b"][0], 8)
    pf[:, PF_BA:PF_BA + 8] = _fm(inp["rg_ba"][0], 8)
    pf[:, PF_BX:PF_BX + 8] = _fm(inp["rg_bx"][0], 8)
    pf[:, PF_LAM:PF_LAM + 8] = _fm(inp["rg_lambda"][0], 8)
    scw = inp["ssd_conv_w"][0]
    pf[:, PF_SCW:PF_SCW + 128] = scw.reshape(4, 32, 128).transpose(2, 1, 0).reshape(128, 128)
    pf[:, PF_SCB:PF_SCB + 32] = _fm(inp["ssd_conv_b"][0], 32)
    pf[:, PF_SNW:PF_SNW + 16] = _fm(inp["ssd_norm_w"][0], 16)
    nv = np.stack([inp["n_ffn1_pre"][0], inp["n_ffn1_post"][0], inp["n_mix_pre"][0], inp["n_mix_post"][0],
                   inp["n_ffn2_pre"][0], inp["n_ffn2_post"][0]], 0)
    pt32 = np.stack([inp["ssd_dt_bias"][0], inp["ssd_a_log"][0], inp["ssd_d"][0]], 0)
    return {
        "wg1": f(inp["ffn1_wg"][0]), "wu1": f(inp["ffn1_wu"][0]), "wd1": f(inp["ffn1_wd"][0]),
        "wg2": f(inp["ffn2_wg"][0]), "wu2": f(inp["ffn2_wu"][0]), "wd2": f(inp["ffn2_wd"][0]),
        "win": f(inp["w_in"][0]), "wa": f(inp["rg_wa"][0]), "wx": f(inp["rg_wx"][0]),
        "wprg": f(inp["w_proj_rg"][0]), "wpssd": f(inp["w_proj_ssd"][0]), "wout": f(inp["w_out"][0]),
        "pf": pf, "nv": f(nv), "pt32": f(pt32), "consts": _consts(), "maskb": _maskb(),
    }


def kernel(**inp):
    inp = {k: np.asarray(v) for k, v in inp.items()}
    nc = bass.Bass("TRN2", target_bir_lowering=False)
    build(nc)
    shared = _prep_shared(inp)
    in_maps = []
    for c in range(8):
        m = dict(shared)
        m["xp"] = np.ascontiguousarray(inp["x_prompt"][c], dtype=np.float32)
        m["xs"] = np.ascontiguousarray(inp["x_sample"][c * 16:(c + 1) * 16], dtype=np.float32)
        m["srh"] = np.ascontiguousarray(inp["state_rg_h"][0, c * 16:(c + 1) * 16], dtype=np.float32)
        m["src"] = np.ascontiguousarray(inp["state_rg_conv"][0, c * 16:(c + 1) * 16], dtype=np.float32)
        m["sss"] = np.ascontiguousarray(inp["state_ssd"][0, c * 16:(c + 1) * 16], dtype=np.float32).reshape(16, 2048, 128)
        m["ssc"] = np.ascontiguousarray(inp["state_ssd_conv"][0, c * 16:(c + 1) * 16], dtype=np.float32)
        in_maps.append(m)
    res = run_bass_kernel_spmd(nc, in_maps, core_ids=list(range(8)))
    R = res.results
    cat = lambda k: np.concatenate([np.asarray(r[k], dtype=np.float32) for r in R], 0)
    y_prompt = np.stack([np.asarray(r["yp"], np.float32) for r in R], 0)
    y_sample = cat("ys")
    p_rg_h = np.stack([np.asarray(r["o_prh"], np.float32).reshape(1024) for r in R], 0)[None]
    p_rg_conv = np.stack([np.asarray(r["o_prc"], np.float32) for r in R], 0)[None]
    p_ssd = np.stack([np.asarray(r["o_pss"], np.float32).reshape(32, 64, 128) for r in R], 0)[None]
    p_ssd_conv = np.stack([np.asarray(r["o_psc"], np.float32) for r in R], 0)[None]
    s_rg_h = cat("o_srh")[None]
    s_rg_conv = cat("o_src")[None]
    s_ssd = cat("o_sss").reshape(128, 32, 64, 128)[None]
    s_ssd_conv = cat("o_ssc")[None]
    return (y_prompt, y_sample, p_rg_h, p_rg_conv, p_ssd, p_ssd_conv, s_rg_h, s_rg_conv, s_ssd, s_ssd_conv)
```
